# Optimizing a Trainium2 kernel written in Bass

```python
import jax, jax.numpy as jnp
from jax import lax
import numpy as np

D_MODEL = 2048
BATCH = 4
SEQ = 2048
DEPTH = 1
DEC_BATCH = 32
DEC_SEQ = 16
PAST_LEN = 4096

CHUNK = 64
Q_BLOCK = 128
EPS = 1e-6

MLA_HEADS = 8
Q_LORA = 512
KV_LORA = 512
NOPE_DIM = 128
ROPE_DIM = 64
V_DIM = 128
ROPE_THETA = 10000.0
MLA_SCALE = (NOPE_DIM + ROPE_DIM) ** -0.5

GLA_HEADS = 4
GLA_DK = 128
GLA_DV = 256
GATE_RANK = 16
GATE_TAU = 16.0

MLA_WIDTH = MLA_HEADS * V_DIM
GLA_WIDTH = GLA_HEADS * GLA_DV
MIX_WIDTH = MLA_WIDTH + GLA_WIDTH

IN_SPLITS = (Q_LORA, KV_LORA, ROPE_DIM, GLA_HEADS * GLA_DK, GLA_HEADS * GLA_DK, GLA_WIDTH, GATE_RANK, GLA_WIDTH)
D_IN = Q_LORA + KV_LORA + ROPE_DIM + 2 * GLA_HEADS * GLA_DK + GLA_WIDTH + GATE_RANK + GLA_WIDTH

D_FF = -(-8 * D_MODEL // (3 * 256)) * 256
PLE_DIM = 256

kernel_name = "hybrid_mla_gla_streaming_step"


def rms_norm(x, g):
    xf = x.astype(jnp.float32)
    y = xf * lax.rsqrt(jnp.mean(xf * xf, axis=-1, keepdims=True) + EPS)
    return (y * g.astype(jnp.float32)).astype(x.dtype)


def rope_angles(pos):
    half = ROPE_DIM // 2
    inv = 1.0 / (ROPE_THETA ** (jnp.arange(half, dtype=jnp.float32) / half))
    ang = pos.astype(jnp.float32)[:, None] * inv[None, :]
    return jnp.cos(ang), jnp.sin(ang)


def apply_rope(x, cos, sin):
    xf = x.astype(jnp.float32)
    x1, x2 = xf[..., :ROPE_DIM // 2], xf[..., ROPE_DIM // 2:]
    return jnp.concatenate([x1 * cos - x2 * sin, x2 * cos + x1 * sin], axis=-1).astype(x.dtype)


def mixer_inputs(h, pos, g_pre_mix, w_in, g_q, w_uq, w_uk, g_kv, w_ga, b_ga):
    B, S, _ = h.shape
    a = rms_norm(h, g_pre_mix)
    z = a @ w_in
    idx = np.cumsum(IN_SPLITS)[:-1].tolist()
    c_q, c_kv, k_r, q_g, k_g, v_g, g_lr, r_g = jnp.split(z, idx, axis=-1)
    cos, sin = rope_angles(pos)
    c_q = rms_norm(c_q, g_q)
    q = (c_q @ w_uq).reshape(B, S, MLA_HEADS, NOPE_DIM + ROPE_DIM)
    q_nope, q_rope = q[..., :NOPE_DIM], q[..., NOPE_DIM:]
    q_rope = apply_rope(q_rope, cos[:, None, :], sin[:, None, :])
    q_lat = jnp.einsum('bshd,chd->bshc', q_nope, w_uk)
    c_kv = rms_norm(c_kv, g_kv)
    k_r = apply_rope(k_r, cos, sin)
    q_g = q_g.reshape(B, S, GLA_HEADS, GLA_DK) * (GLA_DK ** -0.5)
    k_g = k_g.reshape(B, S, GLA_HEADS, GLA_DK)
    v_g = v_g.reshape(B, S, GLA_HEADS, GLA_DV)
    log_a = jax.nn.log_sigmoid((g_lr @ w_ga + b_ga).astype(jnp.float32)) / GATE_TAU
    log_a = log_a.reshape(B, S, GLA_HEADS, GLA_DK)
    return q_lat, q_rope, c_kv, k_r, q_g, k_g, v_g, log_a, r_g


def mla_attend(q_lat, q_rope, ckv, kr, mask):
    s = jnp.einsum('bqhc,bkc->bhqk', q_lat, ckv) + jnp.einsum('bqhr,bkr->bhqk', q_rope, kr)
    s = s.astype(jnp.float32) * MLA_SCALE
    if mask is not None:
        s = jnp.where(mask[None, None], s, -jnp.inf)
    p = jax.nn.softmax(s, axis=-1).astype(ckv.dtype)
    return jnp.einsum('bhqk,bkc->bqhc', p, ckv)


def mla_prompt(q_lat, q_rope, ckv, kr):
    B, S, H, C = q_lat.shape
    nb = S // Q_BLOCK
    ql = q_lat.reshape(B, nb, Q_BLOCK, H, C).swapaxes(0, 1)
    qr = q_rope.reshape(B, nb, Q_BLOCK, H, ROPE_DIM).swapaxes(0, 1)
    key_chunk = jnp.arange(S) // CHUNK

    def block(args):
        j, qlb, qrb = args
        q_chunk = (j * Q_BLOCK + jnp.arange(Q_BLOCK)) // CHUNK
        mask = key_chunk[None, :] <= q_chunk[:, None]
        return mla_attend(qlb, qrb, ckv, kr, mask)

    o = lax.map(block, (jnp.arange(nb), ql, qr))
    return o.swapaxes(0, 1).reshape(B, S, H, C)


def gla_chunk(S0, q, k, v, log_a):
    L = q.shape[1]
    qf, kf, vf = q.astype(jnp.float32), k.astype(jnp.float32), v.astype(jnp.float32)
    b = jnp.cumsum(log_a, axis=1)
    causal = jnp.tril(jnp.ones((L, L), dtype=bool))
    diff = b[:, :, None] - b[:, None, :]
    decay = jnp.exp(jnp.where(causal[None, :, :, None, None], diff, -jnp.inf))
    attn = jnp.einsum('bthd,btshd,bshd->bhts', qf, decay, kf)
    o = jnp.einsum('bhts,bshv->bthv', attn, vf) + jnp.einsum('bthd,bhdv->bthv', qf * jnp.exp(b), S0)
    b_last = b[:, -1]
    S1 = jnp.exp(b_last)[..., None] * S0 + jnp.einsum('bshd,bshv->bhdv', kf * jnp.exp(b_last[:, None] - b), vf)
    return S1, o


def gla_prompt(q, k, v, log_a):
    B, S = q.shape[:2]
    n = S // CHUNK

    def to_blocks(t):
        return t.reshape((B, n, CHUNK) + t.shape[2:]).swapaxes(0, 1)

    S0 = jnp.zeros((B, GLA_HEADS, GLA_DK, GLA_DV), jnp.float32)
    S_fin, o = lax.scan(lambda Sc, xs: gla_chunk(Sc, *xs), S0,
                        (to_blocks(q), to_blocks(k), to_blocks(v), to_blocks(log_a)))
    return o.swapaxes(0, 1).reshape(B, S, GLA_HEADS, GLA_DV), S_fin


def finish_layer(h, p, o_lat, o_gla, r_g, w_uv, g_gla, w_out, g_post_mix, g_pre_ffn,
                 w_gate, w_up, w_down, g_post_ffn, w_ple, w_ple_gate):
    B, S, _ = h.shape
    o_mla = jnp.einsum('bshc,chd->bshd', o_lat, w_uv).reshape(B, S, MLA_WIDTH)
    o_g = rms_norm(o_gla, g_gla) * jax.nn.silu(r_g.astype(jnp.float32)).reshape(B, S, GLA_HEADS, GLA_DV)
    o_g = o_g.reshape(B, S, GLA_WIDTH).astype(h.dtype)
    mix = jnp.concatenate([o_mla.astype(h.dtype), o_g], axis=-1) @ w_out
    h = h + rms_norm(mix, g_post_mix)
    f = rms_norm(h, g_pre_ffn)
    f = (jax.nn.silu(f @ w_gate) * (f @ w_up)) @ w_down
    h = h + rms_norm(f, g_post_ffn)
    return h + jax.nn.sigmoid(h @ w_ple_gate) * (p @ w_ple)


def setup_inputs(seed: int = 0) -> dict:
    key = jax.random.key(seed)
    ks = jax.random.split(key, 32)

    def nrm(k, shape, scale):
        return jax.random.normal(k, shape, jnp.float32) * scale

    def gain(k, n):
        return 1.0 + 0.05 * jax.random.normal(k, (DEPTH, n), jnp.float32)

    return {
        "x_prompt": nrm(ks[0], (BATCH, SEQ, D_MODEL), 1.0),
        "x_sample": nrm(ks[1], (DEC_BATCH, DEC_SEQ, D_MODEL), 1.0),
        "cache_ckv": nrm(ks[2], (DEPTH, DEC_BATCH, PAST_LEN, KV_LORA), 1.0),
        "cache_krope": nrm(ks[3], (DEPTH, DEC_BATCH, PAST_LEN, ROPE_DIM), 1.0),
        "state_gla": nrm(ks[4], (DEPTH, DEC_BATCH, GLA_HEADS, GLA_DK, GLA_DV), 2.0),
        "p_prompt": nrm(ks[5], (DEPTH, BATCH, SEQ, PLE_DIM), 1.0),
        "p_sample": nrm(ks[6], (DEPTH, DEC_BATCH, DEC_SEQ, PLE_DIM), 1.0),
        "g_pre_mix": gain(ks[7], D_MODEL),
        "w_in": nrm(ks[8], (DEPTH, D_MODEL, D_IN), D_MODEL ** -0.5),
        "g_q": gain(ks[9], Q_LORA),
        "w_uq": nrm(ks[10], (DEPTH, Q_LORA, MLA_HEADS * (NOPE_DIM + ROPE_DIM)), Q_LORA ** -0.5),
        "w_uk": nrm(ks[11], (DEPTH, KV_LORA, MLA_HEADS, NOPE_DIM), KV_LORA ** -0.5),
        "g_kv": gain(ks[12], KV_LORA),
        "w_ga": nrm(ks[13], (DEPTH, GATE_RANK, GLA_HEADS * GLA_DK), GATE_RANK ** -0.5),
        "b_ga": nrm(ks[14], (DEPTH, GLA_HEADS * GLA_DK), 0.1),
        "w_uv": nrm(ks[15], (DEPTH, KV_LORA, MLA_HEADS, V_DIM), KV_LORA ** -0.5),
        "g_gla": gain(ks[16], GLA_DV),
        "w_out": nrm(ks[17], (DEPTH, MIX_WIDTH, D_MODEL), MIX_WIDTH ** -0.5),
        "g_post_mix": gain(ks[18], D_MODEL),
        "g_pre_ffn": gain(ks[19], D_MODEL),
        "w_gate": nrm(ks[20], (DEPTH, D_MODEL, D_FF), D_MODEL ** -0.5),
        "w_up": nrm(ks[21], (DEPTH, D_MODEL, D_FF), D_MODEL ** -0.5),
        "w_down": nrm(ks[22], (DEPTH, D_FF, D_MODEL), D_FF ** -0.5),
        "g_post_ffn": gain(ks[23], D_MODEL),
        "w_ple": nrm(ks[24], (DEPTH, PLE_DIM, D_MODEL), PLE_DIM ** -0.5),
        "w_ple_gate": nrm(ks[25], (DEPTH, D_MODEL, D_MODEL), D_MODEL ** -0.5),
    }


def reference(x_prompt, x_sample, cache_ckv, cache_krope, state_gla, p_prompt, p_sample,
              g_pre_mix, w_in, g_q, w_uq, w_uk, g_kv, w_ga, b_ga, w_uv, g_gla, w_out,
              g_post_mix, g_pre_ffn, w_gate, w_up, w_down, g_post_ffn, w_ple, w_ple_gate):
    pos_p = jnp.arange(x_prompt.shape[1])
    pos_s = PAST_LEN + jnp.arange(x_sample.shape[1])
    h_p, h_s = x_prompt, x_sample
    ckv_p, kr_p, st_p, ckv_s, kr_s, st_s = [], [], [], [], [], []
    for i in range(DEPTH):
        q_lat, q_rope, c_kv, k_r, q_g, k_g, v_g, log_a, r_g = mixer_inputs(
            h_p, pos_p, g_pre_mix[i], w_in[i], g_q[i], w_uq[i], w_uk[i], g_kv[i], w_ga[i], b_ga[i])
        o_lat = mla_prompt(q_lat, q_rope, c_kv, k_r)
        o_gla, S_fin = gla_prompt(q_g, k_g, v_g, log_a)
        h_p = finish_layer(h_p, p_prompt[i], o_lat, o_gla, r_g, w_uv[i], g_gla[i], w_out[i], g_post_mix[i],
                           g_pre_ffn[i], w_gate[i], w_up[i], w_down[i], g_post_ffn[i], w_ple[i], w_ple_gate[i])
        ckv_p.append(c_kv)
        kr_p.append(k_r)
        st_p.append(S_fin.astype(x_prompt.dtype))
        q_lat, q_rope, c_kv, k_r, q_g, k_g, v_g, log_a, r_g = mixer_inputs(
            h_s, pos_s, g_pre_mix[i], w_in[i], g_q[i], w_uq[i], w_uk[i], g_kv[i], w_ga[i], b_ga[i])
        ckv_all = jnp.concatenate([cache_ckv[i].astype(c_kv.dtype), c_kv], axis=1)
        kr_all = jnp.concatenate([cache_krope[i].astype(k_r.dtype), k_r], axis=1)
        o_lat = mla_attend(q_lat, q_rope, ckv_all, kr_all, None)
        S_new, o_gla = gla_chunk(state_gla[i].astype(jnp.float32), q_g, k_g, v_g, log_a)
        h_s = finish_layer(h_s, p_sample[i], o_lat, o_gla, r_g, w_uv[i], g_gla[i], w_out[i], g_post_mix[i],
                           g_pre_ffn[i], w_gate[i], w_up[i], w_down[i], g_post_ffn[i], w_ple[i], w_ple_gate[i])
        ckv_s.append(c_kv)
        kr_s.append(k_r)
        st_s.append(S_new.astype(x_sample.dtype))
    return (h_p, h_s, jnp.stack(ckv_p), jnp.stack(kr_p), jnp.stack(st_p),
            jnp.stack(ckv_s), jnp.stack(kr_s), jnp.stack(st_s))
```

```python
import numpy as np
import concourse.bass as bass
import concourse.mybir as mybir
from concourse.bass_utils import run_bass_kernel_spmd
from contextlib import ExitStack

F32 = mybir.dt.float32
BF16 = mybir.dt.bfloat16
AF = mybir.ActivationFunctionType
ALU = mybir.AluOpType

D = 2048
NOWN = 1088
NNAT = 2112
DFF = 5632
EPS = 1e-6
MLA_SCALE = 192 ** -0.5
PSUM_BASE = 1 << 20
PAGE = 64
SAME_ENGINE_SYNC = True


class Op:
    __slots__ = ("eng", "fn", "r", "w", "dma", "deps", "inc", "tok", "sem", "semval", "label")

    def __init__(self, eng, fn, r, w, dma, label=None):
        self.eng, self.fn, self.r, self.w, self.dma, self.label = eng, fn, r, w, dma, label
        self.deps = []
        self.inc = False
        self.tok = None
        self.sem = None
        self.semval = None


def _pages(views):
    out = []
    for v in views:
        lo, hi = v.rg
        if lo >= PSUM_BASE:
            out.extend(range(PSUM_BASE // PAGE + (lo - PSUM_BASE) // 2048,
                             PSUM_BASE // PAGE + (hi - PSUM_BASE + 2047) // 2048))
        else:
            out.extend(range(lo // PAGE, (hi + PAGE - 1) // PAGE))
    return out


class Sched:
    def __init__(self, nc, st):
        self.nc = nc
        self.ops = []
        self.label = None
        self.annotate = False
        self.engs = {"pe": nc.tensor, "act": nc.scalar, "dve": nc.vector, "pool": nc.gpsimd, "sp": nc.sync}
        self.esem = {e: st.enter_context(nc.semaphore("sem_" + e)) for e in ("pe", "act", "dve", "pool")}
        nd = {"sp": 16, "pool": 12}
        self.dsems = {q: [st.enter_context(nc.semaphore(f"dsem_{q}{i}")) for i in range(n)] for q, n in nd.items()}

    def add(self, eng, fn, r=(), w=(), dma=False):
        import os
        mx = int(os.environ.get("MAXOPS", "0"))
        if mx and len(self.ops) >= mx:
            return
        self.ops.append(Op(eng, fn, _pages(r), _pages(w), dma, self.label))

    def finalize(self):
        ops = self.ops
        last_w = {}
        readers = {}
        rr = {q: 0 for q in self.dsems}
        sem_last = {}
        sem_cnt = {}
        for i, op in enumerate(ops):
            deps = {}
            for k in op.r:
                j = last_w.get(k)
                if j is not None:
                    deps[j] = "raw"
                if k >= PSUM_BASE // PAGE:
                    rs = readers.get(k)
                    if rs:
                        for j in rs:
                            if ops[j].eng != op.eng and j not in deps:
                                deps[j] = "rar"
            for k in op.w:
                j = last_w.get(k)
                if j is not None and deps.get(j) != "raw":
                    deps[j] = "waw"
                rs = readers.get(k)
                if rs:
                    for j in rs:
                        if j not in deps:
                            deps[j] = "war"
            if op.dma:
                q = op.eng
                idx = rr[q]
                rr[q] = (idx + 1) % len(self.dsems[q])
                key = (q, idx)
                if key in sem_last:
                    deps[sem_last[key]] = "raw"
                sem_last[key] = i
                sem_cnt[key] = sem_cnt.get(key, 0) + 1
                op.sem = self.dsems[q][idx]
                op.semval = 16 * sem_cnt[key]
            best = {}
            for j, kind in deps.items():
                if j == i:
                    continue
                pj = ops[j]
                if pj.dma:
                    op.deps.append(j)
                    continue
                if pj.eng == op.eng:
                    if op.eng == "pe":
                        continue
                    if not SAME_ENGINE_SYNC:
                        continue
                if pj.eng not in best or best[pj.eng] < j:
                    best[pj.eng] = j
            for e, j in best.items():
                op.deps.append(j)
                ops[j].inc = True
            for k in op.r:
                rs = readers.get(k)
                if rs is None:
                    readers[k] = [i]
                else:
                    eng = op.eng
                    rs[:] = [j for j in rs if ops[j].eng != eng or ops[j].dma]
                    rs.append(i)
            for k in op.w:
                last_w[k] = i
                readers[k] = []
        run = {e: 0 for e in self.esem}
        for op in ops:
            if op.dma:
                op.tok = (op.sem, op.semval)
            elif op.inc:
                run[op.eng] += 1
                op.tok = (self.esem[op.eng], run[op.eng])
        self.final_counts = run
        self.dma_final = {}
        for op in ops:
            if op.dma:
                self.dma_final[op.sem.name] = (op.sem, op.semval)

    def emit(self):
        nc = self.nc
        ops = self.ops
        per = {e: [] for e in self.engs}
        for op in ops:
            per[op.eng].append(op)
        esem = self.esem

        def run_engine(ename, eng):
            seen = {}
            for op in per[ename]:
                for j in op.deps:
                    sem, val = ops[j].tok
                    if seen.get(sem.name, 0) >= val:
                        continue
                    seen[sem.name] = val
                    eng.wait_ge(sem, val)
                inst = op.fn(eng)
                if self.annotate and op.label:
                    inst.annotate(op.label)
                if op.dma:
                    inst.then_inc(op.sem, 16)
                elif op.inc:
                    inst.then_inc(esem[ename], 1)
            if ename == "sp":
                for nm, (sem, val) in self.dma_final.items():
                    if seen.get(nm, 0) < val:
                        eng.wait_ge(sem, val)
                for e, sem in esem.items():
                    if self.final_counts[e] > 0:
                        eng.wait_ge(sem, self.final_counts[e])

        with nc.Block() as block:
            @block.sync
            def _(e):
                run_engine("sp", e)

            @block.tensor
            def _(e):
                run_engine("pe", e)

            @block.scalar
            def _(e):
                run_engine("act", e)

            @block.vector
            def _(e):
                run_engine("dve", e)

            @block.gpsimd
            def _(e):
                run_engine("pool", e)


class View:
    __slots__ = ("ap", "rg")

    def __init__(self, ap, rg):
        self.ap = ap
        self.rg = rg


class Buf:
    def __init__(self, ap, base, shape, esize):
        self.t = ap
        self.base = base
        self.shape = tuple(shape)
        self.es = esize
        st = []
        s = 1
        for n in reversed(self.shape):
            st.append(s)
            s *= n
        self.strides = tuple(reversed(st))
        self.nbytes = s * esize

    def v(self, *idx, p=None):
        key = (slice(None) if p is None else p,) + idx
        ap = self.t[key]
        lo = hi = 0
        for d, (n, stv) in enumerate(zip(self.shape, self.strides)):
            if d < len(idx):
                i = idx[d]
                if isinstance(i, slice):
                    a = 0 if i.start is None else i.start
                    b = n if i.stop is None else i.stop
                    lo += a * stv
                    hi += (b - 1) * stv
                else:
                    lo += i * stv
                    hi += i * stv
            else:
                hi += (n - 1) * stv
        return View(ap, (self.base + lo * self.es, self.base + (hi + 1) * self.es))


class Arena:
    def __init__(self, nc, st, nbytes):
        self.words = nbytes // 4
        self.t = st.enter_context(nc.sbuf_tensor("arena", [128, self.words], F32))
        self.top = 0
        self.peak = 0

    def alloc(self, shape, dt):
        es = 2 if dt == BF16 else 4
        n = 1
        for s in shape:
            n *= s
        nb = (n * es + PAGE - 1) // PAGE * PAGE
        off = self.top
        self.top += nb
        self.peak = max(self.peak, self.top)
        assert self.top <= self.words * 4, f"SBUF arena overflow {self.top}"
        ap = self.t[:, off // 4:(off + nb) // 4]
        if dt == BF16:
            ap = ap.bitcast(BF16)
        ap = ap[:, 0:n]
        if len(shape) == 2:
            ap = ap.rearrange("p (a b) -> p a b", b=shape[1])
        elif len(shape) == 3:
            ap = ap.rearrange("p (a b c) -> p a b c", b=shape[1], c=shape[2])
        elif len(shape) == 4:
            ap = ap.rearrange("p (a b c d) -> p a b c d", b=shape[1], c=shape[2], d=shape[3])
        return Buf(ap, off, shape, es)

    def at(self, off, shape, dt):
        save = self.top
        self.top = off
        b = self.alloc(shape, dt)
        self.top = save
        return b

    def mark(self):
        return self.top

    def release(self, m):
        self.top = m


class Ring:
    def __init__(self, arena, n, shape, dt):
        self.bufs = [arena.alloc(shape, dt) for _ in range(n)]
        self.i = 0

    def next(self):
        b = self.bufs[self.i]
        self.i = (self.i + 1) % len(self.bufs)
        return b


def _gla_consts(T, L):
    seg = np.arange(T) // L
    s = np.arange(T)[:, None]
    t = np.arange(T)[None, :]
    same = seg[:, None] == seg[None, :]
    tri_incl = np.where(same & (s <= t), -1.0 / 16, 0.0)
    tri_rev = np.where(same & (s > t), -1.0 / 16, 0.0)
    mask = np.where(same & (s <= t), 1.0, 0.0)
    nseg = T // L
    ind = np.zeros((T, nseg))
    ind[np.arange(T), seg] = -1.0 / 16
    rowm = np.zeros((T, nseg))
    rowm[np.arange(T), seg] = 1.0
    segs = [np.diag((seg == g).astype(np.float64)) for g in range(nseg)]
    return tri_incl, tri_rev, mask, ind, rowm, segs


class ConstPack:
    def __init__(self):
        self.cols = []
        self.off = {}
        self.n = 0

    def add(self, name, arr):
        arr = np.asarray(arr, np.float32)
        a = np.zeros((128, arr.shape[1]), np.float32)
        a[:arr.shape[0]] = arr
        self.off[name] = (self.n, arr.shape[1])
        self.cols.append(a)
        self.n += arr.shape[1]

    def pack(self):
        return np.ascontiguousarray(np.concatenate(self.cols, axis=1))


def rope_tables(pos):
    half = 32
    inv = 1.0 / (10000.0 ** (np.arange(half, dtype=np.float64) / half))
    ang = pos.astype(np.float64)[:, None] * inv[None, :]
    return np.cos(ang).astype(np.float32), np.sin(ang).astype(np.float32)


def build_consts(p):
    cf = ConstPack()
    cb = ConstPack()
    for tag, T, L in (("p", 128, 64), ("s", 64, 16)):
        tri_incl, tri_rev, mask, ind, rowm, segs = _gla_consts(T, L)
        cf.add("tri_incl_" + tag, tri_incl)
        cf.add("tri_rev_" + tag, tri_rev)
        cf.add("mask_" + tag, mask)
        cf.add("ind_" + tag, ind)
        cf.add("rowm_" + tag, rowm)
        for g, sg in enumerate(segs):
            cb.add(f"seg_{tag}{g}", sg)
    cf.add("ones_f", np.ones((128, 128)))
    cf.add("selw", np.tile(np.array([[1.0 - p, float(p)]]), (128, 1)))
    cb.add("ident", np.eye(128))
    cb.add("ones", np.ones((128, 128)))
    cb.add("sel0", np.eye(128) * (1.0 if p == 0 else 0.0))
    cb.add("sel1", np.eye(128) * (1.0 if p == 1 else 0.0))
    k = np.arange(128)[:, None]
    q = np.arange(128)[None, :]
    diag = ((k // 64) <= (q // 64)).astype(np.float32)
    lo = diag if p == 0 else np.ones((128, 128), np.float32)
    hi = np.zeros((128, 128), np.float32) if p == 0 else diag
    cb.add("mask_lo4", np.tile(lo, (1, 4)))
    cb.add("mask_hi4", np.tile(hi, (1, 4)))
    for s in range(4):
        m = np.zeros((64, 128), np.float32)
        m[16 * s:16 * s + 16, :] = 1.0
        cb.add(f"smask{s}", m)
    posn = np.concatenate([np.arange(2048), 4096 + np.tile(np.arange(16), 4)])
    posn = np.concatenate([posn, np.zeros(17 * 128 - posn.size, np.int64)])
    c, s_ = rope_tables(posn)
    tk = np.concatenate([c, c, s_, s_], axis=1).reshape(17, 128, 128).transpose(1, 0, 2).reshape(128, 17 * 128)
    cf.add("ropek", tk)
    own_pos = np.concatenate([np.concatenate([np.arange(128) + (2 * i + p) * 128 for i in range(8)]),
                              4096 + np.tile(np.arange(16), 4)])
    c, s_ = rope_tables(own_pos)
    cf.add("cosq", np.concatenate([c, c], axis=1).T)
    cf.add("sinq", np.concatenate([s_, s_], axis=1).T)
    return cf, cb


def build_program(cf_off, cf_n, cb_off, cb_n, debug=None, stop_after=None):
    nc = bass.Bass("TRN2", target_bir_lowering=False)

    skip_in = set()
    if stop_after in ("A1", "A2"):
        skip_in = {"c_ckv", "c_kr", "w_out", "w_gate", "w_up", "w_down", "w_pg", "w_ple", "w_uq", "w_uk", "w_uv", "p_own"}
    if stop_after in ("B", "C"):
        skip_in = {"w_out", "w_gate", "w_up", "w_down", "w_pg", "w_ple", "p_own"}
    declared = []

    def din(name, shape):
        if name in skip_in:
            return None
        declared.append(name)
        return nc.dram_tensor(name, list(shape), F32, kind="ExternalInput").ap()

    def dout(name, shape):
        return nc.dram_tensor(name, list(shape), F32, kind="ExternalOutput").ap()

    x_nat = din("x_nat", [NNAT, D])
    x_own = din("x_own", [NOWN, D])
    p_own = din("p_own", [NOWN, 256])
    c_ckv = din("c_ckv", [4, 4096, 512])
    c_kr = din("c_kr", [4, 4096, 64])
    st_in = din("st_in", [4, 4, 128, 256])
    constf = din("constf", [128, cf_n])
    constb = din("constb", [128, cb_n])
    gvec = din("gvec", [128, 38])
    g_kv_b = din("g_kv_b", [128, 512])
    g_pm_b = din("g_pm_b", [128, D])
    g_pf_b = din("g_pf_b", [128, D])
    w_in = din("w_in", [D, 4176])
    w_uq = din("w_uq", [512, 1536])
    w_uk = din("w_uk", [512, 1024])
    w_uv = din("w_uv", [512, 1024])
    w_ga = din("w_ga", [17, 512])
    w_out = din("w_out", [D, D])
    w_gate = din("w_gate", [D, DFF])
    w_up = din("w_up", [D, DFF])
    w_down = din("w_down", [DFF, D])
    w_ple = din("w_ple", [256, D])
    w_pg = din("w_pg", [D, D])

    y_out = dout("y_out", [NOWN, D])
    ckv_out = dout("ckv_out", [NNAT, 512])
    kr_out = dout("kr_out", [NNAT, 64])
    gp_out = dout("gp_out", [4, 128, 256])
    gs_out = dout("gs_out", [4, 4, 128, 256])
    dbg_outs = {}
    if debug:
        for nm, (shp, dtn) in debug.items():
            dbg_outs[nm] = nc.dram_tensor("dbg_" + nm, list(shp), BF16 if dtn == "bf16" else F32, kind="ExternalOutput").ap()

    with ExitStack() as st:
        S = Sched(nc, st)
        A = Arena(nc, st, 204 * 1024)
        banks_t = [st.enter_context(nc.psum_tensor(f"ps{i}", [128, 512], F32)) for i in range(8)]
        PS = [Buf(banks_t[i][:, :], PSUM_BASE + i * 2048, (512,), 4) for i in range(8)]
        PSB = [Buf(banks_t[i][:, :].bitcast(BF16), PSUM_BASE + i * 2048, (1024,), 2) for i in range(8)]

        def dma(q, out_ap, in_ap, r=(), w=()):
            S.add(q, lambda e: e.dma_start(out=out_ap, in_=in_ap), r=r, w=w, dma=True)

        def load(q, dst, src_ap):
            dma(q, dst.ap, src_ap, w=[dst])

        store_q = ["pool"]

        def store(dst_ap, src):
            dma(store_q[0], dst_ap, src.ap, r=[src])

        def mm(out, lhsT, rhs, start, stop):
            S.add("pe", lambda e: e.matmul(out.ap, lhsT=lhsT.ap, rhs=rhs.ap, start=start, stop=stop,
                                           skip_group_check=True),
                  r=[lhsT, rhs], w=[out])

        def tr(out, in_, ident):
            S.add("pe", lambda e: e.transpose(out=out.ap, in_=in_.ap, identity=ident.ap), r=[in_, ident], w=[out])

        def act(out, in_, func, scale=None, bias=None, accum=None, eng="act"):
            kw = {}
            if scale is not None:
                kw["scale"] = scale.ap if isinstance(scale, View) else scale
            if bias is not None:
                kw["bias"] = bias.ap if isinstance(bias, View) else bias
            r = [in_] + [x for x in (scale, bias) if isinstance(x, View)]
            w = [out]
            if accum is not None:
                kw["accum_out"] = accum.ap
                w.append(accum)
            S.add("act", lambda e: e.activation(out=out.ap, in_=in_.ap, func=func, **kw), r=r, w=w)

        def tt(eng, out, in0, in1, op):
            S.add(eng, lambda e: e.tensor_tensor(out=out.ap, in0=in0.ap, in1=in1.ap, op=op), r=[in0, in1], w=[out])

        def ts(eng, out, in0, s1, op0, s2=None, op1=None):
            r = [in0] + [x for x in (s1, s2) if isinstance(x, View)]
            a1 = s1.ap if isinstance(s1, View) else s1
            a2 = s2.ap if isinstance(s2, View) else s2
            if op1 is None:
                S.add(eng, lambda e: e.tensor_scalar(out=out.ap, in0=in0.ap, scalar1=a1, scalar2=None, op0=op0), r=r, w=[out])
            else:
                S.add(eng, lambda e: e.tensor_scalar(out=out.ap, in0=in0.ap, scalar1=a1, scalar2=a2, op0=op0, op1=op1), r=r, w=[out])

        def stt(out, in0, scalar, in1, op0, op1):
            r = [in0, in1] + ([scalar] if isinstance(scalar, View) else [])
            sc = scalar.ap if isinstance(scalar, View) else scalar
            S.add("dve", lambda e: e.scalar_tensor_tensor(out=out.ap, in0=in0.ap, scalar=sc, in1=in1.ap, op0=op0, op1=op1),
                  r=r, w=[out])

        def cp(eng, out, in_):
            if eng == "act":
                S.add("act", lambda e: e.copy(out=out.ap, in_=in_.ap), r=[in_], w=[out])
            else:
                S.add(eng, lambda e: e.tensor_copy(out=out.ap, in_=in_.ap), r=[in_], w=[out])

        def recip(out, in_):
            S.add("dve", lambda e: e.reciprocal(out=out.ap, in_=in_.ap), r=[in_], w=[out])

        def rstd_act(out, in_, inv_n):
            act(out, in_, AF.Ln, scale=inv_n, bias=EPS)
            act(out, out, AF.Exp, scale=-0.5)

        def memset(eng, out, val):
            S.add(eng, lambda e: e.memset(out.ap, val), w=[out])

        def bview(view, shape_fn):
            return View(shape_fn(view.ap), view.rg)

        def dbg(name, view):
            if debug and name in dbg_outs:
                dma("sp", dbg_outs[name], view.ap, r=[view])

        n_core = cf_off["ropek"][0]
        CF = A.alloc((n_core,), F32)
        CB = A.alloc((cb_n,), BF16)
        GV = A.alloc((38,), F32)
        junk = A.alloc((2048,), BF16)
        mixT = A.alloc((16, NOWN), BF16)
        m0 = A.mark()
        m_tail = m0
        CBf = A.alloc((cb_n,), F32)
        load("sp", CF.v(), constf[:, 0:n_core])
        load("sp", CBf.v(), constb)
        load("sp", GV.v(), gvec)
        cp("dve", CB.v(), CBf.v())
        A.release(m0)

        def cfv(name, rows=128, c0=0, c1=None):
            o, n = cf_off[name]
            c1 = n if c1 is None else c1
            return CF.v(slice(o + c0, o + c1), p=slice(0, rows))

        def cbv(name, rows=128, c0=0, c1=None):
            o, n = cb_off[name]
            c1 = n if c1 is None else c1
            return CB.v(slice(o + c0, o + c1), p=slice(0, rows))

        ident = lambda T: cbv("ident", T, 0, T)
        w_in_k = w_in.rearrange("(k p) n -> p k n", p=128)
        nat_blocks = [(j, j * 128, 128) for j in range(16)] + [(16, 2048, 64)]

        import os as _os
        S.annotate = bool(_os.environ.get("ANNOTATE"))

        def LB(name):
            S.label = name

        def finish():
            S.finalize()
            S.emit()
            nc._declared = declared
            return nc, A.peak

        def norm_transpose(xs, T, gcols, dst_fn, psb_a, psb_b, ss, xn, nfeat=D, defer=None, sq_junk=None):
            pT = slice(0, T)
            nch = nfeat // 128
            jk = sq_junk if sq_junk is not None else junk
            do = (lambda f: f()) if defer is None else defer.append
            do(lambda: act(jk.v(slice(0, nfeat), p=pT), xs.v(slice(0, nfeat), p=pT), AF.Square, accum=ss.v(slice(0, 1), p=pT)))
            do(lambda: rstd_act(ss.v(slice(1, 2), p=pT), ss.v(slice(0, 1), p=pT), 1.0 / nfeat))
            do(lambda: ts("dve", xn.v(slice(0, nfeat), p=pT), xs.v(slice(0, nfeat), p=pT), ss.v(slice(1, 2), p=pT), ALU.mult))
            for c4 in range(nch // 4):
                pb = (psb_a, psb_b)[c4 % 2]

                def grp(pb=pb, c4=c4):
                    for jj in range(4):
                        c = c4 * 4 + jj
                        tr(pb.v(slice(jj * 128, jj * 128 + T)), xn.v(slice(c * 128, (c + 1) * 128), p=pT), ident(T))
                    src = bview(pb.v(slice(0, 512)), lambda ap: ap.rearrange("p (a b) -> p a b", b=128)[:, :, 0:T])
                    g = bview(GV.v(slice(gcols + c4 * 4, gcols + c4 * 4 + 4)),
                              lambda ap: ap.unsqueeze(2).broadcast_to([128, 4, T]))
                    tt("dve", dst_fn(c4 * 4, c4 * 4 + 4), src, g, ALU.mult)
                do(grp)

        mA2 = A.mark()
        Wn2 = A.alloc((16, 2064), BF16)
        for kg in range(4):
            load("pool", Wn2.v(slice(kg * 4, kg * 4 + 4)), w_in_k[:, kg * 4:(kg + 1) * 4, 1088:3152])
        WGA = A.alloc((512,), BF16)
        load("pool", WGA.v(p=slice(0, 17)), w_ga)
        glrT = A.alloc((128,), BF16)
        memset("dve", glrT.v(p=slice(0, 32)), 1.0)
        xs_ring = Ring(A, 3, (2048,), F32)
        xn1 = A.alloc((2048,), BF16)
        junk2 = A.alloc((2048,), BF16)
        aT_ring = Ring(A, 3, (16, 128), BF16)
        ss_ring = Ring(A, 6, (16,), F32)
        la = A.alloc((512,), F32)
        E_ring = Ring(A, 2, (512,), F32)
        qt = A.alloc((512,), BF16)
        kt = A.alloc((512,), BF16)
        kh = A.alloc((512,), BF16)
        khm = A.alloc((4, 512), BF16)
        vb = A.alloc((1024,), BF16)
        TT = A.alloc((4, 512), BF16)
        at = A.alloc((4, 128), BF16)
        on2 = [A.alloc((4, 256), BF16) for _ in range(2)]
        sel_q = []
        eb = A.alloc((16,), F32)
        ssg = A.alloc((8,), F32)
        Sst = A.alloc((4, 256), F32)
        Sbf = [A.alloc((4, 256), BF16) for _ in range(2)]
        Ssm = [A.at(Wn2.base + g * 4096, (4, 256), F32) for g in range(4)]
        Ssmb = [A.at(Wn2.base + 16384 + g * 2048, (4, 256), BF16) for g in range(4)]
        memset("dve", Sst.v(), 0.0)
        memset("dve", Sbf[0].v(), 0.0)
        sbf_cur = [0]
        DKS = 128 ** -0.5

        norm_q = []

        def a2_norm(blk, defer=True):
            (j, tok0, T) = blk
            LB("A2.norm")
            pT = slice(0, T)
            xs = xs_ring.next()
            aT = aT_ring.next()
            ss = ss_ring.next()
            load("sp", xs.v(p=pT), x_nat[tok0:tok0 + T, :])
            norm_transpose(xs, T, 0, lambda c0, c1: aT.v(slice(c0, c1), slice(0, T)), PSB[5], PSB[5], ss, xn1,
                           defer=(norm_q if defer else None), sq_junk=junk2)
            return aT

        def nfiller(n):
            if not norm_q:
                return
            saved = S.label
            LB("A2.norm")
            for _ in range(min(n, len(norm_q))):
                norm_q.pop(0)()
            S.label = saved

        def a2_proj_pieces(blk, aT):
            (j, tok0, T) = blk
            pT = slice(0, T)
            pieces = []
            for k in range(16):
                for b in range(4):
                    def piece(k=k, b=b):
                        mm(PS[b].v(p=pT), aT.v(k, slice(0, T)), Wn2.v(k, slice(b * 512, (b + 1) * 512)), k == 0, k == 15)
                    pieces.append(piece)
            return pieces

        fill_q = []
        sel_hold = [0]

        def filler(n):
            nfiller(1 if n < 100 else 1000)
            if sel_q and sel_hold[0] <= 0:
                sel_q.pop(0)()
            sel_hold[0] -= 1
            if not fill_q:
                return
            saved = S.label
            LB("A2.proj")
            for _ in range(min(n, len(fill_q))):
                fill_q.pop(0)()
            S.label = saved

        def a2_decay(blk, aT):
            (j, tok0, T) = blk
            LB("A2.decay")
            pT = slice(0, T)
            prompt = j < 16
            tag = "p" if prompt else "s"
            nseg = 2 if prompt else 4
            Pq, Pk, Pv0, Pv1, Pg, Px = PS[0], PS[1], PS[2], PS[3], PS[4], PS[5]
            for k in range(16):
                mm(Pg.v(slice(0, T), p=slice(0, 16)), Wn2.v(k, slice(2048, 2064)), aT.v(k, slice(0, T)), k == 0, k == 15)
            cp("act", glrT.v(slice(0, T), p=slice(0, 16)), Pg.v(slice(0, T), p=slice(0, 16)))
            mm(Px.v(p=pT), glrT.v(slice(0, T), p=slice(0, 17)), WGA.v(p=slice(0, 17)), True, True)
            act(la.v(p=pT), Px.v(p=pT), AF.Exp, scale=-1.0)
            act(la.v(p=pT), la.v(p=pT), AF.Ln, bias=1.0)
            cp("act", vb.v(slice(0, 512), p=pT), Pv0.v(p=pT))
            cp("dve", vb.v(slice(512, 1024), p=pT), Pv1.v(p=pT))
            Pb, Prb, Pbl = PS[4], PS[5], PS[4]
            mm(Pb.v(p=pT), cfv("tri_incl_" + tag, T, 0, T), la.v(p=pT), True, True)
            mm(Prb.v(p=pT), cfv("tri_rev_" + tag, T, 0, T), la.v(p=pT), True, True)
            E1 = E_ring.next()
            act(E1.v(p=pT), Pb.v(p=pT), AF.Exp)
            stt(qt.v(p=pT), Pq.v(p=pT), DKS, E1.v(p=pT), ALU.mult, ALU.mult)
            E2 = E_ring.next()
            act(E2.v(p=pT), Pb.v(p=pT), AF.Exp, scale=-1.0)
            tt("dve", kt.v(p=pT), Pk.v(p=pT), E2.v(p=pT), ALU.mult)
            E3 = E_ring.next()
            act(E3.v(p=pT), Prb.v(p=pT), AF.Exp)
            tt("dve", kh.v(p=pT), Pk.v(p=pT), E3.v(p=pT), ALU.mult)
            for h in range(4):
                mm(Pbl.v(slice(h * nseg, (h + 1) * nseg)), la.v(slice(h * 128, (h + 1) * 128), p=pT),
                   cfv("ind_" + tag, T), h == 0, h == 3)
            act(eb.v(slice(0, 4 * nseg)), Pbl.v(slice(0, 4 * nseg)), AF.Exp)
            for g in range(nseg):
                ts("dve", khm.v(g, p=pT), kh.v(p=pT), cfv("rowm_" + tag, T, g, g + 1), ALU.mult)

        def a2_rest(blk):
            (j, tok0, T) = blk
            on = on2[j % 2]
            sel_hold[0] = 5
            pT = slice(0, T)
            prompt = j < 16
            tag = "p" if prompt else "s"
            nseg = 2 if prompt else 4
            LB("A2.trans")
            W = (2 + nseg) * T
            for h in range(4):
                PSt = PS[4 + h % 2]
                hs = slice(h * 128, (h + 1) * 128)
                mm(PSt.v(slice(0, T)), kt.v(hs, p=pT), ident(T), True, False)
                mm(PSt.v(slice(T, 2 * T)), qt.v(hs, p=pT), ident(T), False, False)
                for g in range(nseg):
                    mm(PSt.v(slice((2 + g) * T, (3 + g) * T)), qt.v(hs, p=pT), cbv(f"seg_{tag}{g}", T, 0, T),
                       False, g == nseg - 1)
                cp("act" if h % 2 == 0 else "dve", TT.v(h, slice(0, W)), PSt.v(slice(0, W)))
                filler(4)
            PSa = PS[4]
            for h in range(4):
                mm(PSa.v(slice(h * T, (h + 1) * T), p=pT), TT.v(h, slice(0, T)), TT.v(h, slice(T, 2 * T)), h == 0, h == 3)
            msk = bview(cfv("mask_" + tag, T, 0, T), lambda ap: ap.unsqueeze(1).broadcast_to([T, 4, T]))
            tt("dve", at.v(slice(None), slice(0, T), p=pT),
               bview(PSa.v(slice(0, 4 * T), p=pT), lambda ap: ap.rearrange("p (a b) -> p a b", b=T)), msk, ALU.mult)
            filler(4)
            LB("A2.state")
            if not prompt:
                for g in range(4):
                    load("sp", Ssm[g].v(), st_in[g].rearrange("h d v -> d h v"))
                    cp("act", Ssmb[g].v(), Ssm[g].v())
            Po = PS[4]
            PSu = PS[5]
            for hp in range(2):
                LB("A2.state")
                for h in (2 * hp, 2 * hp + 1):
                    out = Po.v(slice((h % 2) * 256, (h % 2 + 1) * 256), p=pT)
                    vh = vb.v(slice(h * 256, (h + 1) * 256), p=pT)
                    mm(out, at.v(h, slice(0, T), p=pT), vh, h % 2 == 0, False)
                for g in range(nseg):
                    for h in (2 * hp, 2 * hp + 1):
                        out = Po.v(slice((h % 2) * 256, (h % 2 + 1) * 256), p=pT)
                        vh = vb.v(slice(h * 256, (h + 1) * 256), p=pT)
                        if prompt:
                            S_f, S_b = Sst, Sbf[(sbf_cur[0] + g) % 2]
                            S_bn = Sbf[(sbf_cur[0] + g + 1) % 2]
                        else:
                            S_f, S_b, S_bn = Ssm[g], Ssmb[g], None
                        mm(out, TT.v(h, slice((2 + g) * T, (3 + g) * T)), S_b.v(h), False, g == nseg - 1)
                        PSu = PS[5] if h % 2 == 0 else PS[7]
                        mm(PSu.v(slice(0, 256)), khm.v(g, slice(h * 128, (h + 1) * 128), p=pT), vh, True, True)
                        stt(S_f.v(h), S_f.v(h), eb.v(slice(h * nseg + g, h * nseg + g + 1)), PSu.v(slice(0, 256)),
                            ALU.mult, ALU.add)
                        if S_bn is not None:
                            cp("act", S_bn.v(h), S_f.v(h))
                        filler(5 if prompt else 3)
                LB("A2.onorm")
                for h in (2 * hp, 2 * hp + 1):
                    act(junk.v(slice((h % 2) * 256, (h % 2 + 1) * 256), p=pT),
                        Po.v(slice((h % 2) * 256, (h % 2 + 1) * 256), p=pT), AF.Square, accum=ssg.v(slice(h, h + 1), p=pT))
                hsl2 = slice(2 * hp, 2 * hp + 2)
                hsl3 = slice(4 + 2 * hp, 6 + 2 * hp)
                rstd_act(ssg.v(hsl3, p=pT), ssg.v(hsl2, p=pT), 1.0 / 256)
                for h in (2 * hp, 2 * hp + 1):
                    ts("dve", on.v(h, p=pT), Po.v(slice((h % 2) * 256, (h % 2 + 1) * 256), p=pT),
                       ssg.v(slice(4 + h, 5 + h), p=pT), ALU.mult)
                filler(4)
            if prompt:
                sbf_cur[0] = (sbf_cur[0] + nseg) % 2
            else:
                for g in range(4):
                    store(gs_out[g].rearrange("h d v -> d h v"), Ssm[g].v())
            o0 = (j // 2) * 128 if prompt else 1024
            Ts = T
            for half in range(2):
                def sel_piece(half=half):
                    saved = S.label
                    LB("A2.sel")
                    pb = PSB[6]
                    for ff in range(4):
                        f = half * 4 + ff
                        tr(pb.v(slice(ff * 128, ff * 128 + Ts)), on.v(f // 2, slice((f % 2) * 128, (f % 2 + 1) * 128), p=pT), ident(T))
                    src = bview(pb.v(slice(0, 512)), lambda ap: ap.rearrange("p (a b) -> p a b", b=128)[:, :, 0:Ts])
                    dst = mixT.v(slice(8 + half * 4, 12 + half * 4), slice(o0, o0 + Ts))
                    if not prompt:
                        cp("dve", dst, src)
                    elif j % 2 == 0:
                        ts("dve", dst, src, cfv("selw", 128, 0, 1), ALU.mult)
                    else:
                        stt(dst, src, cfv("selw", 128, 1, 2), dst, ALU.mult, ALU.add)
                    S.label = saved
                sel_q.append(sel_piece)
            filler(1000)

        aTs = {0: a2_norm(nat_blocks[0], defer=False), 1: a2_norm(nat_blocks[1], defer=False)}
        fill_q.extend(a2_proj_pieces(nat_blocks[0], aTs[0]))
        filler(1000)
        for bidx, blk in enumerate(nat_blocks):
            a2_decay(blk, aTs[bidx])
            if bidx + 2 < len(nat_blocks):
                aTs[bidx + 2] = a2_norm(nat_blocks[bidx + 2])
            if bidx + 1 < len(nat_blocks):
                fill_q.extend(a2_proj_pieces(nat_blocks[bidx + 1], aTs[bidx + 1]))
            a2_rest(blk)
        while sel_q:
            sel_q.pop(0)()
        hsl = Sst.v()
        store(gp_out.rearrange("h d v -> d h v"), hsl)
        dbg("mixT", mixT.v(slice(8, 16)))
        A.release(mA2)
        if stop_after == "A2":
            return finish()

        LB("B")
        q_nopeT = A.alloc((8, NOWN), BF16)
        q_ropeT = A.alloc((8, NOWN), BF16)
        mB = A.mark()
        import os
        lm = (lambda name: print("LANDMARK", name, len(S.ops))) if os.environ.get("LANDMARKS") else (lambda name: None)
        lm("B start")
        ROPQ = A.alloc((2, NOWN), F32)
        cq_o = cf_off["cosq"][0]
        sq_o = cf_off["sinq"][0]
        load("sp", ROPQ.v(0), constf[:, cq_o:cq_o + NOWN])
        load("sp", ROPQ.v(1), constf[:, sq_o:sq_o + NOWN])
        WUQ = A.alloc((4, 2048), BF16)
        load("pool", WUQ.v(slice(None), slice(0, 1536)), w_uq.rearrange("(k p) n -> p k n", p=128))
        w3 = lambda a, b: bview(WUQ.v(slice(None), slice(0, 1536)),
                                lambda ap: ap.rearrange("p k (h e) -> p k h e", e=192)[:, :, :, a:b])
        r3 = lambda a, b: bview(WUQ.v(slice(None), slice(1536, 2048)),
                                lambda ap: ap.rearrange("p k (h e) -> p k h e", e=64)[:, :, :, a:b])
        _o, _i = r3(0, 32), w3(160, 192)
        S.add("act", lambda e: e.mul(out=_o.ap, in_=_i.ap, mul=-1.0), r=[_i], w=[_o])
        cp("dve", r3(32, 64), w3(128, 160))
        c_qnT = A.alloc((4, NOWN), BF16)
        aTg2 = [A.at(mixT.base, (16, 256), BF16), A.at(mixT.base + 16 * 256 * 2, (16, 256), BF16)]
        cqf = A.alloc((4, 256), F32)
        sqb = A.alloc((4, 256), BF16)
        rsb = A.alloc((256,), F32)
        sgb_ring = Ring(A, 2, (256,), F32)
        WOWN = A.alloc((16, 1536), BF16)
        load("pool", WOWN.v(slice(None), slice(0, 512)), w_in_k[:, :, 0:512])
        load("pool", WOWN.v(slice(None), slice(512, 1024)), w_in_k[:, :, 3152:3664])
        load("pool", WOWN.v(slice(None), slice(1024, 1536)), w_in_k[:, :, 3664:4176])
        xs_ring = Ring(A, 2, (2048,), F32)
        xnB = A.alloc((2048,), BF16)
        ss_ring = Ring(A, 8, (16,), F32)
        t_ring = Ring(A, 2, (256,), F32)
        pbank = [0]

        def next_bank():
            b = PS[pbank[0] % 4]
            pbank[0] += 1
            return b

        b_groups = [(0, 256), (256, 256), (512, 256), (768, 256), (1024, 64)]
        b_q = []

        def b_norm(gi, defer=True):
            (t0, N) = b_groups[gi]
            aTg_ = aTg2[gi % 2]
            nb = (N + 127) // 128
            for bi in range(nb):
                T = min(128, N - bi * 128)
                xs = xs_ring.next()
                ss = ss_ring.next()

                def ld(xs=xs, T=T, bi=bi):
                    load("sp", xs.v(p=slice(0, T)), x_own[t0 + bi * 128:t0 + bi * 128 + T, :])
                if defer:
                    b_q.append(ld)
                else:
                    ld()
                norm_transpose(xs, T, 0, lambda c0, c1, bi=bi, T=T: aTg_.v(slice(c0, c1), slice(bi * 128, bi * 128 + T)),
                               PSB[6], PSB[7], ss, xnB, defer=(b_q if defer else None))

        def b_fill(n):
            for _ in range(min(n, len(b_q))):
                b_q.pop(0)()

        b_norm(0, False)
        for gi, (t0, N) in enumerate(b_groups):
            aTg = aTg2[gi % 2]
            if gi + 1 < len(b_groups):
                b_norm(gi + 1)
            tk = slice(t0, t0 + N)
            nn = slice(0, N)
            lm("B norm done")
            for m in range(4):
                P = next_bank()
                for k in range(16):
                    mm(P.v(nn), WOWN.v(k, slice(m * 128, (m + 1) * 128)), aTg.v(k, nn), k == 0, k == 15)
                cp("dve", cqf.v(m, nn), P.v(nn))
                act(sqb.v(m, nn), P.v(nn), AF.Square)
                b_fill(1)
            lm("B cq mm done")
            Pss = PS[4]
            for m in range(4):
                mm(Pss.v(nn), cbv("ones"), sqb.v(m, nn), m == 0, m == 3)
            rstd_act(rsb.v(nn), Pss.v(nn), 1.0 / 512)
            for m in range(4):
                stt(c_qnT.v(m, tk), cqf.v(m, nn), GV.v(slice(32 + m, 33 + m)), rsb.v(nn), ALU.mult, ALU.mult)
            lm("B cqn done")
            for f in range(8):
                P = next_bank()
                for k in range(16):
                    mm(P.v(nn), WOWN.v(k, slice(512 + f * 128, 512 + (f + 1) * 128)), aTg.v(k, nn), k == 0, k == 15)
                sg = sgb_ring.next()
                act(sg.v(nn), P.v(nn), AF.Silu)
                stt(mixT.v(8 + f, tk), sg.v(nn), GV.v(slice(36 + f % 2, 37 + f % 2)), mixT.v(8 + f, tk), ALU.mult, ALU.mult)
                b_fill(2)
            lm("B rg done")
            for h in range(8):
                P = next_bank()
                for c in range(4):
                    mm(P.v(nn), WUQ.v(c, slice(h * 192, h * 192 + 128)), c_qnT.v(c, tk), c == 0, c == 3)
                cp("act", q_nopeT.v(h, tk), P.v(nn))
                Px, Pxr = (PS[5], PS[4]) if h % 2 == 0 else (PS[6], PS[7])
                h64 = slice(0, 64)
                for c in range(4):
                    mm(Px.v(nn, p=h64), WUQ.v(c, slice(h * 192 + 128, h * 192 + 192)), c_qnT.v(c, tk), c == 0, c == 3)
                for c in range(4):
                    mm(Pxr.v(nn, p=h64), WUQ.v(c, slice(1536 + h * 64, 1536 + (h + 1) * 64)), c_qnT.v(c, tk), c == 0, c == 3)
                ta = t_ring.next()
                tb = t_ring.next()
                tt("dve", ta.v(nn, p=h64), Px.v(nn, p=h64), ROPQ.v(0, tk, p=h64), ALU.mult)
                tt("dve", tb.v(nn, p=h64), Pxr.v(nn, p=h64), ROPQ.v(1, tk, p=h64), ALU.mult)
                tt("dve", q_ropeT.v(h, tk, p=h64), ta.v(nn, p=h64), tb.v(nn, p=h64), ALU.add)
                b_fill(1)
            b_fill(1000)
        lm("B end")
        dbg("qn", q_nopeT.v())
        dbg("qr", q_ropeT.v(p=slice(0, 64)))
        A.release(mB)
        if stop_after == "B":
            return finish()

        LB("A1")
        ckv_tok = A.alloc((17, 576), BF16)
        ckvT = A.alloc((5, NNAT), BF16)
        mA1 = A.mark()
        Wn1 = A.alloc((16, 640), BF16)
        load("pool", Wn1.v(slice(None), slice(0, 576)), w_in_k[:, :, 512:1088])
        S.add("act", lambda e: e.mul(out=Wn1.v(slice(None), slice(576, 608)).ap,
                                     in_=Wn1.v(slice(None), slice(544, 576)).ap, mul=-1.0),
              r=[Wn1.v(slice(None), slice(544, 576))], w=[Wn1.v(slice(None), slice(576, 608))])
        cp("dve", Wn1.v(slice(None), slice(608, 640)), Wn1.v(slice(None), slice(512, 544)))
        GKV = A.alloc((512,), F32)
        load("sp", GKV.v(), g_kv_b)
        ROPK = A.alloc((17 * 128,), F32)
        ropek_o = cf_off["ropek"][0]
        load("sp", ROPK.v(), constf[:, ropek_o:ropek_o + 17 * 128])
        xs_ring = Ring(A, 3, (2048,), F32)
        xn_ring = Ring(A, 2, (2048,), BF16)
        aT_ring = Ring(A, 3, (16, 128), BF16)
        ss_ring = Ring(A, 8, (16,), F32)
        ckvf_ring = Ring(A, 2, (576,), F32)
        tmp_ring = Ring(A, 2, (128,), F32)

        a1_q = []

        def a1_stage0(blk, defer=True):
            (j, tok0, T) = blk
            pT = slice(0, T)
            xs = xs_ring.next()
            xn = xn_ring.next()
            aT = aT_ring.next()
            ss = ss_ring.next()
            load("sp", xs.v(p=pT), x_nat[tok0:tok0 + T, :])
            norm_transpose(xs, T, 0, lambda c0, c1: aT.v(slice(c0, c1), slice(0, T)), PSB[4], PSB[5], ss, xn,
                           defer=(a1_q if defer else None))
            return aT

        def a1_fill(n):
            for _ in range(min(n, len(a1_q))):
                a1_q.pop(0)()

        a1T = {0: a1_stage0(nat_blocks[0], False), 1: a1_stage0(nat_blocks[1], False)}
        for bidx, (j, tok0, T) in enumerate(nat_blocks):
            pT = slice(0, T)
            aT = a1T[bidx]
            if bidx + 2 < len(nat_blocks):
                a1T[bidx + 2] = a1_stage0(nat_blocks[bidx + 2])
            Pc = PS[j % 2]
            Pr = PS[2 + j % 2]
            for k in range(16):
                mm(Pc.v(p=pT), aT.v(k, slice(0, T)), Wn1.v(k, slice(0, 512)), k == 0, k == 15)
                if k % 4 == 3:
                    a1_fill(1)
            for k in range(16):
                mm(Pr.v(slice(0, 128), p=pT), aT.v(k, slice(0, T)), Wn1.v(k, slice(512, 640)), k == 0, k == 15)
                if k % 4 == 3:
                    a1_fill(1)
            a1_fill(100)
            ss2 = ss_ring.next()
            act(junk.v(slice(0, 512), p=pT), Pc.v(p=pT), AF.Square, accum=ss2.v(slice(0, 1), p=pT))
            rstd_act(ss2.v(slice(1, 2), p=pT), ss2.v(slice(0, 1), p=pT), 1.0 / 512)
            cf_ = ckvf_ring.next()
            stt(cf_.v(slice(0, 512), p=pT), Pc.v(p=pT), ss2.v(slice(1, 2), p=pT), GKV.v(p=pT), ALU.mult, ALU.mult)
            tmp = tmp_ring.next()
            tt("dve", tmp.v(p=pT), Pr.v(slice(0, 128), p=pT), ROPK.v(slice(j * 128, (j + 1) * 128), p=pT), ALU.mult)
            tt("dve", cf_.v(slice(512, 576), p=pT), tmp.v(slice(0, 64), p=pT), tmp.v(slice(64, 128), p=pT), ALU.add)
            store(ckv_out[tok0:tok0 + T, :], cf_.v(slice(0, 512), p=pT))
            store(kr_out[tok0:tok0 + T, :], cf_.v(slice(512, 576), p=pT))
            cp("act", ckv_tok.v(j, p=pT), cf_.v(p=pT))
            pb = PSB[6]
            pb2 = PSB[7]
            for c in range(4):
                tr(pb.v(slice(c * 128, c * 128 + T)), ckv_tok.v(j, slice(c * 128, (c + 1) * 128), p=pT), ident(T))
            tr(pb2.v(slice(0, T), p=slice(0, 64)), ckv_tok.v(j, slice(512, 576), p=pT), ident(T))
            src = bview(pb.v(slice(0, 512)), lambda ap: ap.rearrange("p (a b) -> p a b", b=128)[:, :, 0:T])
            cp("dve", ckvT.v(slice(0, 4), slice(tok0, tok0 + T)), src)
            cp("act", ckvT.v(4, slice(tok0, tok0 + T), p=slice(0, 64)), pb2.v(slice(0, T), p=slice(0, 64)))
        dbg("ckvT", ckvT.v())
        A.release(mA1)
        if stop_after == "A1":
            return finish()

        LB("C.prep")
        mC = A.mark()
        WUKT = A.alloc((8, 512), BF16)
        WUV = A.alloc((4, 1024), BF16)
        load("pool", WUV.v(), w_uv.rearrange("(k p) n -> p k n", p=128))
        q_cat = A.alloc((4, 2, 512), BF16)
        pT_ring = Ring(A, 3, (512,), BF16)
        rl = A.alloc((512,), F32)
        onb = A.alloc((4, 512), BF16)
        q_cat_s = A.alloc((4, 8, 64), BF16)
        onb_s = A.alloc((4, 128), BF16)
        mC2 = A.mark()
        WUKb = A.alloc((4, 1024), BF16)
        load("pool", WUKb.v(), w_uk.rearrange("(k p) n -> p k n", p=128))
        for h in range(8):
            pb = PSB[6 + h % 2]
            for cc in range(4):
                tr(pb.v(slice(cc * 128, (cc + 1) * 128)), WUKb.v(cc, slice(h * 128, (h + 1) * 128)), ident(128))
            cp("act" if h % 2 == 0 else "dve", WUKT.v(h), pb.v(slice(0, 512)))
        A.release(mC2)
        h64 = slice(0, 64)
        mQ = A.mark()
        q_catB = A.alloc((4, 2, 512), BF16)
        qcs = [q_cat, q_catB]

        def qlat_pieces(i, qc):
            tk = slice(i * 128, (i + 1) * 128)
            pieces = []
            for hg in range(2):
                for cc in range(4):
                    def piece(hg=hg, cc=cc):
                        saved = S.label
                        LB("C.qlat")
                        Pq = PS[7]
                        for hh in range(4):
                            h = hg * 4 + hh
                            mm(Pq.v(slice(hh * 128, (hh + 1) * 128)), WUKT.v(h, slice(cc * 128, (cc + 1) * 128)),
                               q_nopeT.v(h, tk), hh == 0, hh == 3)
                        cp("act" if cc % 2 == 0 else "dve", qc.v(cc, hg), Pq.v())
                        S.label = saved
                    pieces.append(piece)
            return pieces

        def c_scores(i, hg, kb):
            tk = slice(i * 128, (i + 1) * 128)
            qc = qcs[i % 2]
            ks = slice(kb * 128, (kb + 1) * 128)
            Ps = PS[5 + kb % 2]
            for cc in range(4):
                mm(Ps.v(), ckvT.v(cc, ks), qc.v(cc, hg), cc == 0, False)
            mm(Ps.v(), ckvT.v(4, ks, p=h64), q_ropeT.v(slice(hg * 4, hg * 4 + 4), tk, p=h64), False, True)
            pT_ = pT_ring.next()
            act(pT_.v(), Ps.v(), AF.Exp, scale=MLA_SCALE)
            if kb >= 2 * i:
                tt("dve", pT_.v(), pT_.v(), cbv("mask_lo4" if kb == 2 * i else "mask_hi4"), ALU.mult)
            return pT_

        def c_pv(i, kb, pT_):
            nkb = 2 * i + 2
            for cc in range(4):
                mm(PS[cc].v(), ckv_tok.v(kb, slice(cc * 128, (cc + 1) * 128)), pT_.v(), kb == 0, kb == nkb - 1)
            mm(PS[4].v(), cbv("ones"), pT_.v(), kb == 0, kb == nkb - 1)

        for pc in qlat_pieces(0, qcs[0]):
            pc()
        ql_q = []
        units = [(i, hg) for i in range(8) for hg in range(2)]
        pend = None
        for ui, (i, hg) in enumerate(units):
            LB("C.attn")
            tk = slice(i * 128, (i + 1) * 128)
            nkb = 2 * i + 2
            if hg == 0 and i + 1 < 8:
                ql_q.extend(qlat_pieces(i + 1, qcs[(i + 1) % 2]))
            cur = pend if pend is not None else c_scores(i, hg, 0)
            pend = None
            for kb in range(nkb):
                nxt = c_scores(i, hg, kb + 1) if kb + 1 < nkb else None
                c_pv(i, kb, cur)
                cur = nxt
                if ql_q and kb % 2 == 1:
                    ql_q.pop(0)()
            LB("C.fin")
            recip(rl.v(), PS[4].v())
            for cc in range(4):
                tt("dve", onb.v(cc), PS[cc].v(), rl.v(), ALU.mult)
            if hg == 1:
                while ql_q:
                    ql_q.pop(0)()
            if ui + 1 < len(units):
                LB("C.attn")
                ni, nhg = units[ui + 1]
                pend = c_scores(ni, nhg, 0)
                LB("C.fin")
            Pm = PS[7]
            for hh in range(4):
                h = hg * 4 + hh
                for cc in range(4):
                    mm(Pm.v(slice(hh * 128, (hh + 1) * 128)), WUV.v(cc, slice(h * 128, (h + 1) * 128)),
                       onb.v(cc, slice(hh * 128, (hh + 1) * 128)), hh == 0 and cc == 0, hh == 3 and cc == 3)
            cp("act", mixT.v(slice(hg * 4, hg * 4 + 4), tk),
               bview(Pm.v(), lambda ap: ap.rearrange("p (a b) -> p a b", b=128)))
        A.release(mQ)
        LB("C.smp")
        for cc in range(4):
            Pq = PS[7]
            for h in range(8):
                mm(Pq.v(slice(h * 64, (h + 1) * 64)), WUKT.v(h, slice(cc * 128, (cc + 1) * 128)),
                   q_nopeT.v(h, slice(1024, 1088)), h == 0, h == 7)
            cp("act" if cc % 2 == 0 else "dve", q_cat_s.v(cc), Pq.v())
        ctok_ring = Ring(A, 2, (8, 576), BF16)
        cT_ring = Ring(A, 2, (5, 1024), BF16)
        c128 = slice(0, 128)
        for s_ in range(4):
            qs = slice(16 * s_, 16 * s_ + 16)
            tq = slice(1024 + 16 * s_, 1024 + 16 * s_ + 16)
            rhs_cc = [q_cat_s.v(cc, slice(None), qs) for cc in range(4)]
            rhs_rope = q_ropeT.v(slice(0, 8), tq, p=h64)
            Po, Pl = PS[0], PS[1]
            idx = 0
            for ch in range(4):
                ctok = ctok_ring.next()
                cT = cT_ring.next()
                load("pool", ctok.v(slice(None), slice(0, 512)),
                     c_ckv[s_, ch * 1024:(ch + 1) * 1024, :].rearrange("(kb p) c -> p kb c", p=128))
                load("pool", ctok.v(slice(None), slice(512, 576)),
                     c_kr[s_, ch * 1024:(ch + 1) * 1024, :].rearrange("(kb p) c -> p kb c", p=128))
                for kb in range(8):
                    ks = slice(kb * 128, (kb + 1) * 128)
                    pb = PSB[2 + kb % 2]
                    for cc in range(4):
                        tr(pb.v(slice(cc * 128, (cc + 1) * 128)), ctok.v(kb, slice(cc * 128, (cc + 1) * 128)), ident(128))
                    cp("dve", cT.v(slice(0, 4), ks), bview(pb.v(slice(0, 512)), lambda ap: ap.rearrange("p (a b) -> p a b", b=128)))
                    tr(PSB[4].v(c128, p=h64), ctok.v(kb, slice(512, 576)), ident(128))
                    cp("act", cT.v(4, ks, p=h64), PSB[4].v(c128, p=h64))
                def s_scores(kb, cT=cT):
                    ks = slice(kb * 128, (kb + 1) * 128)
                    Ps = PS[5 + kb % 2]
                    for cc in range(4):
                        mm(Ps.v(c128), cT.v(cc, ks), rhs_cc[cc], cc == 0, False)
                    mm(Ps.v(c128), cT.v(4, ks, p=h64), rhs_rope, False, True)
                    pT_ = pT_ring.next()
                    act(pT_.v(c128), Ps.v(c128), AF.Exp, scale=MLA_SCALE)
                    return pT_

                def s_pv(kb, pT_, first, ctok=ctok):
                    for cc in range(4):
                        mm(Po.v(slice(cc * 128, (cc + 1) * 128)), ctok.v(kb, slice(cc * 128, (cc + 1) * 128)), pT_.v(c128),
                           first and cc == 0, False)
                    mm(Pl.v(c128), cbv("ones"), pT_.v(c128), first, False)

                pend = s_scores(0)
                for kb in range(8):
                    nxt = s_scores(kb + 1) if kb + 1 < 8 else None
                    s_pv(kb, pend, idx == 0)
                    pend = nxt
                    idx += 1
            Ps = PS[5]
            kn = slice(2048, 2112)
            for cc in range(4):
                mm(Ps.v(c128, p=h64), ckvT.v(cc, kn), rhs_cc[cc], cc == 0, False)
            mm(Ps.v(c128, p=h64), ckvT.v(4, kn, p=h64), rhs_rope, False, True)
            pT_ = pT_ring.next()
            act(pT_.v(c128, p=h64), Ps.v(c128, p=h64), AF.Exp, scale=MLA_SCALE)
            tt("dve", pT_.v(c128, p=h64), pT_.v(c128, p=h64), cbv(f"smask{s_}", 64), ALU.mult)
            for cc in range(4):
                mm(Po.v(slice(cc * 128, (cc + 1) * 128)), ckv_tok.v(16, slice(cc * 128, (cc + 1) * 128), p=h64), pT_.v(c128, p=h64),
                   False, cc == 3)
            mm(Pl.v(c128), cbv("ones", 64), pT_.v(c128, p=h64), False, True)
            recip(rl.v(c128), Pl.v(c128))
            tt("dve", onb_s.v(), bview(Po.v(), lambda ap: ap.rearrange("p (a b) -> p a b", b=128)),
               bview(rl.v(c128), lambda ap: ap.unsqueeze(1).broadcast_to([128, 4, 128])), ALU.mult)
            Pm = PS[7]
            for h in range(8):
                for cc in range(4):
                    mm(Pm.v(slice(h * 16, (h + 1) * 16)), WUV.v(cc, slice(h * 128, (h + 1) * 128)),
                       onb_s.v(cc, slice(h * 16, (h + 1) * 16)), h == 0 and cc == 0, h == 7 and cc == 3)
            cp("act", mixT.v(slice(0, 8), tq), bview(Pm.v(c128), lambda ap: ap.rearrange("p (a b) -> p a b", b=16)))
        dbg("mixA", mixT.v(slice(0, 8)))
        A.release(mC)
        if stop_after == "C":
            return finish()
        A.release(m_tail)

        store_q[0] = "sp"
        hB = A.alloc((5, D), F32)
        tmpB = A.alloc((5, D), F32)
        fT = A.alloc((16, 576), BF16)
        actT = A.alloc((11, 576), BF16)
        wring = Ring(A, 6, (4, 512), BF16)
        GB = A.alloc((D,), F32)
        pst = A.alloc((256,), F32)
        psb = A.alloc((256,), BF16)
        pTb = A.alloc((2, 576), BF16)
        sg_ring = Ring(A, 2, (512,), F32)
        sg2 = A.alloc((64,), F32)
        ssD = Ring(A, 8, (16,), F32)
        xnD = A.alloc((D,), BF16)
        w_out_k = w_out.rearrange("(k p) n -> p k n", p=128)
        w_gate_k = w_gate.rearrange("(k p) n -> p k n", p=128)
        w_up_k = w_up.rearrange("(k p) n -> p k n", p=128)
        w_down_k = w_down.rearrange("(k p) n -> p k n", p=128)
        w_pg_k = w_pg.rearrange("(k p) n -> p k n", p=128)
        w_ple_k = w_ple.rearrange("(k p) n -> p k n", p=128)

        def wtile_kn(src_k, k0, nk, c0, ncols):
            wt = wring.next()
            v = bview(wt.v(), lambda ap: ap.rearrange("p a b -> p (a b)")[:, 0:nk * ncols].rearrange("p (a b) -> p a b", b=ncols))
            dma("pool", v.ap, src_k[:, k0:k0 + nk, c0:c0 + ncols], w=[v])
            return wt, (lambda kk: View(v.ap[:, kk, :], wt.v().rg))

        groups = [[(0, 128), (128, 128), (256, 128), (384, 128), (1024, 64)],
                  [(512, 128), (640, 128), (768, 128), (896, 128)]]
        for grp in groups:
            nb = len(grp)
            offs = []
            o = 0
            for (_, T) in grp:
                offs.append(o)
                o += T
            ntok = o
            Nmain = min(512, ntok)
            rem = ntok - Nmain

            def tok_linear(src_k, nkc, lhs_fn, consume, ksplit=4):
                for n in range(4):
                    k = 0
                    while k < nkc:
                        nk = min(ksplit, nkc - k)
                        _, wv = wtile_kn(src_k, k, nk, n * 512, 512)
                        for kk in range(nk):
                            for bi, (tok0, T) in enumerate(grp):
                                mm(PS[bi].v(p=slice(0, T)), lhs_fn(k + kk, bi), wv(kk), k + kk == 0, k + kk == nkc - 1)
                        k += nk
                    for bi, (tok0, T) in enumerate(grp):
                        consume(n, bi, T)

            def post_norm_residual(first_x):
                for bi, (tok0, T) in enumerate(grp):
                    pT = slice(0, T)
                    ss = ssD.next()
                    act(junk.v(p=pT), tmpB.v(bi, p=pT), AF.Square, accum=ss.v(slice(0, 1), p=pT))
                    rstd_act(ss.v(slice(1, 2), p=pT), ss.v(slice(0, 1), p=pT), 1.0 / D)
                    if first_x:
                        load("sp", hB.v(bi, p=pT), x_own[tok0:tok0 + T, :])
                    stt(tmpB.v(bi, p=pT), tmpB.v(bi, p=pT), ss.v(slice(1, 2), p=pT), GB.v(p=pT), ALU.mult, ALU.mult)
                    tt("dve", hB.v(bi, p=pT), hB.v(bi, p=pT), tmpB.v(bi, p=pT), ALU.add)

            LB("D1.wout")
            load("sp", GB.v(), g_pm_b)
            tok_linear(w_out_k, 16, lambda k, bi: mixT.v(k, slice(grp[bi][0], grp[bi][0] + grp[bi][1])),
                       lambda n, bi, T: cp("act", tmpB.v(bi, slice(n * 512, (n + 1) * 512), p=slice(0, T)), PS[bi].v(p=slice(0, T))))
            LB("D2.norm")
            post_norm_residual(True)
            LB("D3.fT")
            for bi, (tok0, T) in enumerate(grp):
                ss = ssD.next()
                hv = Buf(hB.t[:, bi, :], hB.base + bi * D * 4, (D,), 4)
                norm_transpose(hv, T, 16, lambda c0, c1, bi=bi, T=T: fT.v(slice(c0, c1), slice(offs[bi], offs[bi] + T)),
                               PSB[6], PSB[7], ss, xnD)
            for qd in range(4):
                LB("D4.gateup")
                for ml in range(11):
                    m = qd * 11 + ml
                    wg = wring.next()
                    wgv = bview(wg.v(), lambda ap: ap.rearrange("p a b -> p (a b)").rearrange("p (a b) -> p a b", b=128))
                    dma("pool", wgv.ap, w_gate_k[:, :, m * 128:(m + 1) * 128], w=[wgv])
                    wu = wring.next()
                    wuv = bview(wu.v(), lambda ap: ap.rearrange("p a b -> p (a b)").rearrange("p (a b) -> p a b", b=128))
                    dma("pool", wuv.ap, w_up_k[:, :, m * 128:(m + 1) * 128], w=[wuv])
                    gk = lambda k: View(wgv.ap[:, k, :], wg.v().rg)
                    uk = lambda k: View(wuv.ap[:, k, :], wu.v().rg)
                    b0 = (m % 2) * 3
                    Pg_, Pu_, Pr_ = PS[b0], PS[b0 + 1], PS[b0 + 2]
                    nm = slice(0, Nmain)
                    for k in range(16):
                        mm(Pg_.v(nm), gk(k), fT.v(k, nm), k == 0, k == 15)
                    for k in range(16):
                        mm(Pu_.v(nm), uk(k), fT.v(k, nm), k == 0, k == 15)
                    if rem:
                        rs_ = slice(Nmain, ntok)
                        for k in range(16):
                            mm(Pr_.v(slice(0, rem)), gk(k), fT.v(k, rs_), k == 0, k == 15)
                        for k in range(16):
                            mm(Pr_.v(slice(64, 64 + rem)), uk(k), fT.v(k, rs_), k == 0, k == 15)
                    sg = sg_ring.next()
                    act(sg.v(nm), Pg_.v(nm), AF.Silu)
                    tt("dve", actT.v(ml, nm), sg.v(nm), Pu_.v(nm), ALU.mult)
                    if rem:
                        act(sg2.v(slice(0, rem)), Pr_.v(slice(0, rem)), AF.Silu)
                        tt("dve", actT.v(ml, rs_), sg2.v(slice(0, rem)), Pr_.v(slice(64, 64 + rem)), ALU.mult)

                def down_consume(n, bi, T, qd=qd):
                    dst = tmpB.v(bi, slice(n * 512, (n + 1) * 512), p=slice(0, T))
                    if qd == 0:
                        cp("act", dst, PS[bi].v(p=slice(0, T)))
                    else:
                        tt("dve", dst, dst, PS[bi].v(p=slice(0, T)), ALU.add)
                LB("D4.down")
                tok_linear(w_down_k[:, qd * 11:(qd + 1) * 11, :], 11,
                           lambda k, bi: actT.v(k, slice(offs[bi], offs[bi] + grp[bi][1])), down_consume)
            LB("D5.norm")
            load("sp", GB.v(), g_pf_b)
            post_norm_residual(False)
            LB("D6.ple_prep")
            for bi, (tok0, T) in enumerate(grp):
                pT = slice(0, T)
                cp("act", xnD.v(p=pT), hB.v(bi, p=pT))
                for c4 in range(4):
                    pb = PSB[6 + c4 % 2]
                    for jj in range(4):
                        c = c4 * 4 + jj
                        tr(pb.v(slice(jj * 128, jj * 128 + T)), xnD.v(slice(c * 128, (c + 1) * 128), p=pT), ident(T))
                    src = bview(pb.v(slice(0, 512)), lambda ap: ap.rearrange("p (a b) -> p a b", b=128)[:, :, 0:T])
                    cp("dve" if c4 % 2 == 0 else "act", fT.v(slice(c4 * 4, c4 * 4 + 4), slice(offs[bi], offs[bi] + T)), src)
                load("sp", pst.v(p=pT), p_own[tok0:tok0 + T, :])
                cp("dve", psb.v(p=pT), pst.v(p=pT))
                pb = PSB[6]
                for c in range(2):
                    tr(pb.v(slice(c * 128, c * 128 + T)), psb.v(slice(c * 128, (c + 1) * 128), p=pT), ident(T))
                src = bview(pb.v(slice(0, 256)), lambda ap: ap.rearrange("p (a b) -> p a b", b=128)[:, :, 0:T])
                cp("dve", pTb.v(slice(0, 2), slice(offs[bi], offs[bi] + T)), src)
            wp_holder = [None]

            def ple_consume(n, bi, T):
                pT = slice(0, T)
                if bi == 0:
                    wp_holder[0] = wtile_kn(w_ple_k, 0, 2, n * 512, 512)[1]
                Pe = PS[5 + bi % 3]
                for kc in range(2):
                    mm(Pe.v(p=pT), pTb.v(kc, slice(offs[bi], offs[bi] + T)), wp_holder[0](kc), kc == 0, kc == 1)
                sg = sg_ring.next()
                act(sg.v(p=pT), PS[bi].v(p=pT), AF.Sigmoid)
                tt("dve", sg.v(p=pT), sg.v(p=pT), Pe.v(p=pT), ALU.mult)
                ns = slice(n * 512, (n + 1) * 512)
                tt("dve", tmpB.v(bi, ns, p=pT), sg.v(p=pT), hB.v(bi, ns, p=pT), ALU.add)
            LB("D6.ple")
            tok_linear(w_pg_k, 16, lambda k, bi: fT.v(k, slice(offs[bi], offs[bi] + grp[bi][1])), ple_consume)
            for bi, (tok0, T) in enumerate(grp):
                store(y_out[tok0:tok0 + T, :], tmpB.v(bi, p=slice(0, T)))

        return finish()


_CACHE = {}


def _prep_inputs(inp):
    f32 = lambda a: np.ascontiguousarray(np.asarray(a, dtype=np.float32))
    xp = f32(inp["x_prompt"])
    xsm = f32(inp["x_sample"])
    pp = f32(inp["p_prompt"])[0]
    psm = f32(inp["p_sample"])[0]
    cck = f32(inp["cache_ckv"])[0]
    ckr = f32(inp["cache_krope"])[0]
    stg = f32(inp["state_gla"])[0]
    g = lambda k: f32(inp[k])[0]
    gvec = np.concatenate([g("g_pre_mix").reshape(16, 128).T, g("g_pre_ffn").reshape(16, 128).T,
                           g("g_q").reshape(4, 128).T, g("g_gla").reshape(2, 128).T], axis=1)
    shared = {
        "gvec": f32(gvec),
        "g_kv_b": f32(np.broadcast_to(g("g_kv")[None, :], (128, 512))),
        "g_pm_b": f32(np.broadcast_to(g("g_post_mix")[None, :], (128, D))),
        "g_pf_b": f32(np.broadcast_to(g("g_post_ffn")[None, :], (128, D))),
        "w_in": g("w_in"), "w_uq": g("w_uq"),
        "w_uk": f32(g("w_uk").reshape(512, 1024)), "w_uv": f32(g("w_uv").reshape(512, 1024)),
        "w_ga": f32(np.concatenate([g("w_ga"), g("b_ga")[None, :]], axis=0)),
        "w_out": g("w_out"), "w_gate": g("w_gate"), "w_up": g("w_up"), "w_down": g("w_down"),
        "w_ple": g("w_ple"), "w_pg": g("w_ple_gate"),
    }
    maps = []
    for c in range(8):
        b, p = c // 2, c % 2
        own = [2 * i + p for i in range(8)]
        xs_c = xsm[4 * c:4 * c + 4].reshape(64, D)
        m = dict(shared)
        m["x_nat"] = f32(np.concatenate([xp[b], xs_c], axis=0))
        m["x_own"] = f32(np.concatenate([xp[b].reshape(16, 128, D)[own].reshape(1024, D), xs_c], axis=0))
        m["p_own"] = f32(np.concatenate([pp[b].reshape(16, 128, 256)[own].reshape(1024, 256),
                                         psm[4 * c:4 * c + 4].reshape(64, 256)], axis=0))
        m["c_ckv"] = f32(cck[4 * c:4 * c + 4])
        m["c_kr"] = f32(ckr[4 * c:4 * c + 4])
        m["st_in"] = f32(stg[4 * c:4 * c + 4])
        cf, cb = build_consts(p)
        m["constf"] = cf.pack()
        m["constb"] = cb.pack()
        maps.append(m)
    return maps


def _get_program(debug=None, stop_after=None):
    key = (None if debug is None else tuple(sorted(debug)), stop_after)
    if key not in _CACHE:
        cf, cb = build_consts(0)
        cf.pack()
        cb.pack()
        _CACHE[key] = build_program(cf.off, cf.n, cb.off, cb.n, debug=debug, stop_after=stop_after)
    return _CACHE[key]


def kernel(**inp):
    maps = _prep_inputs(inp)
    nc, _ = _get_program()
    res = run_bass_kernel_spmd(nc, maps, core_ids=list(range(8)))
    R = res.results
    y_p = np.zeros((4, 2048, D), np.float32)
    y_s = np.zeros((32, 16, D), np.float32)
    ckv_p = np.zeros((1, 4, 2048, 512), np.float32)
    kr_p = np.zeros((1, 4, 2048, 64), np.float32)
    gl_p = np.zeros((1, 4, 4, 128, 256), np.float32)
    ckv_s = np.zeros((1, 32, 16, 512), np.float32)
    kr_s = np.zeros((1, 32, 16, 64), np.float32)
    gl_s = np.zeros((1, 32, 4, 128, 256), np.float32)
    for c in range(8):
        b, p = c // 2, c % 2
        r = R[c]
        yo = np.asarray(r["y_out"])
        for i in range(8):
            j = 2 * i + p
            y_p[b, j * 128:(j + 1) * 128] = yo[i * 128:(i + 1) * 128]
        y_s[4 * c:4 * c + 4] = yo[1024:1088].reshape(4, 16, D)
        ck = np.asarray(r["ckv_out"])
        kr = np.asarray(r["kr_out"])
        if p == 0:
            ckv_p[0, b] = ck[:2048]
            kr_p[0, b] = kr[:2048]
            gl_p[0, b] = np.asarray(r["gp_out"])
        ckv_s[0, 4 * c:4 * c + 4] = ck[2048:2112].reshape(4, 16, 512)
        kr_s[0, 4 * c:4 * c + 4] = kr[2048:2112].reshape(4, 16, 64)
        gl_s[0, 4 * c:4 * c + 4] = np.asarray(r["gs_out"])
    return (y_p, y_s, ckv_p, kr_p, gl_p, ckv_s, kr_s, gl_s)
```

```python
import numpy as np
import concourse.bass as bass
import concourse.mybir as mybir
from concourse.bass_utils import run_bass_kernel_spmd
from contextlib import ExitStack

F32 = mybir.dt.float32
BF16 = mybir.dt.bfloat16
AF = mybir.ActivationFunctionType
ALU = mybir.AluOpType

D = 2048
NOWN = 1088
NNAT = 2112
DFF = 5632
EPS = 1e-6
MLA_SCALE = 192 ** -0.5
PSUM_BASE = 1 << 20
PAGE = 64
SAME_ENGINE_SYNC = True


class Op:
    __slots__ = ("eng", "fn", "r", "w", "dma", "deps", "inc", "tok", "sem", "semval", "label")

    def __init__(self, eng, fn, r, w, dma, label=None):
        self.eng, self.fn, self.r, self.w, self.dma, self.label = eng, fn, r, w, dma, label
        self.deps = []
        self.inc = False
        self.tok = None
        self.sem = None
        self.semval = None


def _pages(views):
    out = []
    for v in views:
        lo, hi = v.rg
        if lo >= PSUM_BASE:
            out.extend(range(PSUM_BASE // PAGE + (lo - PSUM_BASE) // 2048,
                             PSUM_BASE // PAGE + (hi - PSUM_BASE + 2047) // 2048))
        else:
            out.extend(range(lo // PAGE, (hi + PAGE - 1) // PAGE))
    return out


class Sched:
    def __init__(self, nc, st):
        self.nc = nc
        self.ops = []
        self.label = None
        self.annotate = False
        self.engs = {"pe": nc.tensor, "act": nc.scalar, "dve": nc.vector, "pool": nc.gpsimd, "sp": nc.sync}
        self.esem = {e: st.enter_context(nc.semaphore("sem_" + e)) for e in ("pe", "act", "dve", "pool")}
        nd = {"sp": 16, "pool": 12}
        self.dsems = {q: [st.enter_context(nc.semaphore(f"dsem_{q}{i}")) for i in range(n)] for q, n in nd.items()}

    def add(self, eng, fn, r=(), w=(), dma=False):
        import os
        mx = int(os.environ.get("MAXOPS", "0"))
        if mx and len(self.ops) >= mx:
            return
        self.ops.append(Op(eng, fn, _pages(r), _pages(w), dma, self.label))

    def finalize(self):
        ops = self.ops
        last_w = {}
        readers = {}
        rr = {q: 0 for q in self.dsems}
        sem_last = {}
        sem_cnt = {}
        for i, op in enumerate(ops):
            deps = {}
            for k in op.r:
                j = last_w.get(k)
                if j is not None:
                    deps[j] = "raw"
                if k >= PSUM_BASE // PAGE:
                    rs = readers.get(k)
                    if rs:
                        for j in rs:
                            if ops[j].eng != op.eng and j not in deps:
                                deps[j] = "rar"
            for k in op.w:
                j = last_w.get(k)
                if j is not None and deps.get(j) != "raw":
                    deps[j] = "waw"
                rs = readers.get(k)
                if rs:
                    for j in rs:
                        if j not in deps:
                            deps[j] = "war"
            if op.dma:
                q = op.eng
                idx = rr[q]
                rr[q] = (idx + 1) % len(self.dsems[q])
                key = (q, idx)
                if key in sem_last:
                    deps[sem_last[key]] = "raw"
                sem_last[key] = i
                sem_cnt[key] = sem_cnt.get(key, 0) + 1
                op.sem = self.dsems[q][idx]
                op.semval = 16 * sem_cnt[key]
            best = {}
            for j, kind in deps.items():
                if j == i:
                    continue
                pj = ops[j]
                if pj.dma:
                    op.deps.append(j)
                    continue
                if pj.eng == op.eng:
                    if op.eng == "pe":
                        continue
                    if not SAME_ENGINE_SYNC:
                        continue
                if pj.eng not in best or best[pj.eng] < j:
                    best[pj.eng] = j
            for e, j in best.items():
                op.deps.append(j)
                ops[j].inc = True
            for k in op.r:
                rs = readers.get(k)
                if rs is None:
                    readers[k] = [i]
                else:
                    eng = op.eng
                    rs[:] = [j for j in rs if ops[j].eng != eng or ops[j].dma]
                    rs.append(i)
            for k in op.w:
                last_w[k] = i
                readers[k] = []
        run = {e: 0 for e in self.esem}
        for op in ops:
            if op.dma:
                op.tok = (op.sem, op.semval)
            elif op.inc:
                run[op.eng] += 1
                op.tok = (self.esem[op.eng], run[op.eng])
        self.final_counts = run
        self.dma_final = {}
        for op in ops:
            if op.dma:
                self.dma_final[op.sem.name] = (op.sem, op.semval)

    def emit(self):
        nc = self.nc
        ops = self.ops
        per = {e: [] for e in self.engs}
        for op in ops:
            per[op.eng].append(op)
        esem = self.esem

        def run_engine(ename, eng):
            seen = {}
            for op in per[ename]:
                for j in op.deps:
                    sem, val = ops[j].tok
                    if seen.get(sem.name, 0) >= val:
                        continue
                    seen[sem.name] = val
                    eng.wait_ge(sem, val)
                inst = op.fn(eng)
                if self.annotate and op.label:
                    inst.annotate(op.label)
                if op.dma:
                    inst.then_inc(op.sem, 16)
                elif op.inc:
                    inst.then_inc(esem[ename], 1)
            if ename == "sp":
                for nm, (sem, val) in self.dma_final.items():
                    if seen.get(nm, 0) < val:
                        eng.wait_ge(sem, val)
                for e, sem in esem.items():
                    if self.final_counts[e] > 0:
                        eng.wait_ge(sem, self.final_counts[e])

        with nc.Block() as block:
            @block.sync
            def _(e):
                run_engine("sp", e)

            @block.tensor
            def _(e):
                run_engine("pe", e)

            @block.scalar
            def _(e):
                run_engine("act", e)

            @block.vector
            def _(e):
                run_engine("dve", e)

            @block.gpsimd
            def _(e):
                run_engine("pool", e)


class View:
    __slots__ = ("ap", "rg")

    def __init__(self, ap, rg):
        self.ap = ap
        self.rg = rg


class Buf:
    def __init__(self, ap, base, shape, esize):
        self.t = ap
        self.base = base
        self.shape = tuple(shape)
        self.es = esize
        st = []
        s = 1
        for n in reversed(self.shape):
            st.append(s)
            s *= n
        self.strides = tuple(reversed(st))
        self.nbytes = s * esize

    def v(self, *idx, p=None):
        key = (slice(None) if p is None else p,) + idx
        ap = self.t[key]
        lo = hi = 0
        for d, (n, stv) in enumerate(zip(self.shape, self.strides)):
            if d < len(idx):
                i = idx[d]
                if isinstance(i, slice):
                    a = 0 if i.start is None else i.start
                    b = n if i.stop is None else i.stop
                    lo += a * stv
                    hi += (b - 1) * stv
                else:
                    lo += i * stv
                    hi += i * stv
            else:
                hi += (n - 1) * stv
        return View(ap, (self.base + lo * self.es, self.base + (hi + 1) * self.es))


class Arena:
    def __init__(self, nc, st, nbytes):
        self.words = nbytes // 4
        self.t = st.enter_context(nc.sbuf_tensor("arena", [128, self.words], F32))
        self.top = 0
        self.peak = 0

    def alloc(self, shape, dt):
        es = 2 if dt == BF16 else 4
        n = 1
        for s in shape:
            n *= s
        nb = (n * es + PAGE - 1) // PAGE * PAGE
        off = self.top
        self.top += nb
        self.peak = max(self.peak, self.top)
        assert self.top <= self.words * 4, f"SBUF arena overflow {self.top}"
        ap = self.t[:, off // 4:(off + nb) // 4]
        if dt == BF16:
            ap = ap.bitcast(BF16)
        ap = ap[:, 0:n]
        if len(shape) == 2:
            ap = ap.rearrange("p (a b) -> p a b", b=shape[1])
        elif len(shape) == 3:
            ap = ap.rearrange("p (a b c) -> p a b c", b=shape[1], c=shape[2])
        elif len(shape) == 4:
            ap = ap.rearrange("p (a b c d) -> p a b c d", b=shape[1], c=shape[2], d=shape[3])
        return Buf(ap, off, shape, es)

    def at(self, off, shape, dt):
        save = self.top
        self.top = off
        b = self.alloc(shape, dt)
        self.top = save
        return b

    def mark(self):
        return self.top

    def release(self, m):
        self.top = m


class Ring:
    def __init__(self, arena, n, shape, dt):
        self.bufs = [arena.alloc(shape, dt) for _ in range(n)]
        self.i = 0

    def next(self):
        b = self.bufs[self.i]
        self.i = (self.i + 1) % len(self.bufs)
        return b


def _gla_consts(T, L):
    seg = np.arange(T) // L
    s = np.arange(T)[:, None]
    t = np.arange(T)[None, :]
    same = seg[:, None] == seg[None, :]
    tri_incl = np.where(same & (s <= t), -1.0 / 16, 0.0)
    tri_rev = np.where(same & (s > t), -1.0 / 16, 0.0)
    mask = np.where(same & (s <= t), 1.0, 0.0)
    nseg = T // L
    ind = np.zeros((T, nseg))
    ind[np.arange(T), seg] = -1.0 / 16
    rowm = np.zeros((T, nseg))
    rowm[np.arange(T), seg] = 1.0
    segs = [np.diag((seg == g).astype(np.float64)) for g in range(nseg)]
    return tri_incl, tri_rev, mask, ind, rowm, segs


class ConstPack:
    def __init__(self):
        self.cols = []
        self.off = {}
        self.n = 0

    def add(self, name, arr):
        arr = np.asarray(arr, np.float32)
        a = np.zeros((128, arr.shape[1]), np.float32)
        a[:arr.shape[0]] = arr
        self.off[name] = (self.n, arr.shape[1])
        self.cols.append(a)
        self.n += arr.shape[1]

    def pack(self):
        return np.ascontiguousarray(np.concatenate(self.cols, axis=1))


def rope_tables(pos):
    half = 32
    inv = 1.0 / (10000.0 ** (np.arange(half, dtype=np.float64) / half))
    ang = pos.astype(np.float64)[:, None] * inv[None, :]
    return np.cos(ang).astype(np.float32), np.sin(ang).astype(np.float32)


def build_consts(p):
    cf = ConstPack()
    cb = ConstPack()
    for tag, T, L in (("p", 128, 64), ("s", 64, 16)):
        tri_incl, tri_rev, mask, ind, rowm, segs = _gla_consts(T, L)
        cf.add("tri_incl_" + tag, tri_incl)
        cf.add("tri_rev_" + tag, tri_rev)
        cf.add("mask_" + tag, mask)
        cf.add("ind_" + tag, ind)
        cf.add("rowm_" + tag, rowm)
        for g, sg in enumerate(segs):
            cb.add(f"seg_{tag}{g}", sg)
    cf.add("ones_f", np.ones((128, 128)))
    cf.add("selw", np.tile(np.array([[1.0 - p, float(p)]]), (128, 1)))
    cb.add("ident", np.eye(128))
    cb.add("ones", np.ones((128, 128)))
    cb.add("sel0", np.eye(128) * (1.0 if p == 0 else 0.0))
    cb.add("sel1", np.eye(128) * (1.0 if p == 1 else 0.0))
    k = np.arange(128)[:, None]
    q = np.arange(128)[None, :]
    diag = ((k // 64) <= (q // 64)).astype(np.float32)
    lo = diag if p == 0 else np.ones((128, 128), np.float32)
    hi = np.zeros((128, 128), np.float32) if p == 0 else diag
    cb.add("mask_lo4", np.tile(lo, (1, 4)))
    cb.add("mask_hi4", np.tile(hi, (1, 4)))
    for s in range(4):
        m = np.zeros((64, 128), np.float32)
        m[16 * s:16 * s + 16, :] = 1.0
        cb.add(f"smask{s}", m)
    posn = np.concatenate([np.arange(2048), 4096 + np.tile(np.arange(16), 4)])
    posn = np.concatenate([posn, np.zeros(17 * 128 - posn.size, np.int64)])
    c, s_ = rope_tables(posn)
    tk = np.concatenate([c, c, s_, s_], axis=1).reshape(17, 128, 128).transpose(1, 0, 2).reshape(128, 17 * 128)
    cf.add("ropek", tk)
    own_pos = np.concatenate([np.concatenate([np.arange(128) + (2 * i + p) * 128 for i in range(8)]),
                              4096 + np.tile(np.arange(16), 4)])
    c, s_ = rope_tables(own_pos)
    cf.add("cosq", np.concatenate([c, c], axis=1).T)
    cf.add("sinq", np.concatenate([s_, s_], axis=1).T)
    return cf, cb


def build_program(cf_off, cf_n, cb_off, cb_n, debug=None, stop_after=None):
    nc = bass.Bass("TRN2", target_bir_lowering=False)

    skip_in = set()
    if stop_after in ("A1", "A2"):
        skip_in = {"c_ckv", "c_kr", "w_out", "w_gate", "w_up", "w_down", "w_pg", "w_ple", "w_uq", "w_uk", "w_uv", "p_own"}
    if stop_after in ("B", "C"):
        skip_in = {"w_out", "w_gate", "w_up", "w_down", "w_pg", "w_ple", "p_own"}
    declared = []

    def din(name, shape):
        if name in skip_in:
            return None
        declared.append(name)
        return nc.dram_tensor(name, list(shape), F32, kind="ExternalInput").ap()

    def dout(name, shape):
        return nc.dram_tensor(name, list(shape), F32, kind="ExternalOutput").ap()

    x_nat = din("x_nat", [NNAT, D])
    x_own = din("x_own", [NOWN, D])
    p_own = din("p_own", [NOWN, 256])
    c_ckv = din("c_ckv", [4, 4096, 512])
    c_kr = din("c_kr", [4, 4096, 64])
    st_in = din("st_in", [4, 4, 128, 256])
    constf = din("constf", [128, cf_n])
    constb = din("constb", [128, cb_n])
    gvec = din("gvec", [128, 38])
    g_kv_b = din("g_kv_b", [128, 512])
    g_pm_b = din("g_pm_b", [128, D])
    g_pf_b = din("g_pf_b", [128, D])
    w_in = din("w_in", [D, 4176])
    w_uq = din("w_uq", [512, 1536])
    w_uk = din("w_uk", [512, 1024])
    w_uv = din("w_uv", [512, 1024])
    w_ga = din("w_ga", [17, 512])
    w_out = din("w_out", [D, D])
    w_gate = din("w_gate", [D, DFF])
    w_up = din("w_up", [D, DFF])
    w_down = din("w_down", [DFF, D])
    w_ple = din("w_ple", [256, D])
    w_pg = din("w_pg", [D, D])

    y_out = dout("y_out", [NOWN, D])
    ckv_out = dout("ckv_out", [NNAT, 512])
    kr_out = dout("kr_out", [NNAT, 64])
    gp_out = dout("gp_out", [4, 128, 256])
    gs_out = dout("gs_out", [4, 4, 128, 256])
    dbg_outs = {}
    if debug:
        for nm, (shp, dtn) in debug.items():
            dbg_outs[nm] = nc.dram_tensor("dbg_" + nm, list(shp), BF16 if dtn == "bf16" else F32, kind="ExternalOutput").ap()

    with ExitStack() as st:
        S = Sched(nc, st)
        A = Arena(nc, st, 204 * 1024)
        banks_t = [st.enter_context(nc.psum_tensor(f"ps{i}", [128, 512], F32)) for i in range(8)]
        PS = [Buf(banks_t[i][:, :], PSUM_BASE + i * 2048, (512,), 4) for i in range(8)]
        PSB = [Buf(banks_t[i][:, :].bitcast(BF16), PSUM_BASE + i * 2048, (1024,), 2) for i in range(8)]

        def dma(q, out_ap, in_ap, r=(), w=()):
            S.add(q, lambda e: e.dma_start(out=out_ap, in_=in_ap), r=r, w=w, dma=True)

        def load(q, dst, src_ap):
            dma(q, dst.ap, src_ap, w=[dst])

        store_q = ["pool"]

        def store(dst_ap, src):
            dma(store_q[0], dst_ap, src.ap, r=[src])

        def mm(out, lhsT, rhs, start, stop):
            S.add("pe", lambda e: e.matmul(out.ap, lhsT=lhsT.ap, rhs=rhs.ap, start=start, stop=stop,
                                           skip_group_check=True),
                  r=[lhsT, rhs], w=[out])

        def tr(out, in_, ident):
            S.add("pe", lambda e: e.transpose(out=out.ap, in_=in_.ap, identity=ident.ap), r=[in_, ident], w=[out])

        def act(out, in_, func, scale=None, bias=None, accum=None, eng="act"):
            kw = {}
            if scale is not None:
                kw["scale"] = scale.ap if isinstance(scale, View) else scale
            if bias is not None:
                kw["bias"] = bias.ap if isinstance(bias, View) else bias
            r = [in_] + [x for x in (scale, bias) if isinstance(x, View)]
            w = [out]
            if accum is not None:
                kw["accum_out"] = accum.ap
                w.append(accum)
            S.add("act", lambda e: e.activation(out=out.ap, in_=in_.ap, func=func, **kw), r=r, w=w)

        def tt(eng, out, in0, in1, op):
            S.add(eng, lambda e: e.tensor_tensor(out=out.ap, in0=in0.ap, in1=in1.ap, op=op), r=[in0, in1], w=[out])

        def ts(eng, out, in0, s1, op0, s2=None, op1=None):
            r = [in0] + [x for x in (s1, s2) if isinstance(x, View)]
            a1 = s1.ap if isinstance(s1, View) else s1
            a2 = s2.ap if isinstance(s2, View) else s2
            if op1 is None:
                S.add(eng, lambda e: e.tensor_scalar(out=out.ap, in0=in0.ap, scalar1=a1, scalar2=None, op0=op0), r=r, w=[out])
            else:
                S.add(eng, lambda e: e.tensor_scalar(out=out.ap, in0=in0.ap, scalar1=a1, scalar2=a2, op0=op0, op1=op1), r=r, w=[out])

        def stt(out, in0, scalar, in1, op0, op1):
            r = [in0, in1] + ([scalar] if isinstance(scalar, View) else [])
            sc = scalar.ap if isinstance(scalar, View) else scalar
            S.add("dve", lambda e: e.scalar_tensor_tensor(out=out.ap, in0=in0.ap, scalar=sc, in1=in1.ap, op0=op0, op1=op1),
                  r=r, w=[out])

        def cp(eng, out, in_):
            if eng == "act":
                S.add("act", lambda e: e.copy(out=out.ap, in_=in_.ap), r=[in_], w=[out])
            else:
                S.add(eng, lambda e: e.tensor_copy(out=out.ap, in_=in_.ap), r=[in_], w=[out])

        def recip(out, in_):
            S.add("dve", lambda e: e.reciprocal(out=out.ap, in_=in_.ap), r=[in_], w=[out])

        def rstd_act(out, in_, inv_n):
            act(out, in_, AF.Ln, scale=inv_n, bias=EPS)
            act(out, out, AF.Exp, scale=-0.5)

        def memset(eng, out, val):
            S.add(eng, lambda e: e.memset(out.ap, val), w=[out])

        def bview(view, shape_fn):
            return View(shape_fn(view.ap), view.rg)

        def dbg(name, view):
            if debug and name in dbg_outs:
                dma("sp", dbg_outs[name], view.ap, r=[view])

        n_core = cf_off["ropek"][0]
        CF = A.alloc((n_core,), F32)
        CB = A.alloc((cb_n,), BF16)
        GV = A.alloc((38,), F32)
        junk = A.alloc((2048,), BF16)
        mixT = A.alloc((16, NOWN), BF16)
        m0 = A.mark()
        m_tail = m0
        CBf = A.alloc((cb_n,), F32)
        load("sp", CF.v(), constf[:, 0:n_core])
        load("sp", CBf.v(), constb)
        load("sp", GV.v(), gvec)
        cp("dve", CB.v(), CBf.v())
        A.release(m0)

        def cfv(name, rows=128, c0=0, c1=None):
            o, n = cf_off[name]
            c1 = n if c1 is None else c1
            return CF.v(slice(o + c0, o + c1), p=slice(0, rows))

        def cbv(name, rows=128, c0=0, c1=None):
            o, n = cb_off[name]
            c1 = n if c1 is None else c1
            return CB.v(slice(o + c0, o + c1), p=slice(0, rows))

        ident = lambda T: cbv("ident", T, 0, T)
        w_in_k = w_in.rearrange("(k p) n -> p k n", p=128)
        nat_blocks = [(j, j * 128, 128) for j in range(16)] + [(16, 2048, 64)]

        import os as _os
        S.annotate = bool(_os.environ.get("ANNOTATE"))

        def LB(name):
            S.label = name

        def finish():
            S.finalize()
            S.emit()
            nc._declared = declared
            return nc, A.peak

        def norm_transpose(xs, T, gcols, dst_fn, psb_a, psb_b, ss, xn, nfeat=D, defer=None, sq_junk=None):
            pT = slice(0, T)
            nch = nfeat // 128
            jk = sq_junk if sq_junk is not None else junk
            do = (lambda f: f()) if defer is None else defer.append
            do(lambda: act(jk.v(slice(0, nfeat), p=pT), xs.v(slice(0, nfeat), p=pT), AF.Square, accum=ss.v(slice(0, 1), p=pT)))
            do(lambda: rstd_act(ss.v(slice(1, 2), p=pT), ss.v(slice(0, 1), p=pT), 1.0 / nfeat))
            do(lambda: ts("dve", xn.v(slice(0, nfeat), p=pT), xs.v(slice(0, nfeat), p=pT), ss.v(slice(1, 2), p=pT), ALU.mult))
            for c4 in range(nch // 4):
                pb = (psb_a, psb_b)[c4 % 2]

                def grp(pb=pb, c4=c4):
                    for jj in range(4):
                        c = c4 * 4 + jj
                        tr(pb.v(slice(jj * 128, jj * 128 + T)), xn.v(slice(c * 128, (c + 1) * 128), p=pT), ident(T))
                    src = bview(pb.v(slice(0, 512)), lambda ap: ap.rearrange("p (a b) -> p a b", b=128)[:, :, 0:T])
                    g = bview(GV.v(slice(gcols + c4 * 4, gcols + c4 * 4 + 4)),
                              lambda ap: ap.unsqueeze(2).broadcast_to([128, 4, T]))
                    tt("dve", dst_fn(c4 * 4, c4 * 4 + 4), src, g, ALU.mult)
                do(grp)

        mA2 = A.mark()
        Wn2 = A.alloc((16, 2064), BF16)
        for kg in range(4):
            load("pool", Wn2.v(slice(kg * 4, kg * 4 + 4)), w_in_k[:, kg * 4:(kg + 1) * 4, 1088:3152])
        WGA = A.alloc((512,), BF16)
        load("pool", WGA.v(p=slice(0, 17)), w_ga)
        glrT = A.alloc((128,), BF16)
        memset("dve", glrT.v(p=slice(0, 32)), 1.0)
        xs_ring = Ring(A, 3, (2048,), F32)
        xn1 = A.alloc((2048,), BF16)
        junk2 = A.alloc((2048,), BF16)
        aT_ring = Ring(A, 3, (16, 128), BF16)
        ss_ring = Ring(A, 6, (16,), F32)
        la = A.alloc((512,), F32)
        E_ring = Ring(A, 2, (512,), F32)
        qt = A.alloc((512,), BF16)
        kt = A.alloc((512,), BF16)
        kh = A.alloc((512,), BF16)
        khm = A.alloc((4, 512), BF16)
        vb = A.alloc((1024,), BF16)
        TT = A.alloc((4, 512), BF16)
        at = A.alloc((4, 128), BF16)
        on2 = [A.alloc((4, 256), BF16) for _ in range(2)]
        sel_q = []
        eb = A.alloc((16,), F32)
        ssg = A.alloc((8,), F32)
        Sst = A.alloc((4, 256), F32)
        Sbf = [A.alloc((4, 256), BF16) for _ in range(2)]
        Ssm = [A.at(Wn2.base + g * 4096, (4, 256), F32) for g in range(4)]
        Ssmb = [A.at(Wn2.base + 16384 + g * 2048, (4, 256), BF16) for g in range(4)]
        memset("dve", Sst.v(), 0.0)
        memset("dve", Sbf[0].v(), 0.0)
        sbf_cur = [0]
        DKS = 128 ** -0.5

        norm_q = []

        def a2_norm(blk, defer=True):
            (j, tok0, T) = blk
            LB("A2.norm")
            pT = slice(0, T)
            xs = xs_ring.next()
            aT = aT_ring.next()
            ss = ss_ring.next()
            load("sp", xs.v(p=pT), x_nat[tok0:tok0 + T, :])
            norm_transpose(xs, T, 0, lambda c0, c1: aT.v(slice(c0, c1), slice(0, T)), PSB[5], PSB[5], ss, xn1,
                           defer=(norm_q if defer else None), sq_junk=junk2)
            return aT

        def nfiller(n):
            if not norm_q:
                return
            saved = S.label
            LB("A2.norm")
            for _ in range(min(n, len(norm_q))):
                norm_q.pop(0)()
            S.label = saved

        def a2_proj_pieces(blk, aT):
            (j, tok0, T) = blk
            pT = slice(0, T)
            pieces = []
            for k in range(16):
                for b in range(4):
                    def piece(k=k, b=b):
                        mm(PS[b].v(p=pT), aT.v(k, slice(0, T)), Wn2.v(k, slice(b * 512, (b + 1) * 512)), k == 0, k == 15)
                    pieces.append(piece)
            return pieces

        fill_q = []
        sel_hold = [0]

        def filler(n):
            nfiller(1 if n < 100 else 1000)
            if sel_q and sel_hold[0] <= 0:
                sel_q.pop(0)()
            sel_hold[0] -= 1
            if not fill_q:
                return
            saved = S.label
            LB("A2.proj")
            for _ in range(min(n, len(fill_q))):
                fill_q.pop(0)()
            S.label = saved

        def a2_decay(blk, aT):
            (j, tok0, T) = blk
            LB("A2.decay")
            pT = slice(0, T)
            prompt = j < 16
            tag = "p" if prompt else "s"
            nseg = 2 if prompt else 4
            Pq, Pk, Pv0, Pv1, Pg, Px = PS[0], PS[1], PS[2], PS[3], PS[4], PS[5]
            for k in range(16):
                mm(Pg.v(slice(0, T), p=slice(0, 16)), Wn2.v(k, slice(2048, 2064)), aT.v(k, slice(0, T)), k == 0, k == 15)
            cp("act", glrT.v(slice(0, T), p=slice(0, 16)), Pg.v(slice(0, T), p=slice(0, 16)))
            mm(Px.v(p=pT), glrT.v(slice(0, T), p=slice(0, 17)), WGA.v(p=slice(0, 17)), True, True)
            act(la.v(p=pT), Px.v(p=pT), AF.Exp, scale=-1.0)
            act(la.v(p=pT), la.v(p=pT), AF.Ln, bias=1.0)
            cp("act", vb.v(slice(0, 512), p=pT), Pv0.v(p=pT))
            cp("dve", vb.v(slice(512, 1024), p=pT), Pv1.v(p=pT))
            Pb, Prb, Pbl = PS[4], PS[5], PS[4]
            mm(Pb.v(p=pT), cfv("tri_incl_" + tag, T, 0, T), la.v(p=pT), True, True)
            mm(Prb.v(p=pT), cfv("tri_rev_" + tag, T, 0, T), la.v(p=pT), True, True)
            E1 = E_ring.next()
            act(E1.v(p=pT), Pb.v(p=pT), AF.Exp)
            stt(qt.v(p=pT), Pq.v(p=pT), DKS, E1.v(p=pT), ALU.mult, ALU.mult)
            E2 = E_ring.next()
            act(E2.v(p=pT), Pb.v(p=pT), AF.Exp, scale=-1.0)
            tt("dve", kt.v(p=pT), Pk.v(p=pT), E2.v(p=pT), ALU.mult)
            E3 = E_ring.next()
            act(E3.v(p=pT), Prb.v(p=pT), AF.Exp)
            tt("dve", kh.v(p=pT), Pk.v(p=pT), E3.v(p=pT), ALU.mult)
            for h in range(4):
                mm(Pbl.v(slice(h * nseg, (h + 1) * nseg)), la.v(slice(h * 128, (h + 1) * 128), p=pT),
                   cfv("ind_" + tag, T), h == 0, h == 3)
            act(eb.v(slice(0, 4 * nseg)), Pbl.v(slice(0, 4 * nseg)), AF.Exp)
            for g in range(nseg):
                ts("dve", khm.v(g, p=pT), kh.v(p=pT), cfv("rowm_" + tag, T, g, g + 1), ALU.mult)

        def a2_rest(blk):
            (j, tok0, T) = blk
            on = on2[j % 2]
            sel_hold[0] = 5
            pT = slice(0, T)
            prompt = j < 16
            tag = "p" if prompt else "s"
            nseg = 2 if prompt else 4
            LB("A2.trans")
            W = (2 + nseg) * T
            for h in range(4):
                PSt = PS[4 + h % 2]
                hs = slice(h * 128, (h + 1) * 128)
                mm(PSt.v(slice(0, T)), kt.v(hs, p=pT), ident(T), True, False)
                mm(PSt.v(slice(T, 2 * T)), qt.v(hs, p=pT), ident(T), False, False)
                for g in range(nseg):
                    mm(PSt.v(slice((2 + g) * T, (3 + g) * T)), qt.v(hs, p=pT), cbv(f"seg_{tag}{g}", T, 0, T),
                       False, g == nseg - 1)
                cp("act" if h % 2 == 0 else "dve", TT.v(h, slice(0, W)), PSt.v(slice(0, W)))
                filler(4)
            PSa = PS[4]
            for h in range(4):
                mm(PSa.v(slice(h * T, (h + 1) * T), p=pT), TT.v(h, slice(0, T)), TT.v(h, slice(T, 2 * T)), h == 0, h == 3)
            msk = bview(cfv("mask_" + tag, T, 0, T), lambda ap: ap.unsqueeze(1).broadcast_to([T, 4, T]))
            tt("dve", at.v(slice(None), slice(0, T), p=pT),
               bview(PSa.v(slice(0, 4 * T), p=pT), lambda ap: ap.rearrange("p (a b) -> p a b", b=T)), msk, ALU.mult)
            filler(4)
            LB("A2.state")
            if not prompt:
                for g in range(4):
                    load("sp", Ssm[g].v(), st_in[g].rearrange("h d v -> d h v"))
                    cp("act", Ssmb[g].v(), Ssm[g].v())
            Po = PS[4]
            PSu = PS[5]
            for hp in range(2):
                LB("A2.state")
                for h in (2 * hp, 2 * hp + 1):
                    out = Po.v(slice((h % 2) * 256, (h % 2 + 1) * 256), p=pT)
                    vh = vb.v(slice(h * 256, (h + 1) * 256), p=pT)
                    mm(out, at.v(h, slice(0, T), p=pT), vh, h % 2 == 0, False)
                for g in range(nseg):
                    for h in (2 * hp, 2 * hp + 1):
                        out = Po.v(slice((h % 2) * 256, (h % 2 + 1) * 256), p=pT)
                        vh = vb.v(slice(h * 256, (h + 1) * 256), p=pT)
                        if prompt:
                            S_f, S_b = Sst, Sbf[(sbf_cur[0] + g) % 2]
                            S_bn = Sbf[(sbf_cur[0] + g + 1) % 2]
                        else:
                            S_f, S_b, S_bn = Ssm[g], Ssmb[g], None
                        mm(out, TT.v(h, slice((2 + g) * T, (3 + g) * T)), S_b.v(h), False, g == nseg - 1)
                        PSu = PS[5] if h % 2 == 0 else PS[7]
                        mm(PSu.v(slice(0, 256)), khm.v(g, slice(h * 128, (h + 1) * 128), p=pT), vh, True, True)
                        stt(S_f.v(h), S_f.v(h), eb.v(slice(h * nseg + g, h * nseg + g + 1)), PSu.v(slice(0, 256)),
                            ALU.mult, ALU.add)
                        if S_bn is not None:
                            cp("act", S_bn.v(h), S_f.v(h))
                        filler(5 if prompt else 3)
                LB("A2.onorm")
                for h in (2 * hp, 2 * hp + 1):
                    act(junk.v(slice((h % 2) * 256, (h % 2 + 1) * 256), p=pT),
                        Po.v(slice((h % 2) * 256, (h % 2 + 1) * 256), p=pT), AF.Square, accum=ssg.v(slice(h, h + 1), p=pT))
                hsl2 = slice(2 * hp, 2 * hp + 2)
                hsl3 = slice(4 + 2 * hp, 6 + 2 * hp)
                rstd_act(ssg.v(hsl3, p=pT), ssg.v(hsl2, p=pT), 1.0 / 256)
                for h in (2 * hp, 2 * hp + 1):
                    ts("dve", on.v(h, p=pT), Po.v(slice((h % 2) * 256, (h % 2 + 1) * 256), p=pT),
                       ssg.v(slice(4 + h, 5 + h), p=pT), ALU.mult)
                filler(4)
            if prompt:
                sbf_cur[0] = (sbf_cur[0] + nseg) % 2
            else:
                for g in range(4):
                    store(gs_out[g].rearrange("h d v -> d h v"), Ssm[g].v())
            o0 = (j // 2) * 128 if prompt else 1024
            Ts = T
            for half in range(2):
                def sel_piece(half=half):
                    saved = S.label
                    LB("A2.sel")
                    pb = PSB[6]
                    for ff in range(4):
                        f = half * 4 + ff
                        tr(pb.v(slice(ff * 128, ff * 128 + Ts)), on.v(f // 2, slice((f % 2) * 128, (f % 2 + 1) * 128), p=pT), ident(T))
                    src = bview(pb.v(slice(0, 512)), lambda ap: ap.rearrange("p (a b) -> p a b", b=128)[:, :, 0:Ts])
                    dst = mixT.v(slice(8 + half * 4, 12 + half * 4), slice(o0, o0 + Ts))
                    if not prompt:
                        cp("dve", dst, src)
                    elif j % 2 == 0:
                        ts("dve", dst, src, cfv("selw", 128, 0, 1), ALU.mult)
                    else:
                        stt(dst, src, cfv("selw", 128, 1, 2), dst, ALU.mult, ALU.add)
                    S.label = saved
                sel_q.append(sel_piece)
            filler(1000)

        aTs = {0: a2_norm(nat_blocks[0], defer=False), 1: a2_norm(nat_blocks[1], defer=False)}
        fill_q.extend(a2_proj_pieces(nat_blocks[0], aTs[0]))
        filler(1000)
        for bidx, blk in enumerate(nat_blocks):
            a2_decay(blk, aTs[bidx])
            if bidx + 2 < len(nat_blocks):
                aTs[bidx + 2] = a2_norm(nat_blocks[bidx + 2])
            if bidx + 1 < len(nat_blocks):
                fill_q.extend(a2_proj_pieces(nat_blocks[bidx + 1], aTs[bidx + 1]))
            a2_rest(blk)
        while sel_q:
            sel_q.pop(0)()
        hsl = Sst.v()
        store(gp_out.rearrange("h d v -> d h v"), hsl)
        dbg("mixT", mixT.v(slice(8, 16)))
        A.release(mA2)
        if stop_after == "A2":
            return finish()

        LB("B")
        q_nopeT = A.alloc((8, NOWN), BF16)
        q_ropeT = A.alloc((8, NOWN), BF16)
        mB = A.mark()
        import os
        lm = (lambda name: print("LANDMARK", name, len(S.ops))) if os.environ.get("LANDMARKS") else (lambda name: None)
        lm("B start")
        ROPQ = A.alloc((2, NOWN), F32)
        cq_o = cf_off["cosq"][0]
        sq_o = cf_off["sinq"][0]
        load("sp", ROPQ.v(0), constf[:, cq_o:cq_o + NOWN])
        load("sp", ROPQ.v(1), constf[:, sq_o:sq_o + NOWN])
        WUQ = A.alloc((4, 2048), BF16)
        load("pool", WUQ.v(slice(None), slice(0, 1536)), w_uq.rearrange("(k p) n -> p k n", p=128))
        w3 = lambda a, b: bview(WUQ.v(slice(None), slice(0, 1536)),
                                lambda ap: ap.rearrange("p k (h e) -> p k h e", e=192)[:, :, :, a:b])
        r3 = lambda a, b: bview(WUQ.v(slice(None), slice(1536, 2048)),
                                lambda ap: ap.rearrange("p k (h e) -> p k h e", e=64)[:, :, :, a:b])
        _o, _i = r3(0, 32), w3(160, 192)
        S.add("act", lambda e: e.mul(out=_o.ap, in_=_i.ap, mul=-1.0), r=[_i], w=[_o])
        cp("dve", r3(32, 64), w3(128, 160))
        c_qnT = A.alloc((4, NOWN), BF16)
        aTg2 = [A.at(mixT.base, (16, 256), BF16), A.at(mixT.base + 16 * 256 * 2, (16, 256), BF16)]
        cqf = A.alloc((4, 256), F32)
        sqb = A.alloc((4, 256), BF16)
        rsb = A.alloc((256,), F32)
        sgb_ring = Ring(A, 2, (256,), F32)
        WOWN = A.alloc((16, 1536), BF16)
        load("pool", WOWN.v(slice(None), slice(0, 512)), w_in_k[:, :, 0:512])
        load("pool", WOWN.v(slice(None), slice(512, 1024)), w_in_k[:, :, 3152:3664])
        load("pool", WOWN.v(slice(None), slice(1024, 1536)), w_in_k[:, :, 3664:4176])
        xs_ring = Ring(A, 2, (2048,), F32)
        xnB = A.alloc((2048,), BF16)
        ss_ring = Ring(A, 8, (16,), F32)
        t_ring = Ring(A, 2, (256,), F32)
        pbank = [0]

        def next_bank():
            b = PS[pbank[0] % 4]
            pbank[0] += 1
            return b

        b_groups = [(0, 256), (256, 256), (512, 256), (768, 256), (1024, 64)]
        b_q = []

        def b_norm(gi, defer=True):
            (t0, N) = b_groups[gi]
            aTg_ = aTg2[gi % 2]
            nb = (N + 127) // 128
            for bi in range(nb):
                T = min(128, N - bi * 128)
                xs = xs_ring.next()
                ss = ss_ring.next()

                def ld(xs=xs, T=T, bi=bi):
                    load("sp", xs.v(p=slice(0, T)), x_own[t0 + bi * 128:t0 + bi * 128 + T, :])
                if defer:
                    b_q.append(ld)
                else:
                    ld()
                norm_transpose(xs, T, 0, lambda c0, c1, bi=bi, T=T: aTg_.v(slice(c0, c1), slice(bi * 128, bi * 128 + T)),
                               PSB[6], PSB[7], ss, xnB, defer=(b_q if defer else None))

        def b_fill(n):
            for _ in range(min(n, len(b_q))):
                b_q.pop(0)()

        b_norm(0, False)
        for gi, (t0, N) in enumerate(b_groups):
            aTg = aTg2[gi % 2]
            if gi + 1 < len(b_groups):
                b_norm(gi + 1)
            tk = slice(t0, t0 + N)
            nn = slice(0, N)
            lm("B norm done")
            for m in range(4):
                P = next_bank()
                for k in range(16):
                    mm(P.v(nn), WOWN.v(k, slice(m * 128, (m + 1) * 128)), aTg.v(k, nn), k == 0, k == 15)
                cp("dve", cqf.v(m, nn), P.v(nn))
                act(sqb.v(m, nn), P.v(nn), AF.Square)
                b_fill(1)
            lm("B cq mm done")
            Pss = PS[4]
            for m in range(4):
                mm(Pss.v(nn), cbv("ones"), sqb.v(m, nn), m == 0, m == 3)
            rstd_act(rsb.v(nn), Pss.v(nn), 1.0 / 512)
            for m in range(4):
                stt(c_qnT.v(m, tk), cqf.v(m, nn), GV.v(slice(32 + m, 33 + m)), rsb.v(nn), ALU.mult, ALU.mult)
            lm("B cqn done")
            for f in range(8):
                P = next_bank()
                for k in range(16):
                    mm(P.v(nn), WOWN.v(k, slice(512 + f * 128, 512 + (f + 1) * 128)), aTg.v(k, nn), k == 0, k == 15)
                sg = sgb_ring.next()
                act(sg.v(nn), P.v(nn), AF.Silu)
                stt(mixT.v(8 + f, tk), sg.v(nn), GV.v(slice(36 + f % 2, 37 + f % 2)), mixT.v(8 + f, tk), ALU.mult, ALU.mult)
                b_fill(2)
            lm("B rg done")
            for h in range(8):
                P = next_bank()
                for c in range(4):
                    mm(P.v(nn), WUQ.v(c, slice(h * 192, h * 192 + 128)), c_qnT.v(c, tk), c == 0, c == 3)
                cp("act", q_nopeT.v(h, tk), P.v(nn))
                Px, Pxr = (PS[5], PS[4]) if h % 2 == 0 else (PS[6], PS[7])
                h64 = slice(0, 64)
                for c in range(4):
                    mm(Px.v(nn, p=h64), WUQ.v(c, slice(h * 192 + 128, h * 192 + 192)), c_qnT.v(c, tk), c == 0, c == 3)
                for c in range(4):
                    mm(Pxr.v(nn, p=h64), WUQ.v(c, slice(1536 + h * 64, 1536 + (h + 1) * 64)), c_qnT.v(c, tk), c == 0, c == 3)
                ta = t_ring.next()
                tb = t_ring.next()
                tt("dve", ta.v(nn, p=h64), Px.v(nn, p=h64), ROPQ.v(0, tk, p=h64), ALU.mult)
                tt("dve", tb.v(nn, p=h64), Pxr.v(nn, p=h64), ROPQ.v(1, tk, p=h64), ALU.mult)
                tt("dve", q_ropeT.v(h, tk, p=h64), ta.v(nn, p=h64), tb.v(nn, p=h64), ALU.add)
                b_fill(1)
            b_fill(1000)
        lm("B end")
        dbg("qn", q_nopeT.v())
        dbg("qr", q_ropeT.v(p=slice(0, 64)))
        A.release(mB)
        if stop_after == "B":
            return finish()

        LB("A1")
        ckv_tok = A.alloc((17, 576), BF16)
        ckvT = A.alloc((5, NNAT), BF16)
        mA1 = A.mark()
        Wn1 = A.alloc((16, 640), BF16)
        load("pool", Wn1.v(slice(None), slice(0, 576)), w_in_k[:, :, 512:1088])
        S.add("act", lambda e: e.mul(out=Wn1.v(slice(None), slice(576, 608)).ap,
                                     in_=Wn1.v(slice(None), slice(544, 576)).ap, mul=-1.0),
              r=[Wn1.v(slice(None), slice(544, 576))], w=[Wn1.v(slice(None), slice(576, 608))])
        cp("dve", Wn1.v(slice(None), slice(608, 640)), Wn1.v(slice(None), slice(512, 544)))
        GKV = A.alloc((512,), F32)
        load("sp", GKV.v(), g_kv_b)
        ROPK = A.alloc((17 * 128,), F32)
        ropek_o = cf_off["ropek"][0]
        load("sp", ROPK.v(), constf[:, ropek_o:ropek_o + 17 * 128])
        xs_ring = Ring(A, 3, (2048,), F32)
        xn_ring = Ring(A, 2, (2048,), BF16)
        aT_ring = Ring(A, 3, (16, 128), BF16)
        ss_ring = Ring(A, 8, (16,), F32)
        ckvf_ring = Ring(A, 2, (576,), F32)
        tmp_ring = Ring(A, 2, (128,), F32)

        a1_q = []

        def a1_stage0(blk, defer=True):
            (j, tok0, T) = blk
            pT = slice(0, T)
            xs = xs_ring.next()
            xn = xn_ring.next()
            aT = aT_ring.next()
            ss = ss_ring.next()
            load("sp", xs.v(p=pT), x_nat[tok0:tok0 + T, :])
            norm_transpose(xs, T, 0, lambda c0, c1: aT.v(slice(c0, c1), slice(0, T)), PSB[4], PSB[5], ss, xn,
                           defer=(a1_q if defer else None))
            return aT

        def a1_fill(n):
            for _ in range(min(n, len(a1_q))):
                a1_q.pop(0)()

        a1T = {0: a1_stage0(nat_blocks[0], False), 1: a1_stage0(nat_blocks[1], False)}
        for bidx, (j, tok0, T) in enumerate(nat_blocks):
            pT = slice(0, T)
            aT = a1T[bidx]
            if bidx + 2 < len(nat_blocks):
                a1T[bidx + 2] = a1_stage0(nat_blocks[bidx + 2])
            Pc = PS[j % 2]
            Pr = PS[2 + j % 2]
            for k in range(16):
                mm(Pc.v(p=pT), aT.v(k, slice(0, T)), Wn1.v(k, slice(0, 512)), k == 0, k == 15)
                if k % 4 == 3:
                    a1_fill(1)
            for k in range(16):
                mm(Pr.v(slice(0, 128), p=pT), aT.v(k, slice(0, T)), Wn1.v(k, slice(512, 640)), k == 0, k == 15)
                if k % 4 == 3:
                    a1_fill(1)
            a1_fill(100)
            ss2 = ss_ring.next()
            act(junk.v(slice(0, 512), p=pT), Pc.v(p=pT), AF.Square, accum=ss2.v(slice(0, 1), p=pT))
            rstd_act(ss2.v(slice(1, 2), p=pT), ss2.v(slice(0, 1), p=pT), 1.0 / 512)
            cf_ = ckvf_ring.next()
            stt(cf_.v(slice(0, 512), p=pT), Pc.v(p=pT), ss2.v(slice(1, 2), p=pT), GKV.v(p=pT), ALU.mult, ALU.mult)
            tmp = tmp_ring.next()
            tt("dve", tmp.v(p=pT), Pr.v(slice(0, 128), p=pT), ROPK.v(slice(j * 128, (j + 1) * 128), p=pT), ALU.mult)
            tt("dve", cf_.v(slice(512, 576), p=pT), tmp.v(slice(0, 64), p=pT), tmp.v(slice(64, 128), p=pT), ALU.add)
            store(ckv_out[tok0:tok0 + T, :], cf_.v(slice(0, 512), p=pT))
            store(kr_out[tok0:tok0 + T, :], cf_.v(slice(512, 576), p=pT))
            cp("act", ckv_tok.v(j, p=pT), cf_.v(p=pT))
            pb = PSB[6]
            pb2 = PSB[7]
            for c in range(4):
                tr(pb.v(slice(c * 128, c * 128 + T)), ckv_tok.v(j, slice(c * 128, (c + 1) * 128), p=pT), ident(T))
            tr(pb2.v(slice(0, T), p=slice(0, 64)), ckv_tok.v(j, slice(512, 576), p=pT), ident(T))
            src = bview(pb.v(slice(0, 512)), lambda ap: ap.rearrange("p (a b) -> p a b", b=128)[:, :, 0:T])
            cp("dve", ckvT.v(slice(0, 4), slice(tok0, tok0 + T)), src)
            cp("act", ckvT.v(4, slice(tok0, tok0 + T), p=slice(0, 64)), pb2.v(slice(0, T), p=slice(0, 64)))
        dbg("ckvT", ckvT.v())
        A.release(mA1)
        if stop_after == "A1":
            return finish()

        LB("C.prep")
        mC = A.mark()
        WUKT = A.alloc((8, 512), BF16)
        WUV = A.alloc((4, 1024), BF16)
        load("pool", WUV.v(), w_uv.rearrange("(k p) n -> p k n", p=128))
        q_cat = A.alloc((4, 2, 512), BF16)
        pT_ring = Ring(A, 3, (512,), BF16)
        rl = A.alloc((512,), F32)
        onb = A.alloc((4, 512), BF16)
        q_cat_s = A.alloc((4, 8, 64), BF16)
        onb_s = A.alloc((4, 128), BF16)
        mC2 = A.mark()
        WUKb = A.alloc((4, 1024), BF16)
        load("pool", WUKb.v(), w_uk.rearrange("(k p) n -> p k n", p=128))
        for h in range(8):
            pb = PSB[6 + h % 2]
            for cc in range(4):
                tr(pb.v(slice(cc * 128, (cc + 1) * 128)), WUKb.v(cc, slice(h * 128, (h + 1) * 128)), ident(128))
            cp("act" if h % 2 == 0 else "dve", WUKT.v(h), pb.v(slice(0, 512)))
        A.release(mC2)
        h64 = slice(0, 64)
        mQ = A.mark()
        q_catB = A.alloc((4, 2, 512), BF16)
        qcs = [q_cat, q_catB]

        def qlat_pieces(i, qc):
            tk = slice(i * 128, (i + 1) * 128)
            pieces = []
            for hg in range(2):
                for cc in range(4):
                    def piece(hg=hg, cc=cc):
                        saved = S.label
                        LB("C.qlat")
                        Pq = PS[7]
                        for hh in range(4):
                            h = hg * 4 + hh
                            mm(Pq.v(slice(hh * 128, (hh + 1) * 128)), WUKT.v(h, slice(cc * 128, (cc + 1) * 128)),
                               q_nopeT.v(h, tk), hh == 0, hh == 3)
                        cp("act" if cc % 2 == 0 else "dve", qc.v(cc, hg), Pq.v())
                        S.label = saved
                    pieces.append(piece)
            return pieces

        def c_scores(i, hg, kb):
            tk = slice(i * 128, (i + 1) * 128)
            qc = qcs[i % 2]
            ks = slice(kb * 128, (kb + 1) * 128)
            Ps = PS[5 + kb % 2]
            for cc in range(4):
                mm(Ps.v(), ckvT.v(cc, ks), qc.v(cc, hg), cc == 0, False)
            mm(Ps.v(), ckvT.v(4, ks, p=h64), q_ropeT.v(slice(hg * 4, hg * 4 + 4), tk, p=h64), False, True)
            pT_ = pT_ring.next()
            act(pT_.v(), Ps.v(), AF.Exp, scale=MLA_SCALE)
            if kb >= 2 * i:
                tt("dve", pT_.v(), pT_.v(), cbv("mask_lo4" if kb == 2 * i else "mask_hi4"), ALU.mult)
            return pT_

        def c_pv(i, kb, pT_):
            nkb = 2 * i + 2
            for cc in range(4):
                mm(PS[cc].v(), ckv_tok.v(kb, slice(cc * 128, (cc + 1) * 128)), pT_.v(), kb == 0, kb == nkb - 1)
            mm(PS[4].v(), cbv("ones"), pT_.v(), kb == 0, kb == nkb - 1)

        for pc in qlat_pieces(0, qcs[0]):
            pc()
        ql_q = []
        units = [(i, hg) for i in range(8) for hg in range(2)]
        pend = None
        for ui, (i, hg) in enumerate(units):
            LB("C.attn")
            tk = slice(i * 128, (i + 1) * 128)
            nkb = 2 * i + 2
            if hg == 0 and i + 1 < 8:
                ql_q.extend(qlat_pieces(i + 1, qcs[(i + 1) % 2]))
            cur = pend if pend is not None else c_scores(i, hg, 0)
            pend = None
            for kb in range(nkb):
                nxt = c_scores(i, hg, kb + 1) if kb + 1 < nkb else None
                c_pv(i, kb, cur)
                cur = nxt
                if ql_q and kb % 2 == 1:
                    ql_q.pop(0)()
            LB("C.fin")
            for cc in range(4):
                cp("act" if cc % 2 == 0 else "dve", onb.v(cc), PS[cc].v())
            recip(rl.v(), PS[4].v())
            if hg == 1:
                while ql_q:
                    ql_q.pop(0)()
            if ui + 1 < len(units):
                LB("C.attn")
                ni, nhg = units[ui + 1]
                pend = c_scores(ni, nhg, 0)
                LB("C.fin")
            Pm = PS[7]
            for hh in range(4):
                h = hg * 4 + hh
                for cc in range(4):
                    mm(Pm.v(slice(hh * 128, (hh + 1) * 128)), WUV.v(cc, slice(h * 128, (h + 1) * 128)),
                       onb.v(cc, slice(hh * 128, (hh + 1) * 128)), hh == 0 and cc == 0, hh == 3 and cc == 3)
            tt("dve", mixT.v(slice(hg * 4, hg * 4 + 4), tk),
               bview(Pm.v(), lambda ap: ap.rearrange("p (a b) -> p a b", b=128)),
               bview(rl.v(), lambda ap: ap.rearrange("p (a b) -> p a b", b=128)), ALU.mult)
        A.release(mQ)
        LB("C.smp")
        for cc in range(4):
            Pq = PS[7]
            for h in range(8):
                mm(Pq.v(slice(h * 64, (h + 1) * 64)), WUKT.v(h, slice(cc * 128, (cc + 1) * 128)),
                   q_nopeT.v(h, slice(1024, 1088)), h == 0, h == 7)
            cp("act" if cc % 2 == 0 else "dve", q_cat_s.v(cc), Pq.v())
        ctok_ring = Ring(A, 2, (8, 576), BF16)
        cT_ring = Ring(A, 2, (5, 1024), BF16)
        c128 = slice(0, 128)
        for s_ in range(4):
            qs = slice(16 * s_, 16 * s_ + 16)
            tq = slice(1024 + 16 * s_, 1024 + 16 * s_ + 16)
            rhs_cc = [q_cat_s.v(cc, slice(None), qs) for cc in range(4)]
            rhs_rope = q_ropeT.v(slice(0, 8), tq, p=h64)
            Po, Pl = PS[0], PS[1]
            idx = 0
            for ch in range(4):
                ctok = ctok_ring.next()
                cT = cT_ring.next()
                load("pool", ctok.v(slice(None), slice(0, 512)),
                     c_ckv[s_, ch * 1024:(ch + 1) * 1024, :].rearrange("(kb p) c -> p kb c", p=128))
                load("pool", ctok.v(slice(None), slice(512, 576)),
                     c_kr[s_, ch * 1024:(ch + 1) * 1024, :].rearrange("(kb p) c -> p kb c", p=128))
                for kb in range(8):
                    ks = slice(kb * 128, (kb + 1) * 128)
                    pb = PSB[2 + kb % 2]
                    for cc in range(4):
                        tr(pb.v(slice(cc * 128, (cc + 1) * 128)), ctok.v(kb, slice(cc * 128, (cc + 1) * 128)), ident(128))
                    cp("dve", cT.v(slice(0, 4), ks), bview(pb.v(slice(0, 512)), lambda ap: ap.rearrange("p (a b) -> p a b", b=128)))
                    tr(PSB[4].v(c128, p=h64), ctok.v(kb, slice(512, 576)), ident(128))
                    cp("act", cT.v(4, ks, p=h64), PSB[4].v(c128, p=h64))
                def s_scores(kb, cT=cT):
                    ks = slice(kb * 128, (kb + 1) * 128)
                    Ps = PS[5 + kb % 2]
                    for cc in range(4):
                        mm(Ps.v(c128), cT.v(cc, ks), rhs_cc[cc], cc == 0, False)
                    mm(Ps.v(c128), cT.v(4, ks, p=h64), rhs_rope, False, True)
                    pT_ = pT_ring.next()
                    act(pT_.v(c128), Ps.v(c128), AF.Exp, scale=MLA_SCALE)
                    return pT_

                def s_pv(kb, pT_, first, ctok=ctok):
                    for cc in range(4):
                        mm(Po.v(slice(cc * 128, (cc + 1) * 128)), ctok.v(kb, slice(cc * 128, (cc + 1) * 128)), pT_.v(c128),
                           first and cc == 0, False)
                    mm(Pl.v(c128), cbv("ones"), pT_.v(c128), first, False)

                pend = s_scores(0)
                for kb in range(8):
                    nxt = s_scores(kb + 1) if kb + 1 < 8 else None
                    s_pv(kb, pend, idx == 0)
                    pend = nxt
                    idx += 1
            Ps = PS[5]
            kn = slice(2048, 2112)
            for cc in range(4):
                mm(Ps.v(c128, p=h64), ckvT.v(cc, kn), rhs_cc[cc], cc == 0, False)
            mm(Ps.v(c128, p=h64), ckvT.v(4, kn, p=h64), rhs_rope, False, True)
            pT_ = pT_ring.next()
            act(pT_.v(c128, p=h64), Ps.v(c128, p=h64), AF.Exp, scale=MLA_SCALE)
            tt("dve", pT_.v(c128, p=h64), pT_.v(c128, p=h64), cbv(f"smask{s_}", 64), ALU.mult)
            for cc in range(4):
                mm(Po.v(slice(cc * 128, (cc + 1) * 128)), ckv_tok.v(16, slice(cc * 128, (cc + 1) * 128), p=h64), pT_.v(c128, p=h64),
                   False, cc == 3)
            mm(Pl.v(c128), cbv("ones", 64), pT_.v(c128, p=h64), False, True)
            recip(rl.v(c128), Pl.v(c128))
            tt("dve", onb_s.v(), bview(Po.v(), lambda ap: ap.rearrange("p (a b) -> p a b", b=128)),
               bview(rl.v(c128), lambda ap: ap.unsqueeze(1).broadcast_to([128, 4, 128])), ALU.mult)
            Pm = PS[7]
            for h in range(8):
                for cc in range(4):
                    mm(Pm.v(slice(h * 16, (h + 1) * 16)), WUV.v(cc, slice(h * 128, (h + 1) * 128)),
                       onb_s.v(cc, slice(h * 16, (h + 1) * 16)), h == 0 and cc == 0, h == 7 and cc == 3)
            cp("act", mixT.v(slice(0, 8), tq), bview(Pm.v(c128), lambda ap: ap.rearrange("p (a b) -> p a b", b=16)))
        dbg("mixA", mixT.v(slice(0, 8)))
        A.release(mC)
        if stop_after == "C":
            return finish()
        A.release(m_tail)

        store_q[0] = "sp"
        hB = A.alloc((5, D), F32)
        tmpB = A.alloc((5, D), F32)
        fT = A.alloc((16, 576), BF16)
        actT = A.alloc((11, 576), BF16)
        wring = Ring(A, 4, (4, 512), BF16)
        GB = A.alloc((D,), F32)
        pst = A.alloc((256,), F32)
        psb = A.alloc((256,), BF16)
        pTb = A.alloc((2, 576), BF16)
        sg_ring = Ring(A, 2, (512,), F32)
        sg2 = A.alloc((64,), F32)
        ssD = Ring(A, 8, (16,), F32)
        xnD = A.alloc((D,), BF16)
        w_out_k = w_out.rearrange("(k p) n -> p k n", p=128)
        w_gate_k = w_gate.rearrange("(k p) n -> p k n", p=128)
        w_up_k = w_up.rearrange("(k p) n -> p k n", p=128)
        w_down_k = w_down.rearrange("(k p) n -> p k n", p=128)
        w_pg_k = w_pg.rearrange("(k p) n -> p k n", p=128)
        w_ple_k = w_ple.rearrange("(k p) n -> p k n", p=128)

        def wtile_kn(src_k, k0, nk, c0, ncols):
            wt = wring.next()
            v = bview(wt.v(), lambda ap: ap.rearrange("p a b -> p (a b)")[:, 0:nk * ncols].rearrange("p (a b) -> p a b", b=ncols))
            dma("pool", v.ap, src_k[:, k0:k0 + nk, c0:c0 + ncols], w=[v])
            return wt, (lambda kk: View(v.ap[:, kk, :], wt.v().rg))

        groups = [[(0, 128), (128, 128), (256, 128), (384, 128), (1024, 64)],
                  [(512, 128), (640, 128), (768, 128), (896, 128)]]
        for grp in groups:
            nb = len(grp)
            offs = []
            o = 0
            for (_, T) in grp:
                offs.append(o)
                o += T
            ntok = o
            Nmain = min(512, ntok)
            rem = ntok - Nmain

            def tok_linear(src_k, nkc, lhs_fn, consume, ksplit=4):
                for n in range(4):
                    k = 0
                    while k < nkc:
                        nk = min(ksplit, nkc - k)
                        _, wv = wtile_kn(src_k, k, nk, n * 512, 512)
                        for kk in range(nk):
                            for bi, (tok0, T) in enumerate(grp):
                                mm(PS[bi].v(p=slice(0, T)), lhs_fn(k + kk, bi), wv(kk), k + kk == 0, k + kk == nkc - 1)
                        k += nk
                    for bi, (tok0, T) in enumerate(grp):
                        consume(n, bi, T)

            def post_norm_residual(first_x):
                for bi, (tok0, T) in enumerate(grp):
                    pT = slice(0, T)
                    ss = ssD.next()
                    act(junk.v(p=pT), tmpB.v(bi, p=pT), AF.Square, accum=ss.v(slice(0, 1), p=pT))
                    rstd_act(ss.v(slice(1, 2), p=pT), ss.v(slice(0, 1), p=pT), 1.0 / D)
                    if first_x:
                        load("sp", hB.v(bi, p=pT), x_own[tok0:tok0 + T, :])
                    stt(tmpB.v(bi, p=pT), tmpB.v(bi, p=pT), ss.v(slice(1, 2), p=pT), GB.v(p=pT), ALU.mult, ALU.mult)
                    tt("dve", hB.v(bi, p=pT), hB.v(bi, p=pT), tmpB.v(bi, p=pT), ALU.add)

            LB("D1.wout")
            load("sp", GB.v(), g_pm_b)
            tok_linear(w_out_k, 16, lambda k, bi: mixT.v(k, slice(grp[bi][0], grp[bi][0] + grp[bi][1])),
                       lambda n, bi, T: cp("act", tmpB.v(bi, slice(n * 512, (n + 1) * 512), p=slice(0, T)), PS[bi].v(p=slice(0, T))))
            LB("D2.norm")
            post_norm_residual(True)
            LB("D3.fT")
            for bi, (tok0, T) in enumerate(grp):
                ss = ssD.next()
                hv = Buf(hB.t[:, bi, :], hB.base + bi * D * 4, (D,), 4)
                norm_transpose(hv, T, 16, lambda c0, c1, bi=bi, T=T: fT.v(slice(c0, c1), slice(offs[bi], offs[bi] + T)),
                               PSB[6], PSB[7], ss, xnD)
            for qd in range(4):
                LB("D4.gateup")
                for ml in range(11):
                    m = qd * 11 + ml
                    wg = wring.next()
                    wgv = bview(wg.v(), lambda ap: ap.rearrange("p a b -> p (a b)").rearrange("p (a b) -> p a b", b=128))
                    dma("pool", wgv.ap, w_gate_k[:, :, m * 128:(m + 1) * 128], w=[wgv])
                    wu = wring.next()
                    wuv = bview(wu.v(), lambda ap: ap.rearrange("p a b -> p (a b)").rearrange("p (a b) -> p a b", b=128))
                    dma("pool", wuv.ap, w_up_k[:, :, m * 128:(m + 1) * 128], w=[wuv])
                    gk = lambda k: View(wgv.ap[:, k, :], wg.v().rg)
                    uk = lambda k: View(wuv.ap[:, k, :], wu.v().rg)
                    b0 = (m % 2) * 3
                    Pg_, Pu_, Pr_ = PS[b0], PS[b0 + 1], PS[b0 + 2]
                    nm = slice(0, Nmain)
                    for k in range(16):
                        mm(Pg_.v(nm), gk(k), fT.v(k, nm), k == 0, k == 15)
                    for k in range(16):
                        mm(Pu_.v(nm), uk(k), fT.v(k, nm), k == 0, k == 15)
                    if rem:
                        rs_ = slice(Nmain, ntok)
                        for k in range(16):
                            mm(Pr_.v(slice(0, rem)), gk(k), fT.v(k, rs_), k == 0, k == 15)
                        for k in range(16):
                            mm(Pr_.v(slice(64, 64 + rem)), uk(k), fT.v(k, rs_), k == 0, k == 15)
                    sg = sg_ring.next()
                    act(sg.v(nm), Pg_.v(nm), AF.Silu)
                    tt("dve", actT.v(ml, nm), sg.v(nm), Pu_.v(nm), ALU.mult)
                    if rem:
                        act(sg2.v(slice(0, rem)), Pr_.v(slice(0, rem)), AF.Silu)
                        tt("dve", actT.v(ml, rs_), sg2.v(slice(0, rem)), Pr_.v(slice(64, 64 + rem)), ALU.mult)

                def down_consume(n, bi, T, qd=qd):
                    dst = tmpB.v(bi, slice(n * 512, (n + 1) * 512), p=slice(0, T))
                    if qd == 0:
                        cp("act", dst, PS[bi].v(p=slice(0, T)))
                    else:
                        tt("dve", dst, dst, PS[bi].v(p=slice(0, T)), ALU.add)
                LB("D4.down")
                tok_linear(w_down_k[:, qd * 11:(qd + 1) * 11, :], 11,
                           lambda k, bi: actT.v(k, slice(offs[bi], offs[bi] + grp[bi][1])), down_consume)
            LB("D5.norm")
            load("sp", GB.v(), g_pf_b)
            post_norm_residual(False)
            LB("D6.ple_prep")
            for bi, (tok0, T) in enumerate(grp):
                pT = slice(0, T)
                cp("act", xnD.v(p=pT), hB.v(bi, p=pT))
                for c4 in range(4):
                    pb = PSB[6 + c4 % 2]
                    for jj in range(4):
                        c = c4 * 4 + jj
                        tr(pb.v(slice(jj * 128, jj * 128 + T)), xnD.v(slice(c * 128, (c + 1) * 128), p=pT), ident(T))
                    src = bview(pb.v(slice(0, 512)), lambda ap: ap.rearrange("p (a b) -> p a b", b=128)[:, :, 0:T])
                    cp("dve" if c4 % 2 == 0 else "act", fT.v(slice(c4 * 4, c4 * 4 + 4), slice(offs[bi], offs[bi] + T)), src)
                load("sp", pst.v(p=pT), p_own[tok0:tok0 + T, :])
                cp("dve", psb.v(p=pT), pst.v(p=pT))
                pb = PSB[6]
                for c in range(2):
                    tr(pb.v(slice(c * 128, c * 128 + T)), psb.v(slice(c * 128, (c + 1) * 128), p=pT), ident(T))
                src = bview(pb.v(slice(0, 256)), lambda ap: ap.rearrange("p (a b) -> p a b", b=128)[:, :, 0:T])
                cp("dve", pTb.v(slice(0, 2), slice(offs[bi], offs[bi] + T)), src)
            wp_holder = [None]

            def ple_consume(n, bi, T):
                pT = slice(0, T)
                if bi == 0:
                    wp_holder[0] = wtile_kn(w_ple_k, 0, 2, n * 512, 512)[1]
                Pe = PS[5 + bi % 3]
                for kc in range(2):
                    mm(Pe.v(p=pT), pTb.v(kc, slice(offs[bi], offs[bi] + T)), wp_holder[0](kc), kc == 0, kc == 1)
                sg = sg_ring.next()
                act(sg.v(p=pT), PS[bi].v(p=pT), AF.Sigmoid)
                tt("dve", sg.v(p=pT), sg.v(p=pT), Pe.v(p=pT), ALU.mult)
                ns = slice(n * 512, (n + 1) * 512)
                tt("dve", tmpB.v(bi, ns, p=pT), sg.v(p=pT), hB.v(bi, ns, p=pT), ALU.add)
            LB("D6.ple")
            tok_linear(w_pg_k, 16, lambda k, bi: fT.v(k, slice(offs[bi], offs[bi] + grp[bi][1])), ple_consume)
            for bi, (tok0, T) in enumerate(grp):
                store(y_out[tok0:tok0 + T, :], tmpB.v(bi, p=slice(0, T)))

        return finish()


_CACHE = {}


def _prep_inputs(inp):
    f32 = lambda a: np.ascontiguousarray(np.asarray(a, dtype=np.float32))
    xp = f32(inp["x_prompt"])
    xsm = f32(inp["x_sample"])
    pp = f32(inp["p_prompt"])[0]
    psm = f32(inp["p_sample"])[0]
    cck = f32(inp["cache_ckv"])[0]
    ckr = f32(inp["cache_krope"])[0]
    stg = f32(inp["state_gla"])[0]
    g = lambda k: f32(inp[k])[0]
    gvec = np.concatenate([g("g_pre_mix").reshape(16, 128).T, g("g_pre_ffn").reshape(16, 128).T,
                           g("g_q").reshape(4, 128).T, g("g_gla").reshape(2, 128).T], axis=1)
    shared = {
        "gvec": f32(gvec),
        "g_kv_b": f32(np.broadcast_to(g("g_kv")[None, :], (128, 512))),
        "g_pm_b": f32(np.broadcast_to(g("g_post_mix")[None, :], (128, D))),
        "g_pf_b": f32(np.broadcast_to(g("g_post_ffn")[None, :], (128, D))),
        "w_in": g("w_in"), "w_uq": g("w_uq"),
        "w_uk": f32(g("w_uk").reshape(512, 1024)), "w_uv": f32(g("w_uv").reshape(512, 1024)),
        "w_ga": f32(np.concatenate([g("w_ga"), g("b_ga")[None, :]], axis=0)),
        "w_out": g("w_out"), "w_gate": g("w_gate"), "w_up": g("w_up"), "w_down": g("w_down"),
        "w_ple": g("w_ple"), "w_pg": g("w_ple_gate"),
    }
    maps = []
    for c in range(8):
        b, p = c // 2, c % 2
        own = [2 * i + p for i in range(8)]
        xs_c = xsm[4 * c:4 * c + 4].reshape(64, D)
        m = dict(shared)
        m["x_nat"] = f32(np.concatenate([xp[b], xs_c], axis=0))
        m["x_own"] = f32(np.concatenate([xp[b].reshape(16, 128, D)[own].reshape(1024, D), xs_c], axis=0))
        m["p_own"] = f32(np.concatenate([pp[b].reshape(16, 128, 256)[own].reshape(1024, 256),
                                         psm[4 * c:4 * c + 4].reshape(64, 256)], axis=0))
        m["c_ckv"] = f32(cck[4 * c:4 * c + 4])
        m["c_kr"] = f32(ckr[4 * c:4 * c + 4])
        m["st_in"] = f32(stg[4 * c:4 * c + 4])
        cf, cb = build_consts(p)
        m["constf"] = cf.pack()
        m["constb"] = cb.pack()
        maps.append(m)
    return maps


def _get_program(debug=None, stop_after=None):
    key = (None if debug is None else tuple(sorted(debug)), stop_after)
    if key not in _CACHE:
        cf, cb = build_consts(0)
        cf.pack()
        cb.pack()
        _CACHE[key] = build_program(cf.off, cf.n, cb.off, cb.n, debug=debug, stop_after=stop_after)
    return _CACHE[key]


def kernel(**inp):
    maps = _prep_inputs(inp)
    nc, _ = _get_program()
    res = run_bass_kernel_spmd(nc, maps, core_ids=list(range(8)))
    R = res.results
    y_p = np.zeros((4, 2048, D), np.float32)
    y_s = np.zeros((32, 16, D), np.float32)
    ckv_p = np.zeros((1, 4, 2048, 512), np.float32)
    kr_p = np.zeros((1, 4, 2048, 64), np.float32)
    gl_p = np.zeros((1, 4, 4, 128, 256), np.float32)
    ckv_s = np.zeros((1, 32, 16, 512), np.float32)
    kr_s = np.zeros((1, 32, 16, 64), np.float32)
    gl_s = np.zeros((1, 32, 4, 128, 256), np.float32)
    for c in range(8):
        b, p = c // 2, c % 2
        r = R[c]
        yo = np.asarray(r["y_out"])
        for i in range(8):
            j = 2 * i + p
            y_p[b, j * 128:(j + 1) * 128] = yo[i * 128:(i + 1) * 128]
        y_s[4 * c:4 * c + 4] = yo[1024:1088].reshape(4, 16, D)
        ck = np.asarray(r["ckv_out"])
        kr = np.asarray(r["kr_out"])
        if p == 0:
            ckv_p[0, b] = ck[:2048]
            kr_p[0, b] = kr[:2048]
            gl_p[0, b] = np.asarray(r["gp_out"])
        ckv_s[0, 4 * c:4 * c + 4] = ck[2048:2112].reshape(4, 16, 512)
        kr_s[0, 4 * c:4 * c + 4] = kr[2048:2112].reshape(4, 16, 64)
        gl_s[0, 4 * c:4 * c + 4] = np.asarray(r["gs_out"])
    return (y_p, y_s, ckv_p, kr_p, gl_p, ckv_s, kr_s, gl_s)
```

```python
import numpy as np
import concourse.bass as bass
import concourse.mybir as mybir
from concourse.bass_utils import run_bass_kernel_spmd
from contextlib import ExitStack

F32 = mybir.dt.float32
BF16 = mybir.dt.bfloat16
AF = mybir.ActivationFunctionType
ALU = mybir.AluOpType

D = 2048
NOWN = 1088
NNAT = 2112
DFF = 5632
EPS = 1e-6
MLA_SCALE = 192 ** -0.5
PSUM_BASE = 1 << 20
PAGE = 64
SAME_ENGINE_SYNC = True


class Op:
    __slots__ = ("eng", "fn", "r", "w", "dma", "deps", "inc", "tok", "sem", "semval", "label")

    def __init__(self, eng, fn, r, w, dma, label=None):
        self.eng, self.fn, self.r, self.w, self.dma, self.label = eng, fn, r, w, dma, label
        self.deps = []
        self.inc = False
        self.tok = None
        self.sem = None
        self.semval = None


def _pages(views):
    out = []
    for v in views:
        lo, hi = v.rg
        if lo >= PSUM_BASE:
            out.extend(range(PSUM_BASE // PAGE + (lo - PSUM_BASE) // 2048,
                             PSUM_BASE // PAGE + (hi - PSUM_BASE + 2047) // 2048))
        else:
            out.extend(range(lo // PAGE, (hi + PAGE - 1) // PAGE))
    return out


class Sched:
    def __init__(self, nc, st):
        self.nc = nc
        self.ops = []
        self.label = None
        self.annotate = False
        self.engs = {"pe": nc.tensor, "act": nc.scalar, "dve": nc.vector, "pool": nc.gpsimd, "sp": nc.sync}
        self.esem = {e: st.enter_context(nc.semaphore("sem_" + e)) for e in ("pe", "act", "dve", "pool")}
        nd = {"sp": 16, "pool": 12}
        self.dsems = {q: [st.enter_context(nc.semaphore(f"dsem_{q}{i}")) for i in range(n)] for q, n in nd.items()}

    def add(self, eng, fn, r=(), w=(), dma=False):
        import os
        mx = int(os.environ.get("MAXOPS", "0"))
        if mx and len(self.ops) >= mx:
            return
        self.ops.append(Op(eng, fn, _pages(r), _pages(w), dma, self.label))

    def finalize(self):
        ops = self.ops
        last_w = {}
        readers = {}
        rr = {q: 0 for q in self.dsems}
        sem_last = {}
        sem_cnt = {}
        for i, op in enumerate(ops):
            deps = {}
            for k in op.r:
                j = last_w.get(k)
                if j is not None:
                    deps[j] = "raw"
                if k >= PSUM_BASE // PAGE:
                    rs = readers.get(k)
                    if rs:
                        for j in rs:
                            if ops[j].eng != op.eng and j not in deps:
                                deps[j] = "rar"
            for k in op.w:
                j = last_w.get(k)
                if j is not None and deps.get(j) != "raw":
                    deps[j] = "waw"
                rs = readers.get(k)
                if rs:
                    for j in rs:
                        if j not in deps:
                            deps[j] = "war"
            if op.dma:
                q = op.eng
                idx = rr[q]
                rr[q] = (idx + 1) % len(self.dsems[q])
                key = (q, idx)
                if key in sem_last:
                    deps[sem_last[key]] = "raw"
                sem_last[key] = i
                sem_cnt[key] = sem_cnt.get(key, 0) + 1
                op.sem = self.dsems[q][idx]
                op.semval = 16 * sem_cnt[key]
            best = {}
            for j, kind in deps.items():
                if j == i:
                    continue
                pj = ops[j]
                if pj.dma:
                    op.deps.append(j)
                    continue
                if pj.eng == op.eng:
                    if op.eng == "pe":
                        continue
                    if not SAME_ENGINE_SYNC:
                        continue
                if pj.eng not in best or best[pj.eng] < j:
                    best[pj.eng] = j
            for e, j in best.items():
                op.deps.append(j)
                ops[j].inc = True
            for k in op.r:
                rs = readers.get(k)
                if rs is None:
                    readers[k] = [i]
                else:
                    eng = op.eng
                    rs[:] = [j for j in rs if ops[j].eng != eng or ops[j].dma]
                    rs.append(i)
            for k in op.w:
                last_w[k] = i
                readers[k] = []
        run = {e: 0 for e in self.esem}
        for op in ops:
            if op.dma:
                op.tok = (op.sem, op.semval)
            elif op.inc:
                run[op.eng] += 1
                op.tok = (self.esem[op.eng], run[op.eng])
        self.final_counts = run
        self.dma_final = {}
        for op in ops:
            if op.dma:
                self.dma_final[op.sem.name] = (op.sem, op.semval)

    def emit(self):
        nc = self.nc
        ops = self.ops
        per = {e: [] for e in self.engs}
        for op in ops:
            per[op.eng].append(op)
        esem = self.esem

        def run_engine(ename, eng):
            seen = {}
            for op in per[ename]:
                for j in op.deps:
                    sem, val = ops[j].tok
                    if seen.get(sem.name, 0) >= val:
                        continue
                    seen[sem.name] = val
                    eng.wait_ge(sem, val)
                inst = op.fn(eng)
                if self.annotate and op.label:
                    inst.annotate(op.label)
                if op.dma:
                    inst.then_inc(op.sem, 16)
                elif op.inc:
                    inst.then_inc(esem[ename], 1)
            if ename == "sp":
                for nm, (sem, val) in self.dma_final.items():
                    if seen.get(nm, 0) < val:
                        eng.wait_ge(sem, val)
                for e, sem in esem.items():
                    if self.final_counts[e] > 0:
                        eng.wait_ge(sem, self.final_counts[e])

        with nc.Block() as block:
            @block.sync
            def _(e):
                run_engine("sp", e)

            @block.tensor
            def _(e):
                run_engine("pe", e)

            @block.scalar
            def _(e):
                run_engine("act", e)

            @block.vector
            def _(e):
                run_engine("dve", e)

            @block.gpsimd
            def _(e):
                run_engine("pool", e)


class View:
    __slots__ = ("ap", "rg")

    def __init__(self, ap, rg):
        self.ap = ap
        self.rg = rg


class Buf:
    def __init__(self, ap, base, shape, esize):
        self.t = ap
        self.base = base
        self.shape = tuple(shape)
        self.es = esize
        st = []
        s = 1
        for n in reversed(self.shape):
            st.append(s)
            s *= n
        self.strides = tuple(reversed(st))
        self.nbytes = s * esize

    def v(self, *idx, p=None):
        key = (slice(None) if p is None else p,) + idx
        ap = self.t[key]
        lo = hi = 0
        for d, (n, stv) in enumerate(zip(self.shape, self.strides)):
            if d < len(idx):
                i = idx[d]
                if isinstance(i, slice):
                    a = 0 if i.start is None else i.start
                    b = n if i.stop is None else i.stop
                    lo += a * stv
                    hi += (b - 1) * stv
                else:
                    lo += i * stv
                    hi += i * stv
            else:
                hi += (n - 1) * stv
        return View(ap, (self.base + lo * self.es, self.base + (hi + 1) * self.es))


class Arena:
    def __init__(self, nc, st, nbytes):
        self.words = nbytes // 4
        self.t = st.enter_context(nc.sbuf_tensor("arena", [128, self.words], F32))
        self.top = 0
        self.peak = 0

    def alloc(self, shape, dt):
        es = 2 if dt == BF16 else 4
        n = 1
        for s in shape:
            n *= s
        nb = (n * es + PAGE - 1) // PAGE * PAGE
        off = self.top
        self.top += nb
        self.peak = max(self.peak, self.top)
        assert self.top <= self.words * 4, f"SBUF arena overflow {self.top}"
        ap = self.t[:, off // 4:(off + nb) // 4]
        if dt == BF16:
            ap = ap.bitcast(BF16)
        ap = ap[:, 0:n]
        if len(shape) == 2:
            ap = ap.rearrange("p (a b) -> p a b", b=shape[1])
        elif len(shape) == 3:
            ap = ap.rearrange("p (a b c) -> p a b c", b=shape[1], c=shape[2])
        elif len(shape) == 4:
            ap = ap.rearrange("p (a b c d) -> p a b c d", b=shape[1], c=shape[2], d=shape[3])
        return Buf(ap, off, shape, es)

    def at(self, off, shape, dt):
        save = self.top
        self.top = off
        b = self.alloc(shape, dt)
        self.top = save
        return b

    def mark(self):
        return self.top

    def release(self, m):
        self.top = m


class Ring:
    def __init__(self, arena, n, shape, dt):
        self.bufs = [arena.alloc(shape, dt) for _ in range(n)]
        self.i = 0

    def next(self):
        b = self.bufs[self.i]
        self.i = (self.i + 1) % len(self.bufs)
        return b


def _gla_consts(T, L):
    seg = np.arange(T) // L
    s = np.arange(T)[:, None]
    t = np.arange(T)[None, :]
    same = seg[:, None] == seg[None, :]
    tri_incl = np.where(same & (s <= t), -1.0 / 16, 0.0)
    tri_rev = np.where(same & (s > t), -1.0 / 16, 0.0)
    mask = np.where(same & (s <= t), 1.0, 0.0)
    nseg = T // L
    ind = np.zeros((T, nseg))
    ind[np.arange(T), seg] = -1.0 / 16
    rowm = np.zeros((T, nseg))
    rowm[np.arange(T), seg] = 1.0
    segs = [np.diag((seg == g).astype(np.float64)) for g in range(nseg)]
    return tri_incl, tri_rev, mask, ind, rowm, segs


class ConstPack:
    def __init__(self):
        self.cols = []
        self.off = {}
        self.n = 0

    def add(self, name, arr):
        arr = np.asarray(arr, np.float32)
        a = np.zeros((128, arr.shape[1]), np.float32)
        a[:arr.shape[0]] = arr
        self.off[name] = (self.n, arr.shape[1])
        self.cols.append(a)
        self.n += arr.shape[1]

    def pack(self):
        return np.ascontiguousarray(np.concatenate(self.cols, axis=1))


def rope_tables(pos):
    half = 32
    inv = 1.0 / (10000.0 ** (np.arange(half, dtype=np.float64) / half))
    ang = pos.astype(np.float64)[:, None] * inv[None, :]
    return np.cos(ang).astype(np.float32), np.sin(ang).astype(np.float32)


def build_consts(p):
    cf = ConstPack()
    cb = ConstPack()
    for tag, T, L in (("p", 128, 64), ("s", 64, 16)):
        tri_incl, tri_rev, mask, ind, rowm, segs = _gla_consts(T, L)
        cf.add("tri_incl_" + tag, tri_incl)
        cf.add("tri_rev_" + tag, tri_rev)
        cf.add("mask_" + tag, mask)
        cf.add("ind_" + tag, ind)
        cf.add("rowm_" + tag, rowm)
        for g, sg in enumerate(segs):
            cb.add(f"seg_{tag}{g}", sg)
    cf.add("ones_f", np.ones((128, 128)))
    cf.add("selw", np.tile(np.array([[1.0 - p, float(p)]]), (128, 1)))
    cb.add("ident", np.eye(128))
    cb.add("ones", np.ones((128, 128)))
    cb.add("sel0", np.eye(128) * (1.0 if p == 0 else 0.0))
    cb.add("sel1", np.eye(128) * (1.0 if p == 1 else 0.0))
    k = np.arange(128)[:, None]
    q = np.arange(128)[None, :]
    diag = ((k // 64) <= (q // 64)).astype(np.float32)
    lo = diag if p == 0 else np.ones((128, 128), np.float32)
    hi = np.zeros((128, 128), np.float32) if p == 0 else diag
    cb.add("mask_lo4", np.tile(lo, (1, 4)))
    cb.add("mask_hi4", np.tile(hi, (1, 4)))
    for s in range(4):
        m = np.zeros((64, 128), np.float32)
        m[16 * s:16 * s + 16, :] = 1.0
        cb.add(f"smask{s}", m)
    posn = np.concatenate([np.arange(2048), 4096 + np.tile(np.arange(16), 4)])
    posn = np.concatenate([posn, np.zeros(17 * 128 - posn.size, np.int64)])
    c, s_ = rope_tables(posn)
    tk = np.concatenate([c, c, s_, s_], axis=1).reshape(17, 128, 128).transpose(1, 0, 2).reshape(128, 17 * 128)
    cf.add("ropek", tk)
    own_pos = np.concatenate([np.concatenate([np.arange(128) + (2 * i + p) * 128 for i in range(8)]),
                              4096 + np.tile(np.arange(16), 4)])
    c, s_ = rope_tables(own_pos)
    cf.add("cosq", np.concatenate([c, c], axis=1).T)
    cf.add("sinq", np.concatenate([s_, s_], axis=1).T)
    return cf, cb


def build_program(cf_off, cf_n, cb_off, cb_n, debug=None, stop_after=None):
    nc = bass.Bass("TRN2", target_bir_lowering=False)

    skip_in = set()
    if stop_after in ("A1", "A2"):
        skip_in = {"c_ckv", "c_kr", "w_out", "w_gate", "w_up", "w_down", "w_pg", "w_ple", "w_uq", "w_uk", "w_uv", "p_own"}
    if stop_after in ("B", "C"):
        skip_in = {"w_out", "w_gate", "w_up", "w_down", "w_pg", "w_ple", "p_own"}
    declared = []

    def din(name, shape):
        if name in skip_in:
            return None
        declared.append(name)
        return nc.dram_tensor(name, list(shape), F32, kind="ExternalInput").ap()

    def dout(name, shape):
        return nc.dram_tensor(name, list(shape), F32, kind="ExternalOutput").ap()

    x_nat = din("x_nat", [NNAT, D])
    x_own = din("x_own", [NOWN, D])
    p_own = din("p_own", [NOWN, 256])
    c_ckv = din("c_ckv", [4, 4096, 512])
    c_kr = din("c_kr", [4, 4096, 64])
    st_in = din("st_in", [4, 4, 128, 256])
    constf = din("constf", [128, cf_n])
    constb = din("constb", [128, cb_n])
    gvec = din("gvec", [128, 38])
    g_kv_b = din("g_kv_b", [128, 512])
    g_pm_b = din("g_pm_b", [128, D])
    g_pf_b = din("g_pf_b", [128, D])
    w_in = din("w_in", [D, 4176])
    w_uq = din("w_uq", [512, 1536])
    w_uk = din("w_uk", [512, 1024])
    w_uv = din("w_uv", [512, 1024])
    w_ga = din("w_ga", [17, 512])
    w_out = din("w_out", [D, D])
    w_gate = din("w_gate", [D, DFF])
    w_up = din("w_up", [D, DFF])
    w_down = din("w_down", [DFF, D])
    w_ple = din("w_ple", [256, D])
    w_pg = din("w_pg", [D, D])

    y_out = dout("y_out", [NOWN, D])
    ckv_out = dout("ckv_out", [NNAT, 512])
    kr_out = dout("kr_out", [NNAT, 64])
    gp_out = dout("gp_out", [4, 128, 256])
    gs_out = dout("gs_out", [4, 4, 128, 256])
    dbg_outs = {}
    if debug:
        for nm, (shp, dtn) in debug.items():
            dbg_outs[nm] = nc.dram_tensor("dbg_" + nm, list(shp), BF16 if dtn == "bf16" else F32, kind="ExternalOutput").ap()

    with ExitStack() as st:
        S = Sched(nc, st)
        A = Arena(nc, st, 204 * 1024)
        banks_t = [st.enter_context(nc.psum_tensor(f"ps{i}", [128, 512], F32)) for i in range(8)]
        PS = [Buf(banks_t[i][:, :], PSUM_BASE + i * 2048, (512,), 4) for i in range(8)]
        PSB = [Buf(banks_t[i][:, :].bitcast(BF16), PSUM_BASE + i * 2048, (1024,), 2) for i in range(8)]

        def dma(q, out_ap, in_ap, r=(), w=()):
            S.add(q, lambda e: e.dma_start(out=out_ap, in_=in_ap), r=r, w=w, dma=True)

        def load(q, dst, src_ap):
            dma(q, dst.ap, src_ap, w=[dst])

        store_q = ["pool"]

        def store(dst_ap, src):
            dma(store_q[0], dst_ap, src.ap, r=[src])

        def mm(out, lhsT, rhs, start, stop):
            S.add("pe", lambda e: e.matmul(out.ap, lhsT=lhsT.ap, rhs=rhs.ap, start=start, stop=stop,
                                           skip_group_check=True),
                  r=[lhsT, rhs], w=[out])

        def tr(out, in_, ident):
            S.add("pe", lambda e: e.transpose(out=out.ap, in_=in_.ap, identity=ident.ap), r=[in_, ident], w=[out])

        def act(out, in_, func, scale=None, bias=None, accum=None, eng="act"):
            kw = {}
            if scale is not None:
                kw["scale"] = scale.ap if isinstance(scale, View) else scale
            if bias is not None:
                kw["bias"] = bias.ap if isinstance(bias, View) else bias
            r = [in_] + [x for x in (scale, bias) if isinstance(x, View)]
            w = [out]
            if accum is not None:
                kw["accum_out"] = accum.ap
                w.append(accum)
            S.add("act", lambda e: e.activation(out=out.ap, in_=in_.ap, func=func, **kw), r=r, w=w)

        def tt(eng, out, in0, in1, op):
            S.add(eng, lambda e: e.tensor_tensor(out=out.ap, in0=in0.ap, in1=in1.ap, op=op), r=[in0, in1], w=[out])

        def ts(eng, out, in0, s1, op0, s2=None, op1=None):
            r = [in0] + [x for x in (s1, s2) if isinstance(x, View)]
            a1 = s1.ap if isinstance(s1, View) else s1
            a2 = s2.ap if isinstance(s2, View) else s2
            if op1 is None:
                S.add(eng, lambda e: e.tensor_scalar(out=out.ap, in0=in0.ap, scalar1=a1, scalar2=None, op0=op0), r=r, w=[out])
            else:
                S.add(eng, lambda e: e.tensor_scalar(out=out.ap, in0=in0.ap, scalar1=a1, scalar2=a2, op0=op0, op1=op1), r=r, w=[out])

        def stt(out, in0, scalar, in1, op0, op1):
            r = [in0, in1] + ([scalar] if isinstance(scalar, View) else [])
            sc = scalar.ap if isinstance(scalar, View) else scalar
            S.add("dve", lambda e: e.scalar_tensor_tensor(out=out.ap, in0=in0.ap, scalar=sc, in1=in1.ap, op0=op0, op1=op1),
                  r=r, w=[out])

        def cp(eng, out, in_):
            if eng == "act":
                S.add("act", lambda e: e.copy(out=out.ap, in_=in_.ap), r=[in_], w=[out])
            else:
                S.add(eng, lambda e: e.tensor_copy(out=out.ap, in_=in_.ap), r=[in_], w=[out])

        def recip(out, in_):
            S.add("dve", lambda e: e.reciprocal(out=out.ap, in_=in_.ap), r=[in_], w=[out])

        def rstd_act(out, in_, inv_n):
            act(out, in_, AF.Ln, scale=inv_n, bias=EPS)
            act(out, out, AF.Exp, scale=-0.5)

        def memset(eng, out, val):
            S.add(eng, lambda e: e.memset(out.ap, val), w=[out])

        def bview(view, shape_fn):
            return View(shape_fn(view.ap), view.rg)

        def dbg(name, view):
            if debug and name in dbg_outs:
                dma("sp", dbg_outs[name], view.ap, r=[view])

        n_core = cf_off["ropek"][0]
        CF = A.alloc((n_core,), F32)
        CB = A.alloc((cb_n,), BF16)
        GV = A.alloc((38,), F32)
        junk = A.alloc((2048,), BF16)
        mixT = A.alloc((16, NOWN), BF16)
        m0 = A.mark()
        m_tail = m0
        CBf = A.alloc((cb_n,), F32)
        load("sp", CF.v(), constf[:, 0:n_core])
        load("sp", CBf.v(), constb)
        load("sp", GV.v(), gvec)
        cp("dve", CB.v(), CBf.v())
        A.release(m0)

        def cfv(name, rows=128, c0=0, c1=None):
            o, n = cf_off[name]
            c1 = n if c1 is None else c1
            return CF.v(slice(o + c0, o + c1), p=slice(0, rows))

        def cbv(name, rows=128, c0=0, c1=None):
            o, n = cb_off[name]
            c1 = n if c1 is None else c1
            return CB.v(slice(o + c0, o + c1), p=slice(0, rows))

        ident = lambda T: cbv("ident", T, 0, T)
        w_in_k = w_in.rearrange("(k p) n -> p k n", p=128)
        nat_blocks = [(j, j * 128, 128) for j in range(16)] + [(16, 2048, 64)]

        import os as _os
        S.annotate = bool(_os.environ.get("ANNOTATE"))

        def LB(name):
            S.label = name

        def finish():
            S.finalize()
            S.emit()
            nc._declared = declared
            return nc, A.peak

        def norm_transpose(xs, T, gcols, dst_fn, psb_a, psb_b, ss, xn, nfeat=D, defer=None, sq_junk=None):
            pT = slice(0, T)
            nch = nfeat // 128
            jk = sq_junk if sq_junk is not None else junk
            do = (lambda f: f()) if defer is None else defer.append
            do(lambda: act(jk.v(slice(0, nfeat), p=pT), xs.v(slice(0, nfeat), p=pT), AF.Square, accum=ss.v(slice(0, 1), p=pT)))
            do(lambda: rstd_act(ss.v(slice(1, 2), p=pT), ss.v(slice(0, 1), p=pT), 1.0 / nfeat))
            do(lambda: ts("dve", xn.v(slice(0, nfeat), p=pT), xs.v(slice(0, nfeat), p=pT), ss.v(slice(1, 2), p=pT), ALU.mult))
            for c4 in range(nch // 4):
                pb = (psb_a, psb_b)[c4 % 2]

                def grp(pb=pb, c4=c4):
                    for jj in range(4):
                        c = c4 * 4 + jj
                        tr(pb.v(slice(jj * 128, jj * 128 + T)), xn.v(slice(c * 128, (c + 1) * 128), p=pT), ident(T))
                    src = bview(pb.v(slice(0, 512)), lambda ap: ap.rearrange("p (a b) -> p a b", b=128)[:, :, 0:T])
                    g = bview(GV.v(slice(gcols + c4 * 4, gcols + c4 * 4 + 4)),
                              lambda ap: ap.unsqueeze(2).broadcast_to([128, 4, T]))
                    tt("dve", dst_fn(c4 * 4, c4 * 4 + 4), src, g, ALU.mult)
                do(grp)

        mA2 = A.mark()
        Wn2 = A.alloc((16, 2064), BF16)
        for kg in range(4):
            load("pool", Wn2.v(slice(kg * 4, kg * 4 + 4)), w_in_k[:, kg * 4:(kg + 1) * 4, 1088:3152])
        WGA = A.alloc((512,), BF16)
        load("pool", WGA.v(p=slice(0, 17)), w_ga)
        glrT = A.alloc((128,), BF16)
        memset("dve", glrT.v(p=slice(0, 32)), 1.0)
        xs_ring = Ring(A, 3, (2048,), F32)
        xn1 = A.alloc((2048,), BF16)
        junk2 = A.alloc((2048,), BF16)
        aT_ring = Ring(A, 3, (16, 128), BF16)
        ss_ring = Ring(A, 6, (16,), F32)
        la = A.alloc((512,), F32)
        E_ring = Ring(A, 2, (512,), F32)
        qt = A.alloc((512,), BF16)
        kt = A.alloc((512,), BF16)
        kh = A.alloc((512,), BF16)
        khm = A.alloc((4, 512), BF16)
        vb = A.alloc((1024,), BF16)
        TT = A.alloc((4, 512), BF16)
        at = A.alloc((4, 128), BF16)
        on2 = [A.alloc((4, 256), BF16) for _ in range(2)]
        sel_q = []
        eb = A.alloc((16,), F32)
        ssg = A.alloc((8,), F32)
        Sst = A.alloc((4, 256), F32)
        Sbf = [A.alloc((4, 256), BF16) for _ in range(2)]
        Ssm = [A.at(Wn2.base + g * 4096, (4, 256), F32) for g in range(4)]
        Ssmb = [A.at(Wn2.base + 16384 + g * 2048, (4, 256), BF16) for g in range(4)]
        memset("dve", Sst.v(), 0.0)
        memset("dve", Sbf[0].v(), 0.0)
        sbf_cur = [0]
        DKS = 128 ** -0.5

        norm_q = []

        def a2_norm(blk, defer=True):
            (j, tok0, T) = blk
            LB("A2.norm")
            pT = slice(0, T)
            xs = xs_ring.next()
            aT = aT_ring.next()
            ss = ss_ring.next()
            load("sp", xs.v(p=pT), x_nat[tok0:tok0 + T, :])
            norm_transpose(xs, T, 0, lambda c0, c1: aT.v(slice(c0, c1), slice(0, T)), PSB[5], PSB[5], ss, xn1,
                           defer=(norm_q if defer else None), sq_junk=junk2)
            return aT

        def nfiller(n):
            if not norm_q:
                return
            saved = S.label
            LB("A2.norm")
            for _ in range(min(n, len(norm_q))):
                norm_q.pop(0)()
            S.label = saved

        def a2_proj_pieces(blk, aT):
            (j, tok0, T) = blk
            pT = slice(0, T)
            pieces = []
            for k in range(16):
                for b in range(4):
                    def piece(k=k, b=b):
                        mm(PS[b].v(p=pT), aT.v(k, slice(0, T)), Wn2.v(k, slice(b * 512, (b + 1) * 512)), k == 0, k == 15)
                    pieces.append(piece)
            return pieces

        fill_q = []
        sel_hold = [0]

        def filler(n):
            nfiller(1 if n < 100 else 1000)
            if sel_q and sel_hold[0] <= 0:
                sel_q.pop(0)()
            sel_hold[0] -= 1
            if not fill_q:
                return
            saved = S.label
            LB("A2.proj")
            for _ in range(min(n, len(fill_q))):
                fill_q.pop(0)()
            S.label = saved

        def a2_decay(blk, aT):
            (j, tok0, T) = blk
            LB("A2.decay")
            pT = slice(0, T)
            prompt = j < 16
            tag = "p" if prompt else "s"
            nseg = 2 if prompt else 4
            Pq, Pk, Pv0, Pv1, Pg, Px = PS[0], PS[1], PS[2], PS[3], PS[4], PS[5]
            for k in range(16):
                mm(Pg.v(slice(0, T), p=slice(0, 16)), Wn2.v(k, slice(2048, 2064)), aT.v(k, slice(0, T)), k == 0, k == 15)
            cp("act", glrT.v(slice(0, T), p=slice(0, 16)), Pg.v(slice(0, T), p=slice(0, 16)))
            mm(Px.v(p=pT), glrT.v(slice(0, T), p=slice(0, 17)), WGA.v(p=slice(0, 17)), True, True)
            act(la.v(p=pT), Px.v(p=pT), AF.Exp, scale=-1.0)
            act(la.v(p=pT), la.v(p=pT), AF.Ln, bias=1.0)
            cp("act", vb.v(slice(0, 512), p=pT), Pv0.v(p=pT))
            cp("dve", vb.v(slice(512, 1024), p=pT), Pv1.v(p=pT))
            Pb, Prb, Pbl = PS[4], PS[5], PS[4]
            mm(Pb.v(p=pT), cfv("tri_incl_" + tag, T, 0, T), la.v(p=pT), True, True)
            mm(Prb.v(p=pT), cfv("tri_rev_" + tag, T, 0, T), la.v(p=pT), True, True)
            E1 = E_ring.next()
            act(E1.v(p=pT), Pb.v(p=pT), AF.Exp)
            stt(qt.v(p=pT), Pq.v(p=pT), DKS, E1.v(p=pT), ALU.mult, ALU.mult)
            E2 = E_ring.next()
            act(E2.v(p=pT), Pb.v(p=pT), AF.Exp, scale=-1.0)
            tt("dve", kt.v(p=pT), Pk.v(p=pT), E2.v(p=pT), ALU.mult)
            E3 = E_ring.next()
            act(E3.v(p=pT), Prb.v(p=pT), AF.Exp)
            tt("dve", kh.v(p=pT), Pk.v(p=pT), E3.v(p=pT), ALU.mult)
            for h in range(4):
                mm(Pbl.v(slice(h * nseg, (h + 1) * nseg)), la.v(slice(h * 128, (h + 1) * 128), p=pT),
                   cfv("ind_" + tag, T), h == 0, h == 3)
            act(eb.v(slice(0, 4 * nseg)), Pbl.v(slice(0, 4 * nseg)), AF.Exp)
            for g in range(nseg):
                ts("dve", khm.v(g, p=pT), kh.v(p=pT), cfv("rowm_" + tag, T, g, g + 1), ALU.mult)

        def a2_rest(blk):
            (j, tok0, T) = blk
            on = on2[j % 2]
            sel_hold[0] = 5
            pT = slice(0, T)
            prompt = j < 16
            tag = "p" if prompt else "s"
            nseg = 2 if prompt else 4
            LB("A2.trans")
            W = (2 + nseg) * T
            for h in range(4):
                PSt = PS[4 + h % 2]
                hs = slice(h * 128, (h + 1) * 128)
                mm(PSt.v(slice(0, T)), kt.v(hs, p=pT), ident(T), True, False)
                mm(PSt.v(slice(T, 2 * T)), qt.v(hs, p=pT), ident(T), False, False)
                for g in range(nseg):
                    mm(PSt.v(slice((2 + g) * T, (3 + g) * T)), qt.v(hs, p=pT), cbv(f"seg_{tag}{g}", T, 0, T),
                       False, g == nseg - 1)
                cp("act" if h % 2 == 0 else "dve", TT.v(h, slice(0, W)), PSt.v(slice(0, W)))
                filler(4)
            PSa = PS[4]
            for h in range(4):
                mm(PSa.v(slice(h * T, (h + 1) * T), p=pT), TT.v(h, slice(0, T)), TT.v(h, slice(T, 2 * T)), h == 0, h == 3)
            msk = bview(cfv("mask_" + tag, T, 0, T), lambda ap: ap.unsqueeze(1).broadcast_to([T, 4, T]))
            tt("dve", at.v(slice(None), slice(0, T), p=pT),
               bview(PSa.v(slice(0, 4 * T), p=pT), lambda ap: ap.rearrange("p (a b) -> p a b", b=T)), msk, ALU.mult)
            filler(4)
            LB("A2.state")
            if not prompt:
                for g in range(4):
                    load("sp", Ssm[g].v(), st_in[g].rearrange("h d v -> d h v"))
                    cp("act", Ssmb[g].v(), Ssm[g].v())
            Po = PS[4]
            PSu = PS[5]
            for hp in range(2):
                LB("A2.state")
                for h in (2 * hp, 2 * hp + 1):
                    out = Po.v(slice((h % 2) * 256, (h % 2 + 1) * 256), p=pT)
                    vh = vb.v(slice(h * 256, (h + 1) * 256), p=pT)
                    mm(out, at.v(h, slice(0, T), p=pT), vh, h % 2 == 0, False)
                for g in range(nseg):
                    for h in (2 * hp, 2 * hp + 1):
                        out = Po.v(slice((h % 2) * 256, (h % 2 + 1) * 256), p=pT)
                        vh = vb.v(slice(h * 256, (h + 1) * 256), p=pT)
                        if prompt:
                            S_f, S_b = Sst, Sbf[(sbf_cur[0] + g) % 2]
                            S_bn = Sbf[(sbf_cur[0] + g + 1) % 2]
                        else:
                            S_f, S_b, S_bn = Ssm[g], Ssmb[g], None
                        mm(out, TT.v(h, slice((2 + g) * T, (3 + g) * T)), S_b.v(h), False, g == nseg - 1)
                        PSu = PS[5] if h % 2 == 0 else PS[7]
                        mm(PSu.v(slice(0, 256)), khm.v(g, slice(h * 128, (h + 1) * 128), p=pT), vh, True, True)
                        stt(S_f.v(h), S_f.v(h), eb.v(slice(h * nseg + g, h * nseg + g + 1)), PSu.v(slice(0, 256)),
                            ALU.mult, ALU.add)
                        if S_bn is not None:
                            cp("act", S_bn.v(h), S_f.v(h))
                        filler(5 if prompt else 3)
                LB("A2.onorm")
                for h in (2 * hp, 2 * hp + 1):
                    act(junk.v(slice((h % 2) * 256, (h % 2 + 1) * 256), p=pT),
                        Po.v(slice((h % 2) * 256, (h % 2 + 1) * 256), p=pT), AF.Square, accum=ssg.v(slice(h, h + 1), p=pT))
                hsl2 = slice(2 * hp, 2 * hp + 2)
                hsl3 = slice(4 + 2 * hp, 6 + 2 * hp)
                rstd_act(ssg.v(hsl3, p=pT), ssg.v(hsl2, p=pT), 1.0 / 256)
                for h in (2 * hp, 2 * hp + 1):
                    ts("dve", on.v(h, p=pT), Po.v(slice((h % 2) * 256, (h % 2 + 1) * 256), p=pT),
                       ssg.v(slice(4 + h, 5 + h), p=pT), ALU.mult)
                filler(4)
            if prompt:
                sbf_cur[0] = (sbf_cur[0] + nseg) % 2
            else:
                for g in range(4):
                    store(gs_out[g].rearrange("h d v -> d h v"), Ssm[g].v())
            o0 = (j // 2) * 128 if prompt else 1024
            Ts = T
            for half in range(2):
                def sel_piece(half=half):
                    saved = S.label
                    LB("A2.sel")
                    pb = PSB[6]
                    for ff in range(4):
                        f = half * 4 + ff
                        tr(pb.v(slice(ff * 128, ff * 128 + Ts)), on.v(f // 2, slice((f % 2) * 128, (f % 2 + 1) * 128), p=pT), ident(T))
                    src = bview(pb.v(slice(0, 512)), lambda ap: ap.rearrange("p (a b) -> p a b", b=128)[:, :, 0:Ts])
                    dst = mixT.v(slice(8 + half * 4, 12 + half * 4), slice(o0, o0 + Ts))
                    if not prompt:
                        cp("dve", dst, src)
                    elif j % 2 == 0:
                        ts("dve", dst, src, cfv("selw", 128, 0, 1), ALU.mult)
                    else:
                        stt(dst, src, cfv("selw", 128, 1, 2), dst, ALU.mult, ALU.add)
                    S.label = saved
                sel_q.append(sel_piece)
            filler(1000)

        aTs = {0: a2_norm(nat_blocks[0], defer=False), 1: a2_norm(nat_blocks[1], defer=False)}
        fill_q.extend(a2_proj_pieces(nat_blocks[0], aTs[0]))
        filler(1000)
        for bidx, blk in enumerate(nat_blocks):
            a2_decay(blk, aTs[bidx])
            if bidx + 2 < len(nat_blocks):
                aTs[bidx + 2] = a2_norm(nat_blocks[bidx + 2])
            if bidx + 1 < len(nat_blocks):
                fill_q.extend(a2_proj_pieces(nat_blocks[bidx + 1], aTs[bidx + 1]))
            a2_rest(blk)
        while sel_q:
            sel_q.pop(0)()
        hsl = Sst.v()
        store(gp_out.rearrange("h d v -> d h v"), hsl)
        dbg("mixT", mixT.v(slice(8, 16)))
        A.release(mA2)
        if stop_after == "A2":
            return finish()

        LB("B")
        q_nopeT = A.alloc((8, NOWN), BF16)
        q_ropeT = A.alloc((8, NOWN), BF16)
        mB = A.mark()
        import os
        lm = (lambda name: print("LANDMARK", name, len(S.ops))) if os.environ.get("LANDMARKS") else (lambda name: None)
        lm("B start")
        ROPQ = A.alloc((2, NOWN), F32)
        cq_o = cf_off["cosq"][0]
        sq_o = cf_off["sinq"][0]
        load("sp", ROPQ.v(0), constf[:, cq_o:cq_o + NOWN])
        load("sp", ROPQ.v(1), constf[:, sq_o:sq_o + NOWN])
        WUQ = A.alloc((4, 2048), BF16)
        load("pool", WUQ.v(slice(None), slice(0, 1536)), w_uq.rearrange("(k p) n -> p k n", p=128))
        w3 = lambda a, b: bview(WUQ.v(slice(None), slice(0, 1536)),
                                lambda ap: ap.rearrange("p k (h e) -> p k h e", e=192)[:, :, :, a:b])
        r3 = lambda a, b: bview(WUQ.v(slice(None), slice(1536, 2048)),
                                lambda ap: ap.rearrange("p k (h e) -> p k h e", e=64)[:, :, :, a:b])
        _o, _i = r3(0, 32), w3(160, 192)
        S.add("act", lambda e: e.mul(out=_o.ap, in_=_i.ap, mul=-1.0), r=[_i], w=[_o])
        cp("dve", r3(32, 64), w3(128, 160))
        c_qnT = A.alloc((4, NOWN), BF16)
        aTg2 = [A.at(mixT.base, (16, 256), BF16), A.at(mixT.base + 16 * 256 * 2, (16, 256), BF16)]
        cqf = A.alloc((4, 256), F32)
        sqb = A.alloc((4, 256), BF16)
        rsb = A.alloc((256,), F32)
        sgb_ring = Ring(A, 2, (256,), F32)
        WOWN = A.alloc((16, 1536), BF16)
        load("pool", WOWN.v(slice(None), slice(0, 512)), w_in_k[:, :, 0:512])
        load("pool", WOWN.v(slice(None), slice(512, 1024)), w_in_k[:, :, 3152:3664])
        load("pool", WOWN.v(slice(None), slice(1024, 1536)), w_in_k[:, :, 3664:4176])
        xs_ring = Ring(A, 2, (2048,), F32)
        xnB = A.alloc((2048,), BF16)
        ss_ring = Ring(A, 8, (16,), F32)
        t_ring = Ring(A, 2, (256,), F32)
        pbank = [0]

        def next_bank():
            b = PS[pbank[0] % 4]
            pbank[0] += 1
            return b

        b_groups = [(0, 256), (256, 256), (512, 256), (768, 256), (1024, 64)]
        b_q = []

        def b_norm(gi, defer=True):
            (t0, N) = b_groups[gi]
            aTg_ = aTg2[gi % 2]
            nb = (N + 127) // 128
            for bi in range(nb):
                T = min(128, N - bi * 128)
                xs = xs_ring.next()
                ss = ss_ring.next()

                def ld(xs=xs, T=T, bi=bi):
                    load("sp", xs.v(p=slice(0, T)), x_own[t0 + bi * 128:t0 + bi * 128 + T, :])
                if defer:
                    b_q.append(ld)
                else:
                    ld()
                norm_transpose(xs, T, 0, lambda c0, c1, bi=bi, T=T: aTg_.v(slice(c0, c1), slice(bi * 128, bi * 128 + T)),
                               PSB[6], PSB[7], ss, xnB, defer=(b_q if defer else None))

        def b_fill(n):
            for _ in range(min(n, len(b_q))):
                b_q.pop(0)()

        b_norm(0, False)
        for gi, (t0, N) in enumerate(b_groups):
            aTg = aTg2[gi % 2]
            if gi + 1 < len(b_groups):
                b_norm(gi + 1)
            tk = slice(t0, t0 + N)
            nn = slice(0, N)
            lm("B norm done")
            for m in range(4):
                P = next_bank()
                for k in range(16):
                    mm(P.v(nn), WOWN.v(k, slice(m * 128, (m + 1) * 128)), aTg.v(k, nn), k == 0, k == 15)
                cp("dve", cqf.v(m, nn), P.v(nn))
                act(sqb.v(m, nn), P.v(nn), AF.Square)
                b_fill(1)
            lm("B cq mm done")
            Pss = PS[4]
            for m in range(4):
                mm(Pss.v(nn), cbv("ones"), sqb.v(m, nn), m == 0, m == 3)
            rstd_act(rsb.v(nn), Pss.v(nn), 1.0 / 512)
            for m in range(4):
                stt(c_qnT.v(m, tk), cqf.v(m, nn), GV.v(slice(32 + m, 33 + m)), rsb.v(nn), ALU.mult, ALU.mult)
            lm("B cqn done")
            for f in range(8):
                P = next_bank()
                for k in range(16):
                    mm(P.v(nn), WOWN.v(k, slice(512 + f * 128, 512 + (f + 1) * 128)), aTg.v(k, nn), k == 0, k == 15)
                sg = sgb_ring.next()
                act(sg.v(nn), P.v(nn), AF.Silu)
                stt(mixT.v(8 + f, tk), sg.v(nn), GV.v(slice(36 + f % 2, 37 + f % 2)), mixT.v(8 + f, tk), ALU.mult, ALU.mult)
                b_fill(2)
            lm("B rg done")
            for h in range(8):
                P = next_bank()
                for c in range(4):
                    mm(P.v(nn), WUQ.v(c, slice(h * 192, h * 192 + 128)), c_qnT.v(c, tk), c == 0, c == 3)
                cp("act", q_nopeT.v(h, tk), P.v(nn))
                Px, Pxr = (PS[5], PS[4]) if h % 2 == 0 else (PS[6], PS[7])
                h64 = slice(0, 64)
                for c in range(4):
                    mm(Px.v(nn, p=h64), WUQ.v(c, slice(h * 192 + 128, h * 192 + 192)), c_qnT.v(c, tk), c == 0, c == 3)
                for c in range(4):
                    mm(Pxr.v(nn, p=h64), WUQ.v(c, slice(1536 + h * 64, 1536 + (h + 1) * 64)), c_qnT.v(c, tk), c == 0, c == 3)
                ta = t_ring.next()
                tb = t_ring.next()
                tt("dve", ta.v(nn, p=h64), Px.v(nn, p=h64), ROPQ.v(0, tk, p=h64), ALU.mult)
                tt("dve", tb.v(nn, p=h64), Pxr.v(nn, p=h64), ROPQ.v(1, tk, p=h64), ALU.mult)
                tt("dve", q_ropeT.v(h, tk, p=h64), ta.v(nn, p=h64), tb.v(nn, p=h64), ALU.add)
                b_fill(1)
            b_fill(1000)
        lm("B end")
        dbg("qn", q_nopeT.v())
        dbg("qr", q_ropeT.v(p=slice(0, 64)))
        A.release(mB)
        if stop_after == "B":
            return finish()

        LB("A1")
        ckv_tok = A.alloc((17, 576), BF16)
        ckvT = A.alloc((5, NNAT), BF16)
        mA1 = A.mark()
        Wn1 = A.alloc((16, 640), BF16)
        load("pool", Wn1.v(slice(None), slice(0, 576)), w_in_k[:, :, 512:1088])
        S.add("act", lambda e: e.mul(out=Wn1.v(slice(None), slice(576, 608)).ap,
                                     in_=Wn1.v(slice(None), slice(544, 576)).ap, mul=-1.0),
              r=[Wn1.v(slice(None), slice(544, 576))], w=[Wn1.v(slice(None), slice(576, 608))])
        cp("dve", Wn1.v(slice(None), slice(608, 640)), Wn1.v(slice(None), slice(512, 544)))
        GKV = A.alloc((512,), F32)
        load("sp", GKV.v(), g_kv_b)
        ROPK = A.alloc((17 * 128,), F32)
        ropek_o = cf_off["ropek"][0]
        load("sp", ROPK.v(), constf[:, ropek_o:ropek_o + 17 * 128])
        xs_ring = Ring(A, 3, (2048,), F32)
        xn_ring = Ring(A, 2, (2048,), BF16)
        aT_ring = Ring(A, 3, (16, 128), BF16)
        ss_ring = Ring(A, 8, (16,), F32)
        ckvf_ring = Ring(A, 2, (576,), F32)
        tmp_ring = Ring(A, 2, (128,), F32)

        a1_q = []

        def a1_stage0(blk, defer=True):
            (j, tok0, T) = blk
            pT = slice(0, T)
            xs = xs_ring.next()
            xn = xn_ring.next()
            aT = aT_ring.next()
            ss = ss_ring.next()
            load("sp", xs.v(p=pT), x_nat[tok0:tok0 + T, :])
            norm_transpose(xs, T, 0, lambda c0, c1: aT.v(slice(c0, c1), slice(0, T)), PSB[4], PSB[5], ss, xn,
                           defer=(a1_q if defer else None))
            return aT

        def a1_fill(n):
            for _ in range(min(n, len(a1_q))):
                a1_q.pop(0)()

        a1T = {0: a1_stage0(nat_blocks[0], False), 1: a1_stage0(nat_blocks[1], False)}
        for bidx, (j, tok0, T) in enumerate(nat_blocks):
            pT = slice(0, T)
            aT = a1T[bidx]
            if bidx + 2 < len(nat_blocks):
                a1T[bidx + 2] = a1_stage0(nat_blocks[bidx + 2])
            Pc = PS[j % 2]
            Pr = PS[2 + j % 2]
            for k in range(16):
                mm(Pc.v(p=pT), aT.v(k, slice(0, T)), Wn1.v(k, slice(0, 512)), k == 0, k == 15)
                if k % 4 == 3:
                    a1_fill(1)
            for k in range(16):
                mm(Pr.v(slice(0, 128), p=pT), aT.v(k, slice(0, T)), Wn1.v(k, slice(512, 640)), k == 0, k == 15)
                if k % 4 == 3:
                    a1_fill(1)
            a1_fill(100)
            ss2 = ss_ring.next()
            act(junk.v(slice(0, 512), p=pT), Pc.v(p=pT), AF.Square, accum=ss2.v(slice(0, 1), p=pT))
            rstd_act(ss2.v(slice(1, 2), p=pT), ss2.v(slice(0, 1), p=pT), 1.0 / 512)
            cf_ = ckvf_ring.next()
            stt(cf_.v(slice(0, 512), p=pT), Pc.v(p=pT), ss2.v(slice(1, 2), p=pT), GKV.v(p=pT), ALU.mult, ALU.mult)
            tmp = tmp_ring.next()
            tt("dve", tmp.v(p=pT), Pr.v(slice(0, 128), p=pT), ROPK.v(slice(j * 128, (j + 1) * 128), p=pT), ALU.mult)
            tt("dve", cf_.v(slice(512, 576), p=pT), tmp.v(slice(0, 64), p=pT), tmp.v(slice(64, 128), p=pT), ALU.add)
            store(ckv_out[tok0:tok0 + T, :], cf_.v(slice(0, 512), p=pT))
            store(kr_out[tok0:tok0 + T, :], cf_.v(slice(512, 576), p=pT))
            cp("act", ckv_tok.v(j, p=pT), cf_.v(p=pT))
            pb = PSB[6]
            pb2 = PSB[7]
            for c in range(4):
                tr(pb.v(slice(c * 128, c * 128 + T)), ckv_tok.v(j, slice(c * 128, (c + 1) * 128), p=pT), ident(T))
            tr(pb2.v(slice(0, T), p=slice(0, 64)), ckv_tok.v(j, slice(512, 576), p=pT), ident(T))
            src = bview(pb.v(slice(0, 512)), lambda ap: ap.rearrange("p (a b) -> p a b", b=128)[:, :, 0:T])
            cp("dve", ckvT.v(slice(0, 4), slice(tok0, tok0 + T)), src)
            cp("act", ckvT.v(4, slice(tok0, tok0 + T), p=slice(0, 64)), pb2.v(slice(0, T), p=slice(0, 64)))
        dbg("ckvT", ckvT.v())
        A.release(mA1)
        if stop_after == "A1":
            return finish()

        LB("C.prep")
        mC = A.mark()
        WUKT = A.alloc((8, 512), BF16)
        WUV = A.alloc((4, 1024), BF16)
        load("pool", WUV.v(), w_uv.rearrange("(k p) n -> p k n", p=128))
        q_cat = A.alloc((4, 2, 512), BF16)
        pT_ring = Ring(A, 3, (512,), BF16)
        rl = A.alloc((512,), F32)
        onb = A.alloc((4, 512), BF16)
        q_cat_s = A.alloc((4, 8, 64), BF16)
        onb_s = A.alloc((4, 128), BF16)
        mC2 = A.mark()
        WUKb = A.alloc((4, 1024), BF16)
        load("pool", WUKb.v(), w_uk.rearrange("(k p) n -> p k n", p=128))
        for h in range(8):
            pb = PSB[6 + h % 2]
            for cc in range(4):
                tr(pb.v(slice(cc * 128, (cc + 1) * 128)), WUKb.v(cc, slice(h * 128, (h + 1) * 128)), ident(128))
            cp("act" if h % 2 == 0 else "dve", WUKT.v(h), pb.v(slice(0, 512)))
        A.release(mC2)
        h64 = slice(0, 64)
        mQ = A.mark()
        q_catB = A.alloc((4, 2, 512), BF16)
        qcs = [q_cat, q_catB]

        def qlat_pieces(i, qc):
            tk = slice(i * 128, (i + 1) * 128)
            pieces = []
            for hg in range(2):
                for cc in range(4):
                    def piece(hg=hg, cc=cc):
                        saved = S.label
                        LB("C.qlat")
                        Pq = PS[7]
                        for hh in range(4):
                            h = hg * 4 + hh
                            mm(Pq.v(slice(hh * 128, (hh + 1) * 128)), WUKT.v(h, slice(cc * 128, (cc + 1) * 128)),
                               q_nopeT.v(h, tk), hh == 0, hh == 3)
                        cp("act" if cc % 2 == 0 else "dve", qc.v(cc, hg), Pq.v())
                        S.label = saved
                    pieces.append(piece)
            return pieces

        def c_scores(i, hg, kb):
            tk = slice(i * 128, (i + 1) * 128)
            qc = qcs[i % 2]
            ks = slice(kb * 128, (kb + 1) * 128)
            Ps = PS[5 + kb % 2]
            for cc in range(4):
                mm(Ps.v(), ckvT.v(cc, ks), qc.v(cc, hg), cc == 0, False)
            mm(Ps.v(), ckvT.v(4, ks, p=h64), q_ropeT.v(slice(hg * 4, hg * 4 + 4), tk, p=h64), False, True)
            pT_ = pT_ring.next()
            act(pT_.v(), Ps.v(), AF.Exp, scale=MLA_SCALE)
            if kb >= 2 * i:
                tt("dve", pT_.v(), pT_.v(), cbv("mask_lo4" if kb == 2 * i else "mask_hi4"), ALU.mult)
            return pT_

        def c_pv(i, kb, pT_):
            nkb = 2 * i + 2
            for cc in range(4):
                mm(PS[cc].v(), ckv_tok.v(kb, slice(cc * 128, (cc + 1) * 128)), pT_.v(), kb == 0, kb == nkb - 1)
            mm(PS[4].v(), cbv("ones"), pT_.v(), kb == 0, kb == nkb - 1)

        for pc in qlat_pieces(0, qcs[0]):
            pc()
        ql_q = []
        units = [(i, hg) for i in range(8) for hg in range(2)]
        pend = None
        for ui, (i, hg) in enumerate(units):
            LB("C.attn")
            tk = slice(i * 128, (i + 1) * 128)
            nkb = 2 * i + 2
            if hg == 0 and i + 1 < 8:
                ql_q.extend(qlat_pieces(i + 1, qcs[(i + 1) % 2]))
            cur = pend if pend is not None else c_scores(i, hg, 0)
            pend = None
            for kb in range(nkb):
                nxt = c_scores(i, hg, kb + 1) if kb + 1 < nkb else None
                c_pv(i, kb, cur)
                cur = nxt
                if ql_q and kb % 2 == 1:
                    ql_q.pop(0)()
            LB("C.fin")
            for cc in range(4):
                cp("act" if cc % 2 == 0 else "dve", onb.v(cc), PS[cc].v())
            recip(rl.v(), PS[4].v())
            if hg == 1:
                while ql_q:
                    ql_q.pop(0)()
            if ui + 1 < len(units):
                LB("C.attn")
                ni, nhg = units[ui + 1]
                pend = c_scores(ni, nhg, 0)
                LB("C.fin")
            Pm = PS[7]
            for hh in range(4):
                h = hg * 4 + hh
                for cc in range(4):
                    mm(Pm.v(slice(hh * 128, (hh + 1) * 128)), WUV.v(cc, slice(h * 128, (h + 1) * 128)),
                       onb.v(cc, slice(hh * 128, (hh + 1) * 128)), hh == 0 and cc == 0, hh == 3 and cc == 3)
            tt("dve", mixT.v(slice(hg * 4, hg * 4 + 4), tk),
               bview(Pm.v(), lambda ap: ap.rearrange("p (a b) -> p a b", b=128)),
               bview(rl.v(), lambda ap: ap.rearrange("p (a b) -> p a b", b=128)), ALU.mult)
        A.release(mQ)
        LB("C.smp")
        for cc in range(4):
            Pq = PS[7]
            for h in range(8):
                mm(Pq.v(slice(h * 64, (h + 1) * 64)), WUKT.v(h, slice(cc * 128, (cc + 1) * 128)),
                   q_nopeT.v(h, slice(1024, 1088)), h == 0, h == 7)
            cp("act" if cc % 2 == 0 else "dve", q_cat_s.v(cc), Pq.v())
        ctok_ring = Ring(A, 2, (8, 576), BF16)
        cT_ring = Ring(A, 2, (5, 1024), BF16)
        c128 = slice(0, 128)
        for s_ in range(4):
            qs = slice(16 * s_, 16 * s_ + 16)
            tq = slice(1024 + 16 * s_, 1024 + 16 * s_ + 16)
            rhs_cc = [q_cat_s.v(cc, slice(None), qs) for cc in range(4)]
            rhs_rope = q_ropeT.v(slice(0, 8), tq, p=h64)
            Po, Pl = PS[0], PS[1]
            idx = 0
            for ch in range(4):
                ctok = ctok_ring.next()
                cT = cT_ring.next()
                load("pool", ctok.v(slice(None), slice(0, 512)),
                     c_ckv[s_, ch * 1024:(ch + 1) * 1024, :].rearrange("(kb p) c -> p kb c", p=128))
                load("pool", ctok.v(slice(None), slice(512, 576)),
                     c_kr[s_, ch * 1024:(ch + 1) * 1024, :].rearrange("(kb p) c -> p kb c", p=128))
                for kb in range(8):
                    ks = slice(kb * 128, (kb + 1) * 128)
                    pb = PSB[2 + kb % 2]
                    for cc in range(4):
                        tr(pb.v(slice(cc * 128, (cc + 1) * 128)), ctok.v(kb, slice(cc * 128, (cc + 1) * 128)), ident(128))
                    cp("dve", cT.v(slice(0, 4), ks), bview(pb.v(slice(0, 512)), lambda ap: ap.rearrange("p (a b) -> p a b", b=128)))
                    tr(PSB[4].v(c128, p=h64), ctok.v(kb, slice(512, 576)), ident(128))
                    cp("act", cT.v(4, ks, p=h64), PSB[4].v(c128, p=h64))
                def s_scores(kb, cT=cT):
                    ks = slice(kb * 128, (kb + 1) * 128)
                    Ps = PS[5 + kb % 2]
                    for cc in range(4):
                        mm(Ps.v(c128), cT.v(cc, ks), rhs_cc[cc], cc == 0, False)
                    mm(Ps.v(c128), cT.v(4, ks, p=h64), rhs_rope, False, True)
                    pT_ = pT_ring.next()
                    act(pT_.v(c128), Ps.v(c128), AF.Exp, scale=MLA_SCALE)
                    return pT_

                def s_pv(kb, pT_, first, ctok=ctok):
                    for cc in range(4):
                        mm(Po.v(slice(cc * 128, (cc + 1) * 128)), ctok.v(kb, slice(cc * 128, (cc + 1) * 128)), pT_.v(c128),
                           first and cc == 0, False)
                    mm(Pl.v(c128), cbv("ones"), pT_.v(c128), first, False)

                pend = s_scores(0)
                for kb in range(8):
                    nxt = s_scores(kb + 1) if kb + 1 < 8 else None
                    s_pv(kb, pend, idx == 0)
                    pend = nxt
                    idx += 1
            Ps = PS[5]
            kn = slice(2048, 2112)
            for cc in range(4):
                mm(Ps.v(c128, p=h64), ckvT.v(cc, kn), rhs_cc[cc], cc == 0, False)
            mm(Ps.v(c128, p=h64), ckvT.v(4, kn, p=h64), rhs_rope, False, True)
            pT_ = pT_ring.next()
            act(pT_.v(c128, p=h64), Ps.v(c128, p=h64), AF.Exp, scale=MLA_SCALE)
            tt("dve", pT_.v(c128, p=h64), pT_.v(c128, p=h64), cbv(f"smask{s_}", 64), ALU.mult)
            for cc in range(4):
                mm(Po.v(slice(cc * 128, (cc + 1) * 128)), ckv_tok.v(16, slice(cc * 128, (cc + 1) * 128), p=h64), pT_.v(c128, p=h64),
                   False, cc == 3)
            mm(Pl.v(c128), cbv("ones", 64), pT_.v(c128, p=h64), False, True)
            cp("act", onb_s.v(), bview(Po.v(), lambda ap: ap.rearrange("p (a b) -> p a b", b=128)))
            recip(rl.v(c128), Pl.v(c128))
            Pm = PS[7]
            for h in range(8):
                for cc in range(4):
                    mm(Pm.v(slice(h * 16, (h + 1) * 16)), WUV.v(cc, slice(h * 128, (h + 1) * 128)),
                       onb_s.v(cc, slice(h * 16, (h + 1) * 16)), h == 0 and cc == 0, h == 7 and cc == 3)
            tt("dve", mixT.v(slice(0, 8), tq), bview(Pm.v(c128), lambda ap: ap.rearrange("p (a b) -> p a b", b=16)),
               bview(rl.v(c128), lambda ap: ap.rearrange("p (a b) -> p a b", b=16)), ALU.mult)
        dbg("mixA", mixT.v(slice(0, 8)))
        A.release(mC)
        if stop_after == "C":
            return finish()
        A.release(m_tail)

        store_q[0] = "sp"
        hB = A.alloc((5, D), F32)
        tmpB = A.alloc((5, D), F32)
        fT = A.alloc((16, 576), BF16)
        actT = A.alloc((11, 576), BF16)
        wring = Ring(A, 4, (4, 512), BF16)
        GB = A.alloc((D,), F32)
        pst = A.alloc((256,), F32)
        psb = A.alloc((256,), BF16)
        pTb = A.alloc((2, 576), BF16)
        sg_ring = Ring(A, 2, (512,), F32)
        sg2 = A.alloc((64,), F32)
        ssD = Ring(A, 8, (16,), F32)
        xnD = A.alloc((D,), BF16)
        w_out_k = w_out.rearrange("(k p) n -> p k n", p=128)
        w_gate_k = w_gate.rearrange("(k p) n -> p k n", p=128)
        w_up_k = w_up.rearrange("(k p) n -> p k n", p=128)
        w_down_k = w_down.rearrange("(k p) n -> p k n", p=128)
        w_pg_k = w_pg.rearrange("(k p) n -> p k n", p=128)
        w_ple_k = w_ple.rearrange("(k p) n -> p k n", p=128)

        def wtile_kn(src_k, k0, nk, c0, ncols):
            wt = wring.next()
            v = bview(wt.v(), lambda ap: ap.rearrange("p a b -> p (a b)")[:, 0:nk * ncols].rearrange("p (a b) -> p a b", b=ncols))
            dma("pool", v.ap, src_k[:, k0:k0 + nk, c0:c0 + ncols], w=[v])
            return wt, (lambda kk: View(v.ap[:, kk, :], wt.v().rg))

        groups = [[(0, 128), (128, 128), (256, 128), (384, 128), (1024, 64)],
                  [(512, 128), (640, 128), (768, 128), (896, 128)]]
        for grp in groups:
            nb = len(grp)
            offs = []
            o = 0
            for (_, T) in grp:
                offs.append(o)
                o += T
            ntok = o
            Nmain = min(512, ntok)
            rem = ntok - Nmain

            def tok_linear(src_k, nkc, lhs_fn, consume, ksplit=4):
                for n in range(4):
                    k = 0
                    while k < nkc:
                        nk = min(ksplit, nkc - k)
                        _, wv = wtile_kn(src_k, k, nk, n * 512, 512)
                        for kk in range(nk):
                            for bi, (tok0, T) in enumerate(grp):
                                mm(PS[bi].v(p=slice(0, T)), lhs_fn(k + kk, bi), wv(kk), k + kk == 0, k + kk == nkc - 1)
                        k += nk
                    for bi, (tok0, T) in enumerate(grp):
                        consume(n, bi, T)

            def post_norm_residual(first_x):
                for bi, (tok0, T) in enumerate(grp):
                    pT = slice(0, T)
                    ss = ssD.next()
                    act(junk.v(p=pT), tmpB.v(bi, p=pT), AF.Square, accum=ss.v(slice(0, 1), p=pT))
                    rstd_act(ss.v(slice(1, 2), p=pT), ss.v(slice(0, 1), p=pT), 1.0 / D)
                    if first_x:
                        load("sp", hB.v(bi, p=pT), x_own[tok0:tok0 + T, :])
                    stt(tmpB.v(bi, p=pT), tmpB.v(bi, p=pT), ss.v(slice(1, 2), p=pT), GB.v(p=pT), ALU.mult, ALU.mult)
                    tt("dve", hB.v(bi, p=pT), hB.v(bi, p=pT), tmpB.v(bi, p=pT), ALU.add)

            LB("D1.wout")
            load("sp", GB.v(), g_pm_b)
            tok_linear(w_out_k, 16, lambda k, bi: mixT.v(k, slice(grp[bi][0], grp[bi][0] + grp[bi][1])),
                       lambda n, bi, T: cp("act", tmpB.v(bi, slice(n * 512, (n + 1) * 512), p=slice(0, T)), PS[bi].v(p=slice(0, T))))
            LB("D2.norm")
            post_norm_residual(True)
            LB("D3.fT")
            for bi, (tok0, T) in enumerate(grp):
                ss = ssD.next()
                hv = Buf(hB.t[:, bi, :], hB.base + bi * D * 4, (D,), 4)
                norm_transpose(hv, T, 16, lambda c0, c1, bi=bi, T=T: fT.v(slice(c0, c1), slice(offs[bi], offs[bi] + T)),
                               PSB[6], PSB[7], ss, xnD)
            for qd in range(4):
                LB("D4.gateup")
                for ml in range(11):
                    m = qd * 11 + ml
                    wg = wring.next()
                    wgv = bview(wg.v(), lambda ap: ap.rearrange("p a b -> p (a b)").rearrange("p (a b) -> p a b", b=128))
                    dma("pool", wgv.ap, w_gate_k[:, :, m * 128:(m + 1) * 128], w=[wgv])
                    wu = wring.next()
                    wuv = bview(wu.v(), lambda ap: ap.rearrange("p a b -> p (a b)").rearrange("p (a b) -> p a b", b=128))
                    dma("pool", wuv.ap, w_up_k[:, :, m * 128:(m + 1) * 128], w=[wuv])
                    gk = lambda k: View(wgv.ap[:, k, :], wg.v().rg)
                    uk = lambda k: View(wuv.ap[:, k, :], wu.v().rg)
                    b0 = (m % 2) * 3
                    Pg_, Pu_, Pr_ = PS[b0], PS[b0 + 1], PS[b0 + 2]
                    nm = slice(0, Nmain)
                    for k in range(16):
                        mm(Pg_.v(nm), gk(k), fT.v(k, nm), k == 0, k == 15)
                    for k in range(16):
                        mm(Pu_.v(nm), uk(k), fT.v(k, nm), k == 0, k == 15)
                    if rem:
                        rs_ = slice(Nmain, ntok)
                        for k in range(16):
                            mm(Pr_.v(slice(0, rem)), gk(k), fT.v(k, rs_), k == 0, k == 15)
                        for k in range(16):
                            mm(Pr_.v(slice(64, 64 + rem)), uk(k), fT.v(k, rs_), k == 0, k == 15)
                    sg = sg_ring.next()
                    act(sg.v(nm), Pg_.v(nm), AF.Silu)
                    tt("dve", actT.v(ml, nm), sg.v(nm), Pu_.v(nm), ALU.mult)
                    if rem:
                        act(sg2.v(slice(0, rem)), Pr_.v(slice(0, rem)), AF.Silu)
                        tt("dve", actT.v(ml, rs_), sg2.v(slice(0, rem)), Pr_.v(slice(64, 64 + rem)), ALU.mult)

                def down_consume(n, bi, T, qd=qd):
                    dst = tmpB.v(bi, slice(n * 512, (n + 1) * 512), p=slice(0, T))
                    if qd == 0:
                        cp("act", dst, PS[bi].v(p=slice(0, T)))
                    else:
                        tt("dve", dst, dst, PS[bi].v(p=slice(0, T)), ALU.add)
                LB("D4.down")
                tok_linear(w_down_k[:, qd * 11:(qd + 1) * 11, :], 11,
                           lambda k, bi: actT.v(k, slice(offs[bi], offs[bi] + grp[bi][1])), down_consume)
            LB("D5.norm")
            load("sp", GB.v(), g_pf_b)
            post_norm_residual(False)
            LB("D6.ple_prep")
            for bi, (tok0, T) in enumerate(grp):
                pT = slice(0, T)
                cp("act", xnD.v(p=pT), hB.v(bi, p=pT))
                for c4 in range(4):
                    pb = PSB[6 + c4 % 2]
                    for jj in range(4):
                        c = c4 * 4 + jj
                        tr(pb.v(slice(jj * 128, jj * 128 + T)), xnD.v(slice(c * 128, (c + 1) * 128), p=pT), ident(T))
                    src = bview(pb.v(slice(0, 512)), lambda ap: ap.rearrange("p (a b) -> p a b", b=128)[:, :, 0:T])
                    cp("dve" if c4 % 2 == 0 else "act", fT.v(slice(c4 * 4, c4 * 4 + 4), slice(offs[bi], offs[bi] + T)), src)
                load("sp", pst.v(p=pT), p_own[tok0:tok0 + T, :])
                cp("dve", psb.v(p=pT), pst.v(p=pT))
                pb = PSB[6]
                for c in range(2):
                    tr(pb.v(slice(c * 128, c * 128 + T)), psb.v(slice(c * 128, (c + 1) * 128), p=pT), ident(T))
                src = bview(pb.v(slice(0, 256)), lambda ap: ap.rearrange("p (a b) -> p a b", b=128)[:, :, 0:T])
                cp("dve", pTb.v(slice(0, 2), slice(offs[bi], offs[bi] + T)), src)
            wp_holder = [None]

            def ple_consume(n, bi, T):
                pT = slice(0, T)
                if bi == 0:
                    wp_holder[0] = wtile_kn(w_ple_k, 0, 2, n * 512, 512)[1]
                Pe = PS[5 + bi % 3]
                for kc in range(2):
                    mm(Pe.v(p=pT), pTb.v(kc, slice(offs[bi], offs[bi] + T)), wp_holder[0](kc), kc == 0, kc == 1)
                sg = sg_ring.next()
                act(sg.v(p=pT), PS[bi].v(p=pT), AF.Sigmoid)
                tt("dve", sg.v(p=pT), sg.v(p=pT), Pe.v(p=pT), ALU.mult)
                ns = slice(n * 512, (n + 1) * 512)
                tt("dve", tmpB.v(bi, ns, p=pT), sg.v(p=pT), hB.v(bi, ns, p=pT), ALU.add)
            LB("D6.ple")
            tok_linear(w_pg_k, 16, lambda k, bi: fT.v(k, slice(offs[bi], offs[bi] + grp[bi][1])), ple_consume)
            for bi, (tok0, T) in enumerate(grp):
                store(y_out[tok0:tok0 + T, :], tmpB.v(bi, p=slice(0, T)))

        return finish()


_CACHE = {}


def _prep_inputs(inp):
    f32 = lambda a: np.ascontiguousarray(np.asarray(a, dtype=np.float32))
    xp = f32(inp["x_prompt"])
    xsm = f32(inp["x_sample"])
    pp = f32(inp["p_prompt"])[0]
    psm = f32(inp["p_sample"])[0]
    cck = f32(inp["cache_ckv"])[0]
    ckr = f32(inp["cache_krope"])[0]
    stg = f32(inp["state_gla"])[0]
    g = lambda k: f32(inp[k])[0]
    gvec = np.concatenate([g("g_pre_mix").reshape(16, 128).T, g("g_pre_ffn").reshape(16, 128).T,
                           g("g_q").reshape(4, 128).T, g("g_gla").reshape(2, 128).T], axis=1)
    shared = {
        "gvec": f32(gvec),
        "g_kv_b": f32(np.broadcast_to(g("g_kv")[None, :], (128, 512))),
        "g_pm_b": f32(np.broadcast_to(g("g_post_mix")[None, :], (128, D))),
        "g_pf_b": f32(np.broadcast_to(g("g_post_ffn")[None, :], (128, D))),
        "w_in": g("w_in"), "w_uq": g("w_uq"),
        "w_uk": f32(g("w_uk").reshape(512, 1024)), "w_uv": f32(g("w_uv").reshape(512, 1024)),
        "w_ga": f32(np.concatenate([g("w_ga"), g("b_ga")[None, :]], axis=0)),
        "w_out": g("w_out"), "w_gate": g("w_gate"), "w_up": g("w_up"), "w_down": g("w_down"),
        "w_ple": g("w_ple"), "w_pg": g("w_ple_gate"),
    }
    maps = []
    for c in range(8):
        b, p = c // 2, c % 2
        own = [2 * i + p for i in range(8)]
        xs_c = xsm[4 * c:4 * c + 4].reshape(64, D)
        m = dict(shared)
        m["x_nat"] = f32(np.concatenate([xp[b], xs_c], axis=0))
        m["x_own"] = f32(np.concatenate([xp[b].reshape(16, 128, D)[own].reshape(1024, D), xs_c], axis=0))
        m["p_own"] = f32(np.concatenate([pp[b].reshape(16, 128, 256)[own].reshape(1024, 256),
                                         psm[4 * c:4 * c + 4].reshape(64, 256)], axis=0))
        m["c_ckv"] = f32(cck[4 * c:4 * c + 4])
        m["c_kr"] = f32(ckr[4 * c:4 * c + 4])
        m["st_in"] = f32(stg[4 * c:4 * c + 4])
        cf, cb = build_consts(p)
        m["constf"] = cf.pack()
        m["constb"] = cb.pack()
        maps.append(m)
    return maps


def _get_program(debug=None, stop_after=None):
    key = (None if debug is None else tuple(sorted(debug)), stop_after)
    if key not in _CACHE:
        cf, cb = build_consts(0)
        cf.pack()
        cb.pack()
        _CACHE[key] = build_program(cf.off, cf.n, cb.off, cb.n, debug=debug, stop_after=stop_after)
    return _CACHE[key]


def kernel(**inp):
    maps = _prep_inputs(inp)
    nc, _ = _get_program()
    res = run_bass_kernel_spmd(nc, maps, core_ids=list(range(8)))
    R = res.results
    y_p = np.zeros((4, 2048, D), np.float32)
    y_s = np.zeros((32, 16, D), np.float32)
    ckv_p = np.zeros((1, 4, 2048, 512), np.float32)
    kr_p = np.zeros((1, 4, 2048, 64), np.float32)
    gl_p = np.zeros((1, 4, 4, 128, 256), np.float32)
    ckv_s = np.zeros((1, 32, 16, 512), np.float32)
    kr_s = np.zeros((1, 32, 16, 64), np.float32)
    gl_s = np.zeros((1, 32, 4, 128, 256), np.float32)
    for c in range(8):
        b, p = c // 2, c % 2
        r = R[c]
        yo = np.asarray(r["y_out"])
        for i in range(8):
            j = 2 * i + p
            y_p[b, j * 128:(j + 1) * 128] = yo[i * 128:(i + 1) * 128]
        y_s[4 * c:4 * c + 4] = yo[1024:1088].reshape(4, 16, D)
        ck = np.asarray(r["ckv_out"])
        kr = np.asarray(r["kr_out"])
        if p == 0:
            ckv_p[0, b] = ck[:2048]
            kr_p[0, b] = kr[:2048]
            gl_p[0, b] = np.asarray(r["gp_out"])
        ckv_s[0, 4 * c:4 * c + 4] = ck[2048:2112].reshape(4, 16, 512)
        kr_s[0, 4 * c:4 * c + 4] = kr[2048:2112].reshape(4, 16, 64)
        gl_s[0, 4 * c:4 * c + 4] = np.asarray(r["gs_out"])
    return (y_p, y_s, ckv_p, kr_p, gl_p, ckv_s, kr_s, gl_s)
```
